# Optimizing a Trainium2 kernel written in Bass

```python
import jax
import jax.numpy as jnp
from jax import lax
import numpy as np

D_MODEL = 1024
BATCH = 8
SEQ = 2048
DEPTH = 4

GRID_W = 64
CTX_LEN = 256
HEAD_DIM = 64
N_EVEN = (DEPTH + 1) // 2
N_ODD = DEPTH // 2
EPS = 1e-6
NEG_INF = -1e30

NA_HEADS = (D_MODEL // 2) // HEAD_DIM
NA_WIN_ROWS = 8
NA_WIN_COLS = 16
NA_COL_BLOCK = 16
NA_KEY_COLS = NA_COL_BLOCK + NA_WIN_COLS
GLA_HEADS = 4
GLA_DK = (D_MODEL // 4) // GLA_HEADS
GLA_DV = (D_MODEL // 2) // GLA_HEADS
GLA_RANK = 16
GLA_NORMALIZER = 16.0
GLA_CHUNK = 64
SWA_HEADS = D_MODEL // HEAD_DIM
SWA_KV_HEADS = 4
SWA_GROUP = SWA_HEADS // SWA_KV_HEADS
SWA_WINDOW = 128
SWA_BLOCK = 128
D_FF = 4 * D_MODEL
ROPE_THETA = 10000.0

NA_WIDTH = NA_HEADS * HEAD_DIM
GLA_QK_WIDTH = GLA_HEADS * GLA_DK
GLA_V_WIDTH = GLA_HEADS * GLA_DV
AB_IN = 3 * NA_WIDTH + 2 * GLA_QK_WIDTH + 2 * GLA_V_WIDTH + 2 * GLA_RANK
AB_OUT = NA_WIDTH + GLA_V_WIDTH
SWA_Q_WIDTH = SWA_HEADS * HEAD_DIM
SWA_KV_WIDTH = SWA_KV_HEADS * HEAD_DIM
SWA_IN = SWA_Q_WIDTH + 2 * SWA_KV_WIDTH

kernel_name = 'hybrid_na_gla_swa_dit_prefix'


def _split_at(t, sizes):
    idx, acc = [], 0
    for s in sizes[:-1]:
        acc += s
        idx.append(acc)
    return jnp.split(t, idx, axis=-1)


def rms_norm(x, gain):
    xf = x.astype(jnp.float32)
    y = xf * lax.rsqrt(jnp.mean(xf * xf, axis=-1, keepdims=True) + EPS)
    return (y * gain.astype(jnp.float32)).astype(x.dtype)


def modulate(h, shift, scale):
    return h * (1.0 + scale) + shift


def _heads(t, n_heads):
    b, n, _ = t.shape
    return t.reshape(b, n, n_heads, -1).transpose(0, 2, 1, 3)


def _merge_heads(t):
    b, h, n, d = t.shape
    return t.transpose(0, 2, 1, 3).reshape(b, n, h * d)


def axial_rope_tables(n_tokens):
    t = jnp.arange(n_tokens)
    row = (t // GRID_W).astype(jnp.float32)
    col = (t % GRID_W).astype(jnp.float32)
    n_freq = HEAD_DIM // 4
    inv = ROPE_THETA ** (-jnp.arange(n_freq, dtype=jnp.float32) / n_freq)
    ang = jnp.concatenate([row[:, None] * inv, col[:, None] * inv], axis=-1)
    return jnp.cos(ang), jnp.sin(ang)


def apply_rope(x, cos, sin):
    xf = x.astype(jnp.float32).reshape(x.shape[:-1] + (HEAD_DIM // 2, 2))
    xe, xo = xf[..., 0], xf[..., 1]
    out = jnp.stack([xe * cos - xo * sin, xe * sin + xo * cos], axis=-1)
    return out.reshape(x.shape).astype(x.dtype)


def squared_relu_mlp(h, w1, w2):
    return jnp.square(jax.nn.relu(h @ w1)) @ w2


def neighbourhood_attention(q, k, v, kc, vc, rpb):
    b, h, n, dh = q.shape
    rows = n // GRID_W
    kr = min(NA_WIN_ROWS, rows)
    n_cb = GRID_W // NA_COL_BLOCK
    qg = (q * dh ** -0.5).reshape(b, h, rows, n_cb, NA_COL_BLOCK, dh)
    kg = k.reshape(b, h, rows, GRID_W, dh)
    vg = v.reshape(b, h, rows, GRID_W, dh)
    q_col = np.arange(GRID_W).reshape(n_cb, NA_COL_BLOCK)
    strip0 = np.clip(np.arange(n_cb) * NA_COL_BLOCK - NA_WIN_COLS // 2, 0, GRID_W - NA_KEY_COLS)
    k_col = strip0[:, None] + np.arange(NA_KEY_COLS)
    win0 = np.clip(q_col - NA_WIN_COLS // 2, 0, GRID_W - NA_WIN_COLS)
    col_in = (k_col[:, None, :] >= win0[..., None]) & (k_col[:, None, :] < win0[..., None] + NA_WIN_COLS)
    col_off = k_col[:, None, :] - q_col[..., None] + (NA_WIN_COLS - 1)
    rpb32 = rpb.astype(jnp.float32)
    n_nb = kr * NA_KEY_COLS

    def row_block(r):
        r0 = jnp.clip(r - kr // 2, 0, rows - kr)
        k_strip = lax.dynamic_slice_in_dim(kg, r0, kr, axis=2)[:, :, :, k_col]
        v_strip = lax.dynamic_slice_in_dim(vg, r0, kr, axis=2)[:, :, :, k_col]
        q_r = lax.dynamic_index_in_dim(qg, r, axis=2, keepdims=False)
        row_off = r0 + jnp.arange(kr) - r + (NA_WIN_ROWS - 1)
        bias = rpb32[:, row_off][:, :, col_off].transpose(0, 2, 3, 1, 4)
        s_nb = jnp.einsum('bhnqd,bhrnkd->bhnqrk', q_r, k_strip).astype(jnp.float32) + bias
        s_nb = jnp.where(col_in[:, :, None, :], s_nb, NEG_INF).reshape(b, h, n_cb, NA_COL_BLOCK, n_nb)
        s_ctx = jnp.einsum('bhnqd,bhcd->bhnqc', q_r, kc).astype(jnp.float32)
        p = jax.nn.softmax(jnp.concatenate([s_nb, s_ctx], axis=-1), axis=-1).astype(v.dtype)
        p_nb = p[..., :n_nb].reshape(b, h, n_cb, NA_COL_BLOCK, kr, NA_KEY_COLS)
        return (jnp.einsum('bhnqrk,bhrnkd->bhnqd', p_nb, v_strip)
                + jnp.einsum('bhnqc,bhcd->bhnqd', p[..., n_nb:], vc))

    o = lax.map(row_block, jnp.arange(rows))
    return jnp.moveaxis(o, 0, 2).reshape(b, h, n, dh)


def context_attention(qc, kc, vc):
    s = jnp.einsum('bhqd,bhkd->bhqk', qc * qc.shape[-1] ** -0.5, kc).astype(jnp.float32)
    return jnp.einsum('bhqk,bhkd->bhqd', jax.nn.softmax(s, axis=-1).astype(vc.dtype), vc)


def gla_chunked(q, k, v, log_a, s0):
    b, h, n, dk = q.shape
    dv = v.shape[-1]
    n_chunks = n // GLA_CHUNK

    def chunks(t):
        return jnp.moveaxis(t.reshape(b, h, n_chunks, GLA_CHUNK, t.shape[-1]), 2, 0)

    causal = jnp.tril(jnp.ones((GLA_CHUNK, GLA_CHUNK), dtype=bool))

    def step(s, inp):
        qc, kc, vc, gc = inp
        cum = jnp.cumsum(gc, axis=-2)
        cum_last = cum[..., -1:, :]
        q_t = qc * jnp.exp(cum)
        k_t = kc * jnp.exp(-cum)
        k_end = kc * jnp.exp(cum_last - cum)
        a = jnp.where(causal, jnp.einsum('bhid,bhjd->bhij', q_t, k_t), 0.0)
        o = jnp.einsum('bhij,bhjv->bhiv', a, vc) + jnp.einsum('bhid,bhdv->bhiv', q_t, s)
        s_new = jnp.exp(cum_last[..., 0, :])[..., None] * s + jnp.einsum('bhjd,bhjv->bhdv', k_end, vc)
        return s_new, o

    s_fin, o = lax.scan(step, s0, (chunks(q), chunks(k), chunks(v), chunks(log_a)))
    return jnp.moveaxis(o, 0, 2).reshape(b, h, n, dv), s_fin


def _gla_log_decay(lr, wa2_d, ba_d, d):
    z = jnp.einsum('btr,rk->btk', lr[..., d * GLA_RANK:(d + 1) * GLA_RANK], wa2_d) + ba_d
    return _heads(jax.nn.log_sigmoid(z.astype(jnp.float32)) / GLA_NORMALIZER, GLA_HEADS)


def gla_bidirectional(q, k, v, lr, qc, kc, vc, lrc, wa2, ba, with_ctx_out):
    f32 = jnp.float32
    scale = GLA_DK ** -0.5
    q, k, v = q.astype(f32) * scale, k.astype(f32), v.astype(f32)
    qc, kc, vc = qc.astype(f32) * scale, kc.astype(f32), vc.astype(f32)
    s0 = jnp.zeros((q.shape[0], GLA_HEADS, GLA_DK, GLA_DV), f32)

    def direction(d, reverse):
        orient = (lambda t: jnp.flip(t, axis=2)) if reverse else (lambda t: t)
        a_x = orient(_gla_log_decay(lr, wa2[d], ba[d], d))
        a_c = orient(_gla_log_decay(lrc, wa2[d], ba[d], d))
        o_c, s_c = gla_chunked(orient(qc), orient(kc), orient(vc), a_c, s0)
        o_x, _ = gla_chunked(orient(q), orient(k), orient(v), a_x, s_c)
        return orient(o_x), orient(o_c)

    ox_f, oc_f = direction(0, False)
    ox_b, oc_b = direction(1, True)
    oc = (oc_f + oc_b) if with_ctx_out else None
    return ox_f + ox_b, oc


def gla_output(o, gate, gnorm):
    o = o * lax.rsqrt(jnp.mean(o * o, axis=-1, keepdims=True) + EPS)
    o = _merge_heads(o) * gnorm.astype(jnp.float32)
    return (o * jax.nn.silu(gate.astype(jnp.float32))).astype(gate.dtype)


def even_mixer(hx, hc, w_in, w_out, rpb, wa2, ba, gnorm, with_ctx_out):
    sizes = (NA_WIDTH, NA_WIDTH, NA_WIDTH, GLA_QK_WIDTH, GLA_QK_WIDTH, GLA_V_WIDTH, GLA_V_WIDTH, 2 * GLA_RANK)

    def project(h):
        qa, ka, va, qb, kb, vb, gb, lr = _split_at(h @ w_in, sizes)
        return (_heads(qa, NA_HEADS), _heads(ka, NA_HEADS), _heads(va, NA_HEADS),
                _heads(qb, GLA_HEADS), _heads(kb, GLA_HEADS), _heads(vb, GLA_HEADS), gb, lr)

    qa, ka, va, qb, kb, vb, gb, lr = project(hx)
    qac, kac, vac, qbc, kbc, vbc, gbc, lrc = project(hc)
    oa = neighbourhood_attention(qa, ka, va, kac, vac, rpb)
    ob, obc = gla_bidirectional(qb, kb, vb, lr, qbc, kbc, vbc, lrc, wa2, ba, with_ctx_out)
    yx = jnp.concatenate([_merge_heads(oa), gla_output(ob, gb, gnorm)], axis=-1) @ w_out
    if not with_ctx_out:
        return yx, None
    oac = context_attention(qac, kac, vac)
    yc = jnp.concatenate([_merge_heads(oac), gla_output(obc, gbc, gnorm)], axis=-1) @ w_out
    return yx, yc


def sliding_window_attention(q, k, v, kc, vc, sink_logit, cos, sin):
    b, hkv, g, n, dh = q.shape
    n_blk = n // SWA_BLOCK
    q_rot = apply_rope(q, cos, sin)
    pad = ((0, 0), (0, 0), (SWA_BLOCK, SWA_BLOCK), (0, 0))
    k_pad = jnp.pad(apply_rope(k, cos, sin), pad)
    v_pad = jnp.pad(v, pad)
    n_loc = 3 * SWA_BLOCK

    def block(i):
        q0 = i * SWA_BLOCK
        qr = lax.dynamic_slice_in_dim(q_rot, q0, SWA_BLOCK, axis=3)
        qp = lax.dynamic_slice_in_dim(q, q0, SWA_BLOCK, axis=3)
        kb = lax.dynamic_slice_in_dim(k_pad, q0, n_loc, axis=2)
        vb = lax.dynamic_slice_in_dim(v_pad, q0, n_loc, axis=2)
        qpos = q0 + jnp.arange(SWA_BLOCK)
        kpos = q0 - SWA_BLOCK + jnp.arange(n_loc)
        ok = ((jnp.abs(kpos[None, :] - qpos[:, None]) <= SWA_WINDOW)
              & (kpos >= 0)[None, :] & (kpos < n)[None, :])
        s_loc = jnp.where(ok, jnp.einsum('bhgqd,bhkd->bhgqk', qr, kb).astype(jnp.float32), NEG_INF)
        s_ctx = jnp.einsum('bhgqd,bhcd->bhgqc', qp, kc).astype(jnp.float32)
        s_sink = jnp.broadcast_to(sink_logit, s_loc.shape[:-1] + (1,))
        p = jax.nn.softmax(jnp.concatenate([s_loc, s_ctx, s_sink], axis=-1), axis=-1).astype(v.dtype)
        return (jnp.einsum('bhgqk,bhkd->bhgqd', p[..., :n_loc], vb)
                + jnp.einsum('bhgqc,bhcd->bhgqd', p[..., n_loc:n_loc + kc.shape[2]], vc))

    o = lax.map(block, jnp.arange(n_blk))
    return jnp.moveaxis(o, 0, 3).reshape(b, hkv, g, n, dh)


def odd_mixer(hx, hc, w_in, w_out, sink, cos, sin, with_ctx_out):
    def project(h):
        qf, kf, vf = _split_at(h @ w_in, (SWA_Q_WIDTH, SWA_KV_WIDTH, SWA_KV_WIDTH))
        b, n, _ = h.shape
        q = qf.reshape(b, n, SWA_KV_HEADS, SWA_GROUP, HEAD_DIM).transpose(0, 2, 3, 1, 4) * HEAD_DIM ** -0.5
        return q, _heads(kf, SWA_KV_HEADS), _heads(vf, SWA_KV_HEADS)

    def merge(o):
        b, _, _, n, _ = o.shape
        return o.transpose(0, 3, 1, 2, 4).reshape(b, n, SWA_Q_WIDTH)

    sink_logit = sink.astype(jnp.float32).reshape(SWA_KV_HEADS, SWA_GROUP)[None, :, :, None, None]
    q, k, v = project(hx)
    qc, kc, vc = project(hc)
    yx = merge(sliding_window_attention(q, k, v, kc, vc, sink_logit, cos, sin)) @ w_out
    if not with_ctx_out:
        return yx, None
    s = jnp.einsum('bhgqd,bhkd->bhgqk', qc, kc).astype(jnp.float32)
    s = jnp.concatenate([s, jnp.broadcast_to(sink_logit, s.shape[:-1] + (1,))], axis=-1)
    p = jax.nn.softmax(s, axis=-1)[..., :-1].astype(vc.dtype)
    yc = merge(jnp.einsum('bhgqk,bhkd->bhgqd', p, vc)) @ w_out
    return yx, yc


def setup_inputs(seed: int = 0) -> dict:
    key = jax.random.key(seed)
    ks = jax.random.split(key, 24)
    d = D_MODEL

    def nrm(k, shape, s):
        return jax.random.normal(k, shape, jnp.float32) * s

    return {
        'x': nrm(ks[0], (BATCH, SEQ, d), 1.0),
        'c': nrm(ks[1], (BATCH, d), 1.0),
        'ctx': nrm(ks[2], (BATCH, CTX_LEN, d), 1.0),
        'c_ctx': nrm(ks[3], (d,), 1.0),
        'ada_w': nrm(ks[4], (DEPTH, d, 6 * d), 0.3 * d ** -0.5),
        'ada_b': nrm(ks[5], (DEPTH, 6 * d), 0.02),
        'norm_mix': 1.0 + nrm(ks[6], (DEPTH, d), 0.02),
        'norm_mlp': 1.0 + nrm(ks[7], (DEPTH, d), 0.02),
        'mlp_w1': nrm(ks[8], (DEPTH, d, D_FF), d ** -0.5),
        'mlp_w2': nrm(ks[9], (DEPTH, D_FF, d), D_FF ** -0.5),
        'ab_w_in': nrm(ks[10], (N_EVEN, d, AB_IN), d ** -0.5),
        'ab_w_out': nrm(ks[11], (N_EVEN, AB_OUT, d), AB_OUT ** -0.5),
        'na_rpb': nrm(ks[12], (N_EVEN, NA_HEADS, 2 * NA_WIN_ROWS - 1, 2 * NA_WIN_COLS - 1), 0.1),
        'gla_wa2': nrm(ks[13], (N_EVEN, 2, GLA_RANK, GLA_QK_WIDTH), GLA_RANK ** -0.5),
        'gla_ba': nrm(ks[14], (N_EVEN, 2, GLA_QK_WIDTH), 0.1),
        'gla_gnorm': 1.0 + nrm(ks[15], (N_EVEN, GLA_V_WIDTH), 0.02),
        'swa_w_in': nrm(ks[16], (N_ODD, d, SWA_IN), d ** -0.5),
        'swa_w_out': nrm(ks[17], (N_ODD, SWA_Q_WIDTH, d), SWA_Q_WIDTH ** -0.5),
        'swa_sink': nrm(ks[18], (N_ODD, SWA_HEADS), 0.5),
        'norm_final': 1.0 + nrm(ks[19], (d,), 0.02),
    }


def reference(x, c, ctx, c_ctx, ada_w, ada_b, norm_mix, norm_mlp, mlp_w1, mlp_w2,
              ab_w_in, ab_w_out, na_rpb, gla_wa2, gla_ba, gla_gnorm,
              swa_w_in, swa_w_out, swa_sink, norm_final):
    cos, sin = axial_rope_tables(x.shape[1])
    sc = jax.nn.silu(c)
    sc_ctx = jax.nn.silu(c_ctx)
    for l in range(DEPTH):
        ctx_out = l < DEPTH - 1
        mx = jnp.split(sc @ ada_w[l] + ada_b[l], 6, axis=-1)
        mc = jnp.split(sc_ctx @ ada_w[l] + ada_b[l], 6, axis=-1)
        hx = modulate(rms_norm(x, norm_mix[l]), mx[0][:, None], mx[1][:, None])
        hc = modulate(rms_norm(ctx, norm_mix[l]), mc[0], mc[1])
        if l % 2 == 0:
            j = l // 2
            yx, yc = even_mixer(hx, hc, ab_w_in[j], ab_w_out[j], na_rpb[j], gla_wa2[j], gla_ba[j],
                                gla_gnorm[j], ctx_out)
        else:
            j = l // 2
            yx, yc = odd_mixer(hx, hc, swa_w_in[j], swa_w_out[j], swa_sink[j], cos, sin, ctx_out)
        x = x + mx[2][:, None] * yx
        hx = modulate(rms_norm(x, norm_mlp[l]), mx[3][:, None], mx[4][:, None])
        x = x + mx[5][:, None] * squared_relu_mlp(hx, mlp_w1[l], mlp_w2[l])
        if ctx_out:
            ctx = ctx + mc[2] * yc
            hc = modulate(rms_norm(ctx, norm_mlp[l]), mc[3], mc[4])
            ctx = ctx + mc[5] * squared_relu_mlp(hc, mlp_w1[l], mlp_w2[l])
    return rms_norm(x, norm_final)
```

```python
import numpy as np
import concourse.bass as bass
import concourse.mybir as mybir
from concourse.bass_utils import run_bass_kernel_spmd

F32 = mybir.dt.float32
BF16 = mybir.dt.bfloat16
AF = mybir.ActivationFunctionType
ALU = mybir.AluOpType
AX = mybir.AxisListType

D = 1024
SEQ = 2048
CTX = 256
NT = 18
NTOK = NT * 128
DEPTH = 4
EPS = 1e-6
GROUPS = [(0, 512), (512, 1024), (1024, 1536), (1536, 2048), (2048, 2304)]


class Sched:
    def __init__(self, nc):
        self.nc = nc
        self.eng = {'pe': nc.tensor, 'act': nc.scalar, 'dve': nc.vector, 'pool': nc.gpsimd, 'sp': nc.sync}
        self.sem, self.cnt, self.ctx, self.semobj = {}, {}, [], {}
        for e in self.eng:
            cm = nc.semaphore('s_' + e)
            self.sem[e] = cm.__enter__(); self.ctx.append(cm)
            self.cnt[e] = 0
            self.semobj['s_' + e] = self.sem[e]
        self.dsem, self.dcnt = {}, {}
        self.seen = {e: {} for e in self.eng}
        self.lastw, self.readers = {}, {}
        self.rr = {}

    def rot(self, name, n):
        v = self.rr.get(name, 0)
        self.rr[name] = v + 1
        return v % n

    def dma_sem(self, name):
        if name not in self.dsem:
            cm = self.nc.semaphore('d_' + name)
            self.dsem[name] = cm.__enter__(); self.ctx.append(cm)
            self.dcnt[name] = 0
            self.semobj['d_' + name] = self.dsem[name]
        return self.dsem[name]

    def _wait(self, e, tok):
        if tok is None:
            return
        sname, val = tok
        if e == 'pe' and sname == 's_pe':
            return
        if self.seen[e].get(sname, 0) >= val:
            return
        self.eng[e].wait_ge(self.semobj[sname], val)
        self.seen[e][sname] = val

    def _deps(self, e, reads, writes):
        for k in reads:
            self._wait(e, self.lastw.get(k))
        for k in writes:
            self._wait(e, self.lastw.get(k))
            for t in self.readers.get(k, ()):
                self._wait(e, t)

    def _commit(self, tok, reads, writes):
        for k in reads:
            self.readers.setdefault(k, []).append(tok)
        for k in writes:
            self.lastw[k] = tok
            self.readers[k] = []

    def op(self, e, fn, reads=(), writes=(), inc=True):
        self._deps(e, reads, writes)
        inst = fn(self.eng[e])
        if inc:
            self.cnt[e] += 1
            inst.then_inc(self.sem[e], 1)
            tok = ('s_' + e, self.cnt[e])
        else:
            tok = ('s_' + e, self.cnt[e] + 1)
        self._commit(tok, reads, writes)
        return tok

    def dma(self, e, slot, out, in_, reads=(), writes=(), **kw):
        sem = self.dma_sem(slot)
        self._deps(e, reads, writes)
        inst = self.eng[e].dma_start(out=out, in_=in_, **kw)
        self.dcnt[slot] += 16
        inst.then_inc(sem, 16)
        tok = ('d_' + slot, self.dcnt[slot])
        self._commit(tok, reads, writes)
        return tok

    def wait_all(self, e):
        for f in self.eng:
            if self.cnt[f] > 0:
                self._wait(e, ('s_' + f, self.cnt[f]))
        for s in self.dsem:
            if self.dcnt[s] > 0:
                self._wait(e, ('d_' + s, self.dcnt[s]))

    def barrier(self):
        for e in self.eng:
            self.wait_all(e)
        self.lastw, self.readers = {}, {}

    def close(self):
        for cm in reversed(self.ctx):
            cm.__exit__(None, None, None)


def _na_patterns():
    pats, keymap, tilemap = [], {}, {}
    qi = np.arange(128)
    for i in range(16):
        r = 2 * i + qi // 64
        c = qi % 64
        r0 = np.clip(r - 4, 0, 24)
        w0 = np.clip(c - 8, 0, 48)
        lst = []
        for jt in range(16):
            kr = 2 * jt + qi // 64
            kc = qi % 64
            valid = ((kr[:, None] >= r0[None, :]) & (kr[:, None] < r0[None, :] + 8)
                     & (kc[:, None] >= w0[None, :]) & (kc[:, None] < w0[None, :] + 16))
            if not valid.any():
                continue
            ro = kr[:, None] - r[None, :] + 7
            co = kc[:, None] - c[None, :] + 15
            idx = np.where(valid, ro * 31 + co, -1)
            key = idx.tobytes()
            if key not in keymap:
                keymap[key] = len(pats)
                pats.append(idx)
            lst.append((jt, keymap[key]))
        tilemap[i] = lst
    return pats, tilemap


NA_PATS, NA_TILEMAP = _na_patterns()
N_PAT = len(NA_PATS)


def _rope_tables():
    t = np.arange(SEQ)
    row = (t // 64).astype(np.float32)
    col = (t % 64).astype(np.float32)
    inv = (np.float32(10000.0) ** (-np.arange(16, dtype=np.float32) / np.float32(16))).astype(np.float32)
    ang = np.concatenate([row[:, None] * inv, col[:, None] * inv], axis=-1).astype(np.float32)
    cos, sin = np.cos(ang).astype(np.float32), np.sin(ang).astype(np.float32)
    p = np.arange(128)
    d = p % 64
    m = d // 2
    Ct = cos[:, m].T.copy()
    St = (sin[:, m] * np.where(d % 2 == 0, -1.0, 1.0)[None, :]).T.astype(np.float32).copy()
    return Ct, St


class _Stop(Exception):
    pass


def build(n_layers=DEPTH, stop=None):
    nc = bass.Bass("TRN2", target_bir_lowering=False)

    def din(name, shape):
        return nc.dram_tensor(name, list(shape), F32, kind="ExternalInput").ap()

    x_d = din("x", [SEQ, D]); ctx_d = din("ctx", [CTX, D])
    scin_d = din("scin", [128, 16])
    adaw_d = din("ada_w", [DEPTH, D, 6 * D])
    adabT_d = din("ada_bT", [DEPTH, 128, 48]); adabf_d = din("ada_bf", [DEPTH, 1, 6 * D])
    nmix_d = din("nmix", [DEPTH, 128, 8]); nmlp_d = din("nmlp", [DEPTH, 128, 8]); nfin_d = din("nfin", [1, D])
    w1_d = din("w1", [DEPTH, D, 4 * D]); w2_d = din("w2", [DEPTH, 4 * D, D])
    abin_d = din("abin", [2, D, 3104]); about_d = din("about", [2, D, D])
    nab_d = din("nabias", [2, N_PAT, 128, 1024])
    wa2b_d = din("wa2b", [2, 33, 512]); gn_d = din("gn", [2, 1, 512])
    swin_d = din("swin", [2, D, 3328]); swout_d = din("swout", [2, D, D]); sink_d = din("sink", [2, 1, 16])
    ropeC_d = din("ropeC", [128, SEQ]); ropeS_d = din("ropeS", [128, SEQ])
    out_d = nc.dram_tensor("out", [SEQ, D], F32, kind="ExternalOutput").ap()
    xres_d = nc.dram_tensor("xres", [NTOK, D], F32, kind="Internal").ap()

    S = Sched(nc)
    ARENA_W = 52480
    arena_cm = nc.sbuf_tensor("arena", [128, ARENA_W], F32)
    arena = arena_cm.__enter__()
    ps_cm = [nc.psum_tensor(f"ps{i}", [128, 512], F32) for i in range(6)] + \
            [nc.psum_tensor(f"ps{i}", [128, 1024], BF16) for i in (6, 7)]
    PS = [c.__enter__() for c in ps_cm]

    def V(off, shape, dt, parts=128):
        n = int(np.prod(shape[1:]))
        assert off % 4 == 0
        if dt == F32:
            assert off // 4 + n <= ARENA_W, (off, shape)
            a = arena[0:parts, off // 4: off // 4 + n]
        else:
            assert n % 2 == 0 and off // 4 + n // 2 <= ARENA_W, (off, shape)
            a = arena[0:parts, off // 4: off // 4 + n // 2].bitcast(BF16)
        if len(shape) == 3:
            a = a.rearrange("p (a b) -> p a b", a=shape[1])
        elif len(shape) == 4:
            a = a.rearrange("p (a b c) -> p a b c", a=shape[1], b=shape[2])
        return a

    KB = 1024
    R0, R1, R2, R3 = 0, 72 * KB, 108 * KB, 144 * KB
    o = R3
    ident = V(o, [128, 128], BF16); o += 256
    Uf = V(o, [128, 128], F32); o += 512
    Ub = V(o, [128, 128], F32); o += 512
    Rf = V(o, [128, 128], F32); o += 512
    Rb = V(o, [128, 128], F32); o += 512
    mkf = V(o, [128, 128], F32); o += 512
    mkb = V(o, [128, 128], F32); o += 512
    bmp = V(o, [128, 128], BF16); o += 256
    bmn = V(o, [128, 128], BF16); o += 256
    scT = V(o, [128, 8, 2], BF16); o += 32
    scin = V(o, [128, 16], F32); o += 64
    ones_row = V(o, [128, 128], BF16); o += 256
    nmix = V(o, [128, DEPTH, 8], F32); o += 4 * DEPTH * 8
    nmlp = V(o, [128, DEPTH, 8], F32); o += 4 * DEPTH * 8
    adabT = V(o, [128, DEPTH, 48], F32); o += 4 * DEPTH * 48
    modT = V(o, [128, 4, 8, 2], F32); o += 256
    GpM = V(o, [128, 8, 2], F32); o += 64
    GpL = V(o, [128, 8, 2], F32); o += 64
    ssA = V(o, [128, 32], F32); o += 128
    rsA = V(o, [128, 32], F32); o += 128
    gate_bc = V(o, [128, 4, 1024], F32); o += 16 * KB
    R3T = o
    R3END = ARENA_W * 4

    def memset(e, ap, val, key):
        S.op(e, lambda en: en.memset(ap, val), writes=[key])

    def asel(ap, pattern, cm, base, op, key, fill=0.0):
        S.op('pool', lambda en: en.affine_select(out=ap, in_=ap, pattern=pattern, compare_op=op, fill=fill,
                                                 base=base, channel_multiplier=cm), reads=[key], writes=[key])

    memset('pool', ident, 1.0, 'ident')
    asel(ident, [[-1, 128]], 1, 0, ALU.is_equal, 'ident')
    memset('pool', ones_row, 1.0, 'ones_row')
    for (ap, key, pat, cm, base) in [(Uf, 'Uf', [[1, 128]], -1, 0), (Ub, 'Ub', [[-1, 128]], 1, 0),
                                     (Rf, 'Rf', [[-1, 128]], 1, -1), (Rb, 'Rb', [[1, 128]], -1, -1)]:
        memset('pool', ap, -1.0 / 16.0, key)
        asel(ap, pat, cm, base, ALU.is_ge, key)
    memset('pool', mkf, 1.0, 'mkf'); asel(mkf, [[1, 128]], -1, 0, ALU.is_ge, 'mkf')
    memset('pool', mkb, 1.0, 'mkb'); asel(mkb, [[-1, 128]], 1, 0, ALU.is_ge, 'mkb')
    memset('pool', bmp, 0.0, 'bmp'); asel(bmp, [[-1, 128]], 1, 0, ALU.is_ge, 'bmp', fill=-1e30)
    memset('pool', bmn, 0.0, 'bmn'); asel(bmn, [[1, 128]], -1, 0, ALU.is_ge, 'bmn', fill=-1e30)
    S.dma('sp', 'c0', scin, scin_d, writes=['scin'])
    S.dma('sp', 'c1', nmix, nmix_d.rearrange("l p c -> p l c"), writes=['nmix'])
    S.dma('sp', 'c2', nmlp, nmlp_d.rearrange("l p c -> p l c"), writes=['nmlp'])
    S.dma('sp', 'c3', adabT, adabT_d.rearrange("l p c -> p l c"), writes=['adabT'])
    S.op('act', lambda e: e.activation(out=scT.rearrange("p c w -> p w c"), in_=scin.rearrange("p (w c) -> p w c", w=2),
                                       func=AF.Silu), reads=['scin'], writes=['scT'])

    x_res = V(R0, [128, NT, D], F32)

    def tok_rows(i):
        return (x_d[i * 128:(i + 1) * 128, :] if i < 16 else ctx_d[(i - 16) * 128:(i - 15) * 128, :])

    def adaln(l):
        ob = R2
        adab = [V(ob, [128, 8, 1024], BF16), V(ob + 16 * KB, [128, 8, 1024], BF16)]
        ob += 32 * KB
        sc_rep = V(ob, [128, 8, 2, 128], BF16); ob += 4 * KB
        abf = V(R3T, [128, 2048], BF16)
        for kc in range(8):
            for w in range(2):
                S.op('dve', lambda e, kc=kc, w=w: e.tensor_copy(out=sc_rep[:, kc, w, :],
                                                                in_=scT[:, kc, w:w + 1].to_broadcast([128, 128])),
                     reads=['scT'], writes=[('sc_rep', kc, w)])
        S.dma('pool', 'abf0', abf[0:1, 0:1024], adabf_d[l, :, 2 * D:3 * D], writes=['abf0'])
        S.dma('pool', 'abf1', abf[0:1, 1024:2048], adabf_d[l, :, 5 * D:6 * D], writes=['abf1'])
        kind_of = {0: 0, 1: 1, 3: 2, 4: 3}
        for blk in range(6):
            bi = S.rot('adab', 2)
            buf = adab[bi]
            src = adaw_d[l, :, blk * D:(blk + 1) * D].rearrange("(c p) n -> p c n", p=128)
            for hq in range(2):
                S.dma('pool', f'adab{bi}_{hq}', buf[:, hq * 4:(hq + 1) * 4, :], src[:, hq * 4:(hq + 1) * 4, :],
                      writes=[('adab', bi, hq)])
            rk = [('adab', bi, 0), ('adab', bi, 1)]
            if blk in kind_of:
                pb = S.rot('psA', 2)
                ps = PS[pb]
                for j in range(8):
                    for kc in range(8):
                        S.op('pe', lambda e, j=j, kc=kc: e.matmul(ps[:, j * 2:(j + 1) * 2], lhsT=buf[:, kc, j * 128:(j + 1) * 128],
                                                                  rhs=scT[:, kc, :], start=(kc == 0), stop=(kc == 7)),
                             reads=rk + ['scT'], writes=[('ps', pb)])
                S.op('dve', lambda e, blk=blk: e.tensor_tensor(
                    out=modT[:, kind_of[blk], :, :], in0=ps[:, 0:16].rearrange("p (c w) -> p c w", w=2),
                    in1=adabT[:, l, blk * 8:(blk + 1) * 8].unsqueeze(2).to_broadcast([128, 8, 2]), op=ALU.add),
                    reads=[('ps', pb), 'adabT'], writes=[('modT', kind_of[blk])])
            else:
                gi = 0 if blk == 2 else 1
                for w in range(2):
                    for hf in range(2):
                        pb = S.rot('psA', 2)
                        ps = PS[pb]
                        for kc in range(8):
                            S.op('pe', lambda e, kc=kc, w=w, hf=hf: e.matmul(ps[:, :], lhsT=sc_rep[:, kc, w, :],
                                                                           rhs=buf[:, kc, hf * 512:(hf + 1) * 512],
                                                                           start=(kc == 0), stop=False),
                                 reads=rk + [('sc_rep', kc, w)], writes=[('ps', pb)])
                        S.op('pe', lambda e, hf=hf, gi=gi: e.matmul(ps[:, :], lhsT=ones_row[0:1, :],
                                                                  rhs=abf[0:1, gi * 1024 + hf * 512: gi * 1024 + (hf + 1) * 512],
                                                                  start=False, stop=True),
                             reads=['ones_row', 'abf0', 'abf1'], writes=[('ps', pb)])
                        S.op('act', lambda e, w=w, hf=hf, gi=gi: e.activation(out=gate_bc[:, gi * 2 + w, hf * 512:(hf + 1) * 512],
                                                                           in_=ps[:, :], func=AF.Copy),
                             reads=[('ps', pb)], writes=[('gate', gi * 2 + w, hf)])
        for (Gp, nrm, kind, key) in [(GpM, nmix, 1, 'GpM'), (GpL, nmlp, 3, 'GpL')]:
            S.op('dve', lambda e, Gp=Gp, nrm=nrm, kind=kind: e.scalar_tensor_tensor(
                out=Gp[:, :, :], in0=modT[:, kind, :, :], scalar=1.0,
                in1=nrm[:, l, :].unsqueeze(2).to_broadcast([128, 8, 2]), op0=ALU.add, op1=ALU.mult),
                reads=[('modT', kind), 'nmix', 'nmlp'], writes=[key])

    def norm_to_hT(hT, Gp, gkey, shift_kind, tiles, tmp_off):
        junk = V(tmp_off, [128, 1024], F32)
        xn = [V(tmp_off + 4 * KB, [128, 1024], BF16), V(tmp_off + 6 * KB, [128, 1024], BF16)]
        for i in tiles:
            S.op('act', lambda e, i=i: e.activation(out=junk, in_=x_res[:, i, :], func=AF.Square, accum_out=ssA[:, i:i + 1]),
                 reads=[('xres', i)], writes=['junk', ('ssA', i)])
        n = len(tiles)
        t0 = tiles[0]
        S.op('act', lambda e: e.activation(out=rsA[:, t0:t0 + n], in_=ssA[:, t0:t0 + n], func=AF.Sqrt, scale=1.0 / D, bias=EPS),
             reads=[('ssA', i) for i in tiles], writes=['rsA_t'])
        S.op('dve', lambda e: e.reciprocal(out=rsA[:, t0:t0 + n], in_=rsA[:, t0:t0 + n]), reads=['rsA_t'], writes=['rsA'])
        for i in tiles:
            b = S.rot('xn', 2)
            w = 0 if i < 16 else 1
            S.op('dve', lambda e, i=i, b=b: e.tensor_scalar(out=xn[b], in0=x_res[:, i, :], scalar1=rsA[:, i:i + 1], scalar2=None,
                                                          op0=ALU.mult), reads=[('xres', i), 'rsA'], writes=[('xn', b)])
            pb = 6 + S.rot('psT', 2)
            pst = PS[pb]
            for c in range(8):
                S.op('pe', lambda e, c=c, b=b: e.transpose(out=pst[:, c * 128:(c + 1) * 128], in_=xn[b][:, c * 128:(c + 1) * 128],
                                                         identity=ident), reads=[('xn', b), 'ident'], writes=[('ps', pb)])
            use_act = (S.rot('nev', 2) == 0)
            for c in range(8):
                if use_act:
                    S.op('act', lambda e, c=c, i=i, w=w: e.activation(
                        out=hT[:, c, i * 128:(i + 1) * 128], in_=pst[:, c * 128:(c + 1) * 128], func=AF.Identity,
                        scale=Gp[:, c, w:w + 1], bias=modT[:, shift_kind, c, w:w + 1]),
                        reads=[('ps', pb), gkey, ('modT', shift_kind)], writes=[('hT', c, i)])
                else:
                    S.op('dve', lambda e, c=c, i=i, w=w: e.tensor_scalar(
                        out=hT[:, c, i * 128:(i + 1) * 128], in0=pst[:, c * 128:(c + 1) * 128],
                        scalar1=Gp[:, c, w:w + 1], scalar2=modT[:, shift_kind, c, w:w + 1], op0=ALU.mult, op1=ALU.add),
                        reads=[('ps', pb), gkey, ('modT', shift_kind)], writes=[('hT', c, i)])

    def load_w(buf, src, ncols, key):
        sv = src.rearrange("(c p) n -> p c n", p=128)
        for hq in range(2):
            S.dma('pool', key[0] + str(key[1]) + '_' + str(hq), buf[:, hq * 4:(hq + 1) * 4, 0:ncols], sv[:, hq * 4:(hq + 1) * 4, :],
                  writes=[(key, hq)])
        return [(key, 0), (key, 1)]

    def proj_fm(hT, wbuf, wkeys, c0, M, evac, tiles_hi=NTOK):
        for (t0, t1) in GROUPS:
            if t0 >= tiles_hi:
                continue
            pb = S.rot('psP', 4)
            ps = PS[pb]
            for kc in range(8):
                S.op('pe', lambda e, kc=kc: e.matmul(ps[0:M, 0:t1 - t0], lhsT=wbuf[:, kc, c0:c0 + M], rhs=hT[:, kc, t0:t1],
                                                    start=(kc == 0), stop=(kc == 7)),
                     reads=wkeys + [('hT', kc, i) for i in range(t0 // 128, t1 // 128)], writes=[('ps', pb)])
            evac(ps, pb, t0, t1)

    def proj_tm(hT, wbuf, wkeys, c0, n, evac, tiles):
        for i in tiles:
            pb = S.rot('psP', 4)
            ps = PS[pb]
            for kc in range(8):
                S.op('pe', lambda e, kc=kc: e.matmul(ps[:, 0:n], lhsT=hT[:, kc, i * 128:(i + 1) * 128], rhs=wbuf[:, kc, c0:c0 + n],
                                                    start=(kc == 0), stop=(kc == 7)),
                     reads=wkeys + [('hT', kc, i)], writes=[('ps', pb)])
            evac(ps, pb, i)

    def ev_copy(eng, out_ap, in_ap, pb, wkey, scale=None):
        if eng == 'act':
            if scale is None:
                S.op('act', lambda e: e.activation(out=out_ap, in_=in_ap, func=AF.Copy), reads=[('ps', pb)], writes=[wkey])
            else:
                S.op('act', lambda e: e.activation(out=out_ap, in_=in_ap, func=AF.Copy, scale=scale), reads=[('ps', pb)], writes=[wkey])
        else:
            if scale is None:
                S.op(eng, lambda e: e.tensor_copy(out=out_ap, in_=in_ap), reads=[('ps', pb)], writes=[wkey])
            else:
                S.op(eng, lambda e: e.tensor_scalar(out=out_ap, in0=in_ap, scalar1=scale, scalar2=None, op0=ALU.mult),
                     reads=[('ps', pb)], writes=[wkey])

    def alt(name):
        return 'act' if S.rot(name, 2) == 0 else 'dve'

    def transpose_otok(o_tok, OT, tiles):
        for i in tiles:
            pb = 6 + S.rot('psT', 2)
            pst = PS[pb]
            for c in range(8):
                S.op('pe', lambda e, c=c: e.transpose(out=pst[:, c * 128:(c + 1) * 128], in_=o_tok[:, i, c * 128:(c + 1) * 128],
                                                    identity=ident), reads=[('otok', i), 'ident'], writes=[('ps', pb)])
            eng = alt('otev')
            ev_copy(eng, OT[:, :, i * 128:(i + 1) * 128], pst.rearrange("p (c t) -> p c t", c=8), pb, ('OT', i))

    def even_mixer(j, hT, o_tok):
        wb = [V(R3T, [128, 8, 512], BF16), V(R3T + 8 * KB, [128, 8, 512], BF16)]
        qaT = V(R0, [128, 4, NTOK], BF16)
        kaT = V(R0 + 18 * KB, [128, 4, NTOK], BF16)
        va = V(R0 + 36 * KB, [128, NT, 8, 65], BF16)
        S.op('pool', lambda e: e.memset(va[:, :, :, 64:65], 1.0), writes=['va1'])
        for blk in range(3):
            bi = S.rot('wb', 2)
            wk = load_w(wb[bi], abin_d[j, :, blk * 512:(blk + 1) * 512], 512, ('wb', bi))
            if blk < 2:
                dst = qaT if blk == 0 else kaT
                nm = 'qaT' if blk == 0 else 'kaT'
                sc = 0.125 if blk == 0 else None
                for c in range(4):
                    def evac(ps, pb, t0, t1, c=c, dst=dst, nm=nm, sc=sc):
                        ev_copy(alt('ev'), dst[:, c, t0:t1], ps[:, 0:t1 - t0], pb, (nm, c, t0), scale=sc)
                    proj_fm(hT, wb[bi], wk, c * 128, 128, evac)
            else:
                def evac(ps, pb, i):
                    ev_copy(alt('ev'), va[:, i, :, 0:64], ps[:, 0:512].rearrange("p (h d) -> p h d", h=8), pb, ('va', i))
                proj_tm(hT, wb[bi], wk, 0, 512, evac, range(NT))
        bt = [V(R3T + 16 * KB, [128, 5, 8, 128], BF16), V(R3T + 26 * KB, [128, 5, 8, 128], BF16)]
        pT = [V(R0 + 55 * KB, [128, 7, 128], BF16), V(R0 + 55 * KB + 1792, [128, 7, 128], BF16)]
        rec = V(R0 + 59 * KB, [128, 8, 1], F32)
        assert R3T + 36 * KB <= R3END
        for i in range(NT):
            if i < 16:
                blocks = [(jt, pat) for (jt, pat) in NA_TILEMAP[i]] + [(16, None), (17, None)]
                bb = S.rot('bt', 2)
                for bi_, (jt, pat) in enumerate(NA_TILEMAP[i]):
                    S.dma('pool', f'bt{bb}_{bi_}', bt[bb][:, bi_, :, :], nab_d[j, pat, :, :].rearrange("k (h q) -> k h q", h=8),
                          writes=[('bt', bb, bi_)])
            else:
                blocks = [(16, None), (17, None)]
            nb = len(blocks)
            for h in range(8):
                p, pbs = h // 2, 64 * (h % 2)
                par = S.rot('nasc', 2)
                banks = [2 * par, 2 * par + 1]
                for bi_, (jt, pat) in enumerate(blocks):
                    bk = banks[bi_ // 4]
                    dst = PS[bk][:, (bi_ % 4) * 128:(bi_ % 4 + 1) * 128]
                    rk = [('kaT', p, t0) for (t0, t1) in GROUPS if t0 <= jt * 128 < t1] + \
                         [('qaT', p, t0) for (t0, t1) in GROUPS if t0 <= i * 128 < t1]
                    if pat is not None:
                        S.op('pe', lambda e, dst=dst, bi_=bi_, h=h, bb=bb: e.matmul(dst, lhsT=ident, rhs=bt[bb][:, bi_, h, :],
                                                                                 start=True, stop=False),
                             reads=['ident', ('bt', bb, bi_)], writes=[('ps', bk)])
                    S.op('pe', lambda e, dst=dst, jt=jt, p=p, pbs=pbs, pat=pat: e.matmul(
                        dst, lhsT=kaT[pbs:pbs + 64, p, jt * 128:(jt + 1) * 128], rhs=qaT[pbs:pbs + 64, p, i * 128:(i + 1) * 128],
                        start=(pat is None), stop=True), reads=rk, writes=[('ps', bk)])
                n0 = min(nb, 4)
                S.op('act', lambda e, par=par, n0=n0: e.activation(out=pT[par][:, 0:n0, :], in_=PS[2 * par][:, 0:n0 * 128].rearrange(
                    "p (b q) -> p b q", b=n0), func=AF.Exp), reads=[('ps', 2 * par)], writes=[('pT', par, 0)])
                if nb > 4:
                    n1 = nb - 4
                    S.op('act', lambda e, par=par, n1=n1: e.activation(out=pT[par][:, 4:4 + n1, :], in_=PS[2 * par + 1][:, 0:n1 * 128].rearrange(
                        "p (b q) -> p b q", b=n1), func=AF.Exp), reads=[('ps', 2 * par + 1)], writes=[('pT', par, 1)])
                ob = 4 + h // 4
                od = PS[ob][:, (h % 4) * 65:(h % 4) * 65 + 65]
                for bi_, (jt, pat) in enumerate(blocks):
                    S.op('pe', lambda e, od=od, bi_=bi_, jt=jt, h=h, par=par: e.matmul(od, lhsT=pT[par][:, bi_, :], rhs=va[:, jt, h, :],
                                                                                     start=(bi_ == 0), stop=(bi_ == nb - 1)),
                         reads=[('pT', par, 0), ('pT', par, 1), ('va', jt), 'va1'], writes=[('ps', ob)])
            for hf in range(2):
                ob = 4 + hf
                ov = PS[ob][:, 0:260].rearrange("p (h e) -> p h e", e=65)
                S.op('dve', lambda e, ov=ov, hf=hf: e.reciprocal(out=rec[:, hf * 4:(hf + 1) * 4, :], in_=ov[:, :, 64:65]),
                     reads=[('ps', ob)], writes=[('rec', hf)])
                S.op('dve', lambda e, ov=ov, hf=hf: e.tensor_tensor(
                    out=o_tok[:, i, hf * 256:(hf + 1) * 256].rearrange("p (h d) -> p h d", h=4), in0=ov[:, :, 0:64],
                    in1=rec[:, hf * 4:(hf + 1) * 4, :].to_broadcast([128, 4, 64]), op=ALU.mult),
                    reads=[('ps', ob), ('rec', hf)], writes=[('otok', i)])
        S.barrier()
        if stop == 'na':
            raise _Stop()
        qbT = V(R0, [128, 2, NTOK], BF16)
        kbT = V(R0 + 9 * KB, [128, 2, NTOK], BF16)
        kbk = V(R0 + 18 * KB, [128, NT, 256], BF16)
        vb = V(R0 + 27 * KB, [128, NT, 512], BF16)
        sgg = V(R0 + 45 * KB, [128, NT, 512], BF16)
        lrT1 = V(R0 + 63 * KB, [128, NTOK], BF16)
        W2b = V(R0 + 68 * KB, [128, 512], BF16)
        gnbc = V(R0 + 69 * KB, [128, 512], F32)
        S.dma('pool', 'w2b', W2b[0:33, :], wa2b_d[j], writes=['W2b'])
        S.dma('sp', 'gn', gnbc, gn_d[j].broadcast_to([128, 512]), writes=['gnbc'])
        S.op('pool', lambda e: e.memset(lrT1[32:33, :], 1.0), writes=['lr1'])
        sgt = [V(R3T + 16 * KB, [128, 512], F32), V(R3T + 18 * KB, [128, 512], F32)]
        bi = S.rot('wb', 2)
        wk = load_w(wb[bi], abin_d[j, :, 1536:2048], 512, ('wb', bi))
        for c in range(2):
            def evq(ps, pb, t0, t1, c=c):
                ev_copy(alt('ev'), qbT[:, c, t0:t1], ps[:, 0:t1 - t0], pb, ('qbT', c, t0), scale=0.125)
            proj_fm(hT, wb[bi], wk, c * 128, 128, evq)
            def evk(ps, pb, t0, t1, c=c):
                ev_copy(alt('ev'), kbT[:, c, t0:t1], ps[:, 0:t1 - t0], pb, ('kbT', c, t0))
            proj_fm(hT, wb[bi], wk, 256 + c * 128, 128, evk)
        def evkk(ps, pb, i):
            ev_copy(alt('ev'), kbk[:, i, :], ps[:, 0:256], pb, ('kbk', i))
        proj_tm(hT, wb[bi], wk, 256, 256, evkk, range(NT))
        bi = S.rot('wb', 2)
        wk = load_w(wb[bi], abin_d[j, :, 2048:2560], 512, ('wb', bi))
        def evv(ps, pb, i):
            ev_copy(alt('ev'), vb[:, i, :], ps[:, 0:512], pb, ('vb', i))
        proj_tm(hT, wb[bi], wk, 0, 512, evv, range(NT))
        bi = S.rot('wb', 2)
        wk = load_w(wb[bi], abin_d[j, :, 2560:3072], 512, ('wb', bi))
        def evg(ps, pb, i):
            b = S.rot('sgt', 2)
            S.op('act', lambda e: e.activation(out=sgt[b], in_=ps[:, 0:512], func=AF.Silu), reads=[('ps', pb)], writes=[('sgt', b)])
            S.op('pool', lambda e: e.tensor_tensor(out=sgg[:, i, :], in0=sgt[b], in1=gnbc, op=ALU.mult),
                 reads=[('sgt', b), 'gnbc'], writes=[('sgg', i)])
        proj_tm(hT, wb[bi], wk, 0, 512, evg, range(NT))
        bi = S.rot('wb', 2)
        wk = load_w(wb[bi], abin_d[j, :, 3072:3104], 32, ('wb', bi))
        def evl(ps, pb, t0, t1):
            ev_copy(alt('ev'), lrT1[0:32, t0:t1], ps[0:32, 0:t1 - t0], pb, ('lrT', t0))
        proj_fm(hT, wb[bi], wk, 0, 32, evl)
        S.barrier()
        if stop == 'glaproj':
            raise _Stop()
        import os
        GP = int(os.environ.get('GLA_PART', '99'))
        oacc = V(R1, [128, NT, 512], F32)
        t = R3T
        E32 = V(t, [128, 256], F32); t += KB
        L32 = V(t, [128, 256], F32); t += KB
        eT = V(t, [128, 2, 128], F32); t += KB
        enT = V(t, [128, 2, 128], F32); t += KB
        krem = V(t, [128, 256], F32); t += KB
        qtT = V(t, [128, 2, 128], BF16); t += 512
        ktT = V(t, [128, 2, 128], BF16); t += 512
        kend = V(t, [128, 256], BF16); t += 512
        Abf = V(t, [128, 4, 128], BF16); t += KB
        Sst = V(t, [128, 2, 128], F32); t += KB
        Sbf = V(t, [128, 2, 128], BF16); t += 512
        sq = V(t, [128, 512], F32); t += 2 * KB
        t1b = V(t, [128, 512], F32); t += 2 * KB
        ss4 = V(t, [128, 4], F32); t += 16
        rs4 = V(t, [128, 4], F32); t += 16
        for d in range(2):
            Ud, Rd, mk = (Uf, Rf, mkf) if d == 0 else (Ub, Rb, mkb)
            Uk, Rk, mkk = ('Uf', 'Rf', 'mkf') if d == 0 else ('Ub', 'Rb', 'mkb')
            last = 127 if d == 0 else 0
            order = [16, 17] + list(range(16)) if d == 0 else [17, 16] + list(range(15, -1, -1))
            S.op('pool', lambda e: e.memset(Sst, 0.0), writes=['Sst0', 'Sst1'])
            S.op('pool', lambda e: e.memset(Sbf, 0.0), writes=['Sbf0', 'Sbf1'])
            for ti in order:
                tc0 = ti * 128
                tg = [t0 for (t0, t1_) in GROUPS if t0 <= tc0 < t1_][0]
                S.op('pe', lambda e: e.matmul(PS[0][:, 0:256], lhsT=lrT1[0:33, tc0:tc0 + 128], rhs=W2b[0:33, d * 256:(d + 1) * 256],
                                              start=True, stop=True), reads=[('lrT', tg), 'lr1', 'W2b'], writes=[('ps', 0)])
                S.op('act', lambda e: e.activation(out=E32, in_=PS[0][:, 0:256], func=AF.Exp, scale=-1.0), reads=[('ps', 0)], writes=['E32'])
                S.op('act', lambda e: e.activation(out=L32, in_=E32, func=AF.Ln, bias=1.0), reads=['E32'], writes=['L32'])
                if GP < 2:
                    continue
                for p in range(2):
                    S.op('pe', lambda e, p=p: e.matmul(PS[1][:, p * 128:(p + 1) * 128], lhsT=L32[:, p * 128:(p + 1) * 128], rhs=Ud,
                                                      start=True, stop=True), reads=['L32', Uk], writes=[('ps', 1)])
                S.op('pe', lambda e: e.matmul(PS[2][:, 0:256], lhsT=Rd, rhs=L32, start=True, stop=True), reads=['L32', Rk], writes=[('ps', 2)])
                if GP < 3:
                    continue
                S.op('act', lambda e: e.activation(out=eT, in_=PS[1][:, 0:256].rearrange("p (a b) -> p a b", a=2), func=AF.Exp),
                     reads=[('ps', 1)], writes=['eT'])
                S.op('act', lambda e: e.activation(out=enT, in_=PS[1][:, 0:256].rearrange("p (a b) -> p a b", a=2), func=AF.Exp, scale=-1.0),
                     reads=[('ps', 1)], writes=['enT'])
                S.op('act', lambda e: e.activation(out=krem, in_=PS[2][:, 0:256], func=AF.Exp), reads=[('ps', 2)], writes=['krem'])
                S.op('dve', lambda e: e.tensor_tensor(out=qtT, in0=qbT[:, :, tc0:tc0 + 128], in1=eT, op=ALU.mult),
                     reads=[('qbT', 0, tg), ('qbT', 1, tg), 'eT'], writes=['qtT'])
                S.op('pool', lambda e: e.tensor_tensor(out=ktT, in0=kbT[:, :, tc0:tc0 + 128], in1=enT, op=ALU.mult),
                     reads=[('kbT', 0, tg), ('kbT', 1, tg), 'enT'], writes=['ktT'])
                S.op('pool', lambda e: e.tensor_tensor(out=kend, in0=kbk[:, ti, :], in1=krem, op=ALU.mult),
                     reads=[('kbk', ti), 'krem'], writes=['kend'])
                if GP < 4:
                    continue
                PA = [PS[3], PS[6][:, :].bitcast(F32)]
                PO = [PS[4], PS[7][:, :].bitcast(F32)]
                pak = [3, 6]
                pok = [4, 7]
                for h in range(4):
                    p, pbs = h // 2, 64 * (h % 2)
                    S.op('pe', lambda e, h=h, p=p, pbs=pbs: e.matmul(PA[h % 2][:, p * 128:(p + 1) * 128], lhsT=ktT[pbs:pbs + 64, p, :],
                                                                  rhs=qtT[pbs:pbs + 64, p, :], start=True, stop=True),
                         reads=['ktT', 'qtT'], writes=[('ps', pak[h % 2])])
                for h in range(4):
                    S.op('dve', lambda e, h=h: e.tensor_tensor(out=Abf[:, h, :], in0=PA[h % 2][:, (h // 2) * 128:(h // 2 + 1) * 128],
                                                               in1=mk, op=ALU.mult),
                         reads=[('ps', pak[h % 2]), mkk], writes=['Abf'])
                if GP < 5:
                    continue
                for h in range(4):
                    p, pbs = h // 2, 64 * (h % 2)
                    S.op('pe', lambda e, h=h, p=p: e.matmul(PO[h % 2][:, p * 128:(p + 1) * 128], lhsT=Abf[:, h, :], rhs=vb[:, ti, h * 128:(h + 1) * 128],
                                                           start=True, stop=False), reads=['Abf', ('vb', ti)], writes=[('ps', pok[h % 2])])
                    S.op('pe', lambda e, h=h, p=p, pbs=pbs: e.matmul(PO[h % 2][:, p * 128:(p + 1) * 128], lhsT=qtT[pbs:pbs + 64, p, :],
                                                                  rhs=Sbf[pbs:pbs + 64, p, :], start=False, stop=True),
                         reads=['qtT', 'Sbf0', 'Sbf1'], writes=[('ps', pok[h % 2])])
                for par in range(2):
                    ov_ = oacc[:, ti, :].rearrange("p (a b v) -> p a b v", a=2, b=2)[:, :, par, :]
                    pv_ = PO[par][:, 0:256].rearrange("p (a v) -> p a v", a=2)
                    if d == 0:
                        S.op('act', lambda e, ov_=ov_, pv_=pv_: e.activation(out=ov_, in_=pv_, func=AF.Copy),
                             reads=[('ps', pok[par])], writes=[('oacc', ti, par)])
                    else:
                        S.op('dve', lambda e, ov_=ov_, pv_=pv_: e.tensor_tensor(out=ov_, in0=pv_, in1=ov_, op=ALU.add),
                             reads=[('ps', pok[par]), ('oacc', ti, par)], writes=[('oacc', ti, par)])
                if GP < 6:
                    continue
                for p in range(2):
                    S.op('pe', lambda e, p=p: e.matmul(PS[5][:, p * 256:(p + 1) * 256], lhsT=kend[:, p * 128:(p + 1) * 128],
                                                      rhs=vb[:, ti, p * 256:(p + 1) * 256], start=True, stop=True),
                         reads=['kend', ('vb', ti)], writes=[('ps', 5)])
                for p in range(2):
                    for hp in range(2):
                        r0 = hp * 64
                        S.op('dve', lambda e, p=p, hp=hp, r0=r0: e.scalar_tensor_tensor(
                            out=Sst[r0:r0 + 64, p, :], in0=Sst[r0:r0 + 64, p, :], scalar=eT[r0:r0 + 64, p, last:last + 1],
                            in1=PS[5][r0:r0 + 64, p * 256 + hp * 128: p * 256 + (hp + 1) * 128], op0=ALU.mult, op1=ALU.add),
                            reads=[('ps', 5), 'eT', f'Sst{hp}'], writes=[f'Sst{hp}'])
                for hp in range(2):
                    r0 = hp * 64
                    S.op('act', lambda e, r0=r0: e.activation(out=Sbf[r0:r0 + 64, :, :], in_=Sst[r0:r0 + 64, :, :], func=AF.Copy),
                         reads=[f'Sst{hp}'], writes=[f'Sbf{hp}'])
                if GP < 7:
                    continue
                if d == 1:
                    S.op('dve', lambda e: e.tensor_tensor(out=sq, in0=oacc[:, ti, :], in1=oacc[:, ti, :], op=ALU.mult),
                         reads=[('oacc', ti, 0), ('oacc', ti, 1)], writes=['sq'])
                    S.op('dve', lambda e: e.tensor_reduce(out=ss4, in_=sq.rearrange("p (h v) -> p h v", h=4), axis=AX.X, op=ALU.add),
                         reads=['sq'], writes=['ss4'])
                    S.op('act', lambda e: e.activation(out=rs4, in_=ss4, func=AF.Ln, scale=1.0 / 128.0, bias=EPS), reads=['ss4'], writes=['rs4t'])
                    S.op('act', lambda e: e.activation(out=rs4, in_=rs4, func=AF.Exp, scale=-0.5), reads=['rs4t'], writes=['rs4'])
                    S.op('dve', lambda e: e.tensor_tensor(out=t1b.rearrange("p (h v) -> p h v", h=4),
                                                          in0=oacc[:, ti, :].rearrange("p (h v) -> p h v", h=4),
                                                          in1=rs4.unsqueeze(2).to_broadcast([128, 4, 128]), op=ALU.mult),
                         reads=[('oacc', ti, 0), ('oacc', ti, 1), 'rs4'], writes=['t1b'])
                    S.op('pool', lambda e: e.tensor_tensor(out=o_tok[:, ti, 512:1024], in0=t1b, in1=sgg[:, ti, :], op=ALU.mult),
                         reads=['t1b', ('sgg', ti)], writes=[('otok', ti)])
        S.barrier()

    def odd_mixer(j, hT, o_tok, ctx_out):
        wb = [V(R3T, [128, 8, 512], BF16), V(R3T + 8 * KB, [128, 8, 512], BF16)]
        ropeC = V(R3T + 16 * KB, [128, SEQ], F32)
        ropeS = V(R3T + 24 * KB, [128, SEQ], F32)
        pT = V(R0 + 66 * KB, [128, 5, 4, 128], BF16)
        es = V(R3T + 32 * KB, [128, 16], F32)
        den = V(R3T + 32 * KB + 64, [128, 4, 1], F32)
        tq = [V(R0 + 62 * KB, [128, 512], F32), V(R0 + 64 * KB, [128, 512], F32)]
        assert R3T + 33 * KB <= R3END
        S.dma('sp', 'rc', ropeC, ropeC_d, writes=['ropeC'])
        S.dma('sp', 'rs', ropeS, ropeS_d, writes=['ropeS'])
        S.dma('sp', 'sk', es, sink_d[j].broadcast_to([128, 16]), writes=['es_raw'])
        S.op('act', lambda e: e.activation(out=es, in_=es, func=AF.Exp), reads=['es_raw'], writes=['es'])
        kT2 = V(R0, [128, 4, NTOK], BF16)
        krT2 = V(R0 + 18 * KB, [128, 4, SEQ], BF16)
        v65 = V(R0 + 34 * KB, [128, NT, 4, 65], BF16)
        qT = V(R0 + 44 * KB, [128, 2, NTOK], BF16)
        qrT = V(R0 + 53 * KB, [128, 2, SEQ], BF16)
        S.op('pool', lambda e: e.memset(v65[:, :, :, 64:65], 1.0), writes=['v1'])
        ntl = NT if ctx_out else 16

        def rope_evac(ps_a, pb_a, ps_b, pb_b, t0, t1, dst, key):
            n = t1 - t0
            b = S.rot('tq', 2)
            S.op('dve', lambda e: e.tensor_tensor(out=tq[b][:, 0:n], in0=ps_a[:, 0:n], in1=ropeC[:, t0:t1], op=ALU.mult),
                 reads=[('ps', pb_a), 'ropeC'], writes=[('tq', b)])
            b2 = S.rot('tq', 2)
            S.op('dve', lambda e: e.tensor_tensor(out=tq[b2][:, 0:n], in0=ps_b[:, 0:n], in1=ropeS[:, t0:t1], op=ALU.mult),
                 reads=[('ps', pb_b), 'ropeS'], writes=[('tq', b2)])
            S.op('pool', lambda e: e.tensor_tensor(out=dst, in0=tq[b][:, 0:n], in1=tq[b2][:, 0:n], op=ALU.add),
                 reads=[('tq', b), ('tq', b2)], writes=[key])

        def proj_pair(wbuf, wk, ca, cb, scale, dst_plain, dst_rope, nm, c):
            for (t0, t1) in GROUPS:
                n = t1 - t0
                pa = S.rot('psP', 4); psa = PS[pa]
                rk = wk + [('hT', kc, i) for kc in range(8) for i in range(t0 // 128, t1 // 128)]
                for kc in range(8):
                    S.op('pe', lambda e, kc=kc: e.matmul(psa[:, 0:n], lhsT=wbuf[:, kc, ca:ca + 128], rhs=hT[:, kc, t0:t1],
                                                        start=(kc == 0), stop=(kc == 7)), reads=rk, writes=[('ps', pa)])
                ev_copy('act', dst_plain[:, c, t0:t1], psa[:, 0:n], pa, (nm, c, t0), scale=scale)
                if t0 < SEQ:
                    pb2 = S.rot('psP', 4); psb = PS[pb2]
                    for kc in range(8):
                        S.op('pe', lambda e, kc=kc: e.matmul(psb[:, 0:n], lhsT=wbuf[:, kc, cb:cb + 128], rhs=hT[:, kc, t0:t1],
                                                            start=(kc == 0), stop=(kc == 7)), reads=rk, writes=[('ps', pb2)])
                    rope_evac(psa, pa, psb, pb2, t0, t1, dst_rope[:, c, t0:t1], (nm + 'r', c, t0))

        bi = S.rot('wb', 2); wk = load_w(wb[bi], swin_d[j, :, 2048:2560], 512, ('wb', bi))
        bi2 = S.rot('wb', 2); wk2 = load_w(wb[bi2], swin_d[j, :, 2560:3072], 512, ('wb', bi2))
        for g in range(4):
            for (t0, t1) in GROUPS:
                n = t1 - t0
                pa = S.rot('psP', 4); psa = PS[pa]
                rk = [('hT', kc, i) for kc in range(8) for i in range(t0 // 128, t1 // 128)]
                for kc in range(8):
                    S.op('pe', lambda e, kc=kc: e.matmul(psa[:, 0:n], lhsT=wb[bi][:, kc, g * 128:(g + 1) * 128], rhs=hT[:, kc, t0:t1],
                                                        start=(kc == 0), stop=(kc == 7)), reads=wk + rk, writes=[('ps', pa)])
                ev_copy('dve', kT2[:, g, t0:t1], psa[:, 0:n], pa, ('kT2', g, t0))
                if t0 < SEQ:
                    pb2 = S.rot('psP', 4); psb = PS[pb2]
                    for kc in range(8):
                        S.op('pe', lambda e, kc=kc: e.matmul(psb[:, 0:n], lhsT=wb[bi2][:, kc, g * 128:(g + 1) * 128], rhs=hT[:, kc, t0:t1],
                                                            start=(kc == 0), stop=(kc == 7)), reads=wk2 + rk, writes=[('ps', pb2)])
                    rope_evac(psa, pa, psb, pb2, t0, t1, krT2[:, g, t0:t1], ('krT2', g, t0))
        bi = S.rot('wb', 2); wk = load_w(wb[bi], swin_d[j, :, 3072:3328], 256, ('wb', bi))
        def evv(ps, pb, i):
            ev_copy(alt('ev'), v65[:, i, :, 0:64], ps[:, 0:256].rearrange("p (h d) -> p h d", h=4), pb, ('v65', i))
        proj_tm(hT, wb[bi], wk, 0, 256, evv, range(NT))

        for g in range(4):
            bi = S.rot('wb', 2); wk = load_w(wb[bi], swin_d[j, :, g * 256:(g + 1) * 256], 256, ('wb', bi))
            bi2 = S.rot('wb', 2); wk2 = load_w(wb[bi2], swin_d[j, :, 1024 + g * 256:1024 + (g + 1) * 256], 256, ('wb', bi2))
            for c in range(2):
                for (t0, t1) in GROUPS:
                    n = t1 - t0
                    pa = S.rot('psP', 4); psa = PS[pa]
                    rk = [('hT', kc, i) for kc in range(8) for i in range(t0 // 128, t1 // 128)]
                    for kc in range(8):
                        S.op('pe', lambda e, kc=kc: e.matmul(psa[:, 0:n], lhsT=wb[bi][:, kc, c * 128:(c + 1) * 128], rhs=hT[:, kc, t0:t1],
                                                            start=(kc == 0), stop=(kc == 7)), reads=wk + rk, writes=[('ps', pa)])
                    ev_copy('dve', qT[:, c, t0:t1], psa[:, 0:n], pa, ('qT', c, t0), scale=0.125)
                    if t0 < SEQ:
                        pb2 = S.rot('psP', 4); psb = PS[pb2]
                        for kc in range(8):
                            S.op('pe', lambda e, kc=kc: e.matmul(psb[:, 0:n], lhsT=wb[bi2][:, kc, c * 128:(c + 1) * 128], rhs=hT[:, kc, t0:t1],
                                                                start=(kc == 0), stop=(kc == 7)), reads=wk2 + rk, writes=[('ps', pb2)])
                        n_ = n
                        b = S.rot('tq', 2)
                        S.op('dve', lambda e, b=b: e.scalar_tensor_tensor(out=tq[b][:, 0:n_], in0=psa[:, 0:n_], scalar=0.125,
                                                                          in1=ropeC[:, t0:t1], op0=ALU.mult, op1=ALU.mult),
                             reads=[('ps', pa), 'ropeC'], writes=[('tq', b)])
                        b2 = S.rot('tq', 2)
                        S.op('dve', lambda e, b2=b2: e.scalar_tensor_tensor(out=tq[b2][:, 0:n_], in0=psb[:, 0:n_], scalar=0.125,
                                                                            in1=ropeS[:, t0:t1], op0=ALU.mult, op1=ALU.mult),
                             reads=[('ps', pb2), 'ropeS'], writes=[('tq', b2)])
                        S.op('pool', lambda e, b=b, b2=b2: e.tensor_tensor(out=qrT[:, c, t0:t1], in0=tq[b][:, 0:n_], in1=tq[b2][:, 0:n_], op=ALU.add),
                             reads=[('tq', b), ('tq', b2)], writes=[('qrT', c, t0)])
            for i in range(ntl):
                if i < 16:
                    blocks = []
                    if i > 0:
                        blocks.append((i - 1, 'p'))
                    blocks.append((i, 'l'))
                    if i < 15:
                        blocks.append((i + 1, 'n'))
                    blocks += [(16, 'c'), (17, 'c')]
                else:
                    blocks = [(16, 'c'), (17, 'c')]
                nb = len(blocks)
                tgq = [t0 for (t0, t1_) in GROUPS if t0 <= i * 128 < t1_][0]
                for bi_, (jt, kind) in enumerate(blocks):
                    tgk = [t0 for (t0, t1_) in GROUPS if t0 <= jt * 128 < t1_][0]
                    roped = kind in ('p', 'l', 'n')
                    ksrc = krT2 if roped else kT2
                    qsrc = qrT if roped else qT
                    kkey = ('krT2', g, tgk) if roped else ('kT2', g, tgk)
                    qn = 'qrT' if roped else 'qT'
                    par = S.rot('swsc', 2)
                    for half in range(2):
                        bk = 2 * par + half
                        r0 = half * 64
                        first = True
                        if kind in ('p', 'n'):
                            bm = bmp if kind == 'p' else bmn
                            S.op('pe', lambda e, bk=bk, bm=bm: e.matmul(PS[bk][:, 0:256].rearrange("p (h q) -> p h q", h=2), lhsT=ident,
                                                                      rhs=bm.unsqueeze(1).to_broadcast([128, 2, 128]), start=True, stop=False),
                                 reads=['ident', 'bmp', 'bmn'], writes=[('ps', bk)])
                            first = False
                        S.op('pe', lambda e, bk=bk, r0=r0, jt=jt, ksrc=ksrc, qsrc=qsrc, first=first: e.matmul(
                            PS[bk][:, 0:256].rearrange("p (c q) -> p c q", c=2), lhsT=ksrc[r0:r0 + 64, g, jt * 128:(jt + 1) * 128],
                            rhs=qsrc[r0:r0 + 64, :, i * 128:(i + 1) * 128], start=first, stop=True),
                            reads=[kkey, (qn, 0, tgq), (qn, 1, tgq)], writes=[('ps', bk)])
                        S.op('act', lambda e, bk=bk, bi_=bi_, half=half: e.activation(
                            out=pT[:, bi_, :, :].rearrange("p (c f) q -> p c f q", c=2)[:, :, half, :],
                            in_=PS[bk][:, 0:256].rearrange("p (c q) -> p c q", c=2), func=AF.Exp),
                            reads=[('ps', bk)], writes=[('pT', bi_, half)])
                ob = 5
                for hh in range(4):
                    od = PS[ob][:, hh * 65:(hh + 1) * 65]
                    for bi_, (jt, kind) in enumerate(blocks):
                        S.op('pe', lambda e, od=od, bi_=bi_, jt=jt, hh=hh: e.matmul(od, lhsT=pT[:, bi_, hh, :], rhs=v65[:, jt, g, :],
                                                                                 start=(bi_ == 0), stop=(bi_ == nb - 1)),
                             reads=[('pT', bi_, 0), ('pT', bi_, 1), ('v65', jt), 'v1'], writes=[('ps', ob)])
                ov = PS[ob][:, 0:260].rearrange("p (h e) -> p h e", e=65)
                S.op('dve', lambda e, ov=ov: e.tensor_tensor(out=den, in0=ov[:, :, 64:65], in1=es[:, g * 4:(g + 1) * 4].unsqueeze(2), op=ALU.add),
                     reads=[('ps', ob), 'es'], writes=['den_t'])
                S.op('dve', lambda e: e.reciprocal(out=den, in_=den), reads=['den_t'], writes=['den'])
                S.op('dve', lambda e, ov=ov: e.tensor_tensor(
                    out=o_tok[:, i, g * 256:(g + 1) * 256].rearrange("p (h d) -> p h d", h=4), in0=ov[:, :, 0:64],
                    in1=den.to_broadcast([128, 4, 64]), op=ALU.mult), reads=[('ps', ob), 'den'], writes=[('otok', i)])
        S.barrier()

    def out_proj(wsrc, OT, l, first_layer, tiles):
        wout = V(R3T, [128, 8, 1024], BF16)
        tmp = [V(R3T + 16 * KB, [128, 1024], F32), V(R3T + 20 * KB, [128, 1024], F32)]
        wk = []
        sv = wsrc.rearrange("(c p) n -> p c n", p=128)
        for hq in range(4):
            S.dma('pool', f'wout{hq}', wout[:, hq * 2:(hq + 1) * 2, :], sv[:, hq * 2:(hq + 1) * 2, :], writes=[('wout', hq)])
            wk.append(('wout', hq))
        for i in tiles:
            src = tok_rows(i) if first_layer else xres_d[i * 128:(i + 1) * 128, :]
            S.dma('sp', f'xld{i}', x_res[:, i, :], src, writes=[('xres', i)])
        for i in tiles:
            w = 0 if i < 16 else 1
            b = S.rot('tmp', 2)
            for hf in range(2):
                pb = 4 + hf
                for kc in range(8):
                    S.op('pe', lambda e, kc=kc, hf=hf, pb=pb: e.matmul(PS[pb][:, :], lhsT=OT[:, kc, i * 128:(i + 1) * 128],
                                                                     rhs=wout[:, kc, hf * 512:(hf + 1) * 512], start=(kc == 0), stop=(kc == 7)),
                         reads=wk + [('OT', i)], writes=[('ps', pb)])
                S.op('dve', lambda e, hf=hf, pb=pb, w=w, b=b: e.tensor_tensor(out=tmp[b][:, hf * 512:(hf + 1) * 512], in0=PS[pb][:, :],
                                                                             in1=gate_bc[:, w, hf * 512:(hf + 1) * 512], op=ALU.mult),
                     reads=[('ps', pb), ('gate', w, hf)], writes=[('tmp', b, hf)])
            S.op('pool', lambda e, b=b: e.tensor_tensor(out=x_res[:, i, :], in0=x_res[:, i, :], in1=tmp[b], op=ALU.add),
                 reads=[('tmp', b, 0), ('tmp', b, 1), ('xres', i)], writes=[('xres', i)])

    def mlp(l, hT, tiles):
        uT = V(R1, [128, 4, NTOK], BF16)
        tmp = [V(R1 + 18 * KB, [128, 1024], F32), V(R1 + 22 * KB, [128, 1024], F32)]
        rl = [V(R1 + 26 * KB, [128, 512], F32), V(R1 + 28 * KB, [128, 512], F32)]
        W1 = [V(R3T, [128, 8, 512], BF16), V(R3T + 8 * KB, [128, 8, 512], BF16)]
        W2 = [V(R3T + 16 * KB, [128, 4, 1024], BF16), V(R3T + 24 * KB, [128, 4, 1024], BF16)]
        assert R3T + 32 * KB <= R3END
        thi = (max(tiles) + 1) * 128
        for blk in range(8):
            bi = S.rot('mlpw', 2)
            w1v = w1_d[l, :, blk * 512:(blk + 1) * 512].rearrange("(c p) n -> p c n", p=128)
            w2v = w2_d[l, blk * 512:(blk + 1) * 512, :].rearrange("(c p) n -> p c n", p=128)
            k1, k2 = [], []
            for hq in range(2):
                S.dma('pool', f'w1_{bi}_{hq}', W1[bi][:, hq * 4:(hq + 1) * 4, :], w1v[:, hq * 4:(hq + 1) * 4, :], writes=[('W1', bi, hq)])
                k1.append(('W1', bi, hq))
            for hq in range(2):
                S.dma('pool', f'w2_{bi}_{hq}', W2[bi][:, hq * 2:(hq + 1) * 2, :], w2v[:, hq * 2:(hq + 1) * 2, :], writes=[('W2', bi, hq)])
                k2.append(('W2', bi, hq))
            for (t0, t1) in GROUPS:
                if t0 >= thi:
                    continue
                n = t1 - t0
                for fc in range(4):
                    pb = S.rot('psU', 4)
                    for kc in range(8):
                        S.op('pe', lambda e, kc=kc, fc=fc, pb=pb: e.matmul(PS[pb][:, 0:n], lhsT=W1[bi][:, kc, fc * 128:(fc + 1) * 128],
                                                                         rhs=hT[:, kc, t0:t1], start=(kc == 0), stop=(kc == 7)),
                             reads=k1 + [('hT', kc, i) for i in range(t0 // 128, t1 // 128)], writes=[('ps', pb)])
                    rb = S.rot('rl', 2)
                    S.op('act', lambda e, pb=pb, rb=rb: e.activation(out=rl[rb][:, 0:n], in_=PS[pb][:, 0:n], func=AF.Relu),
                         reads=[('ps', pb)], writes=[('rl', rb)])
                    S.op('pool', lambda e, rb=rb, fc=fc: e.tensor_tensor(out=uT[:, fc, t0:t1], in0=rl[rb][:, 0:n], in1=rl[rb][:, 0:n], op=ALU.mult),
                         reads=[('rl', rb)], writes=[('uT', fc, t0)])
            for i in tiles:
                w = 2 + (0 if i < 16 else 1)
                tg = [t0 for (t0, t1_) in GROUPS if t0 <= i * 128 < t1_][0]
                b = S.rot('tmp', 2)
                for hf in range(2):
                    pb = 4 + hf
                    for fc in range(4):
                        S.op('pe', lambda e, fc=fc, hf=hf, pb=pb: e.matmul(PS[pb][:, :], lhsT=uT[:, fc, i * 128:(i + 1) * 128],
                                                                         rhs=W2[bi][:, fc, hf * 512:(hf + 1) * 512], start=(fc == 0), stop=(fc == 3)),
                             reads=k2 + [('uT', fc, tg)], writes=[('ps', pb)])
                    S.op('dve', lambda e, hf=hf, pb=pb, w=w, b=b: e.tensor_tensor(out=tmp[b][:, hf * 512:(hf + 1) * 512], in0=PS[pb][:, :],
                                                                                 in1=gate_bc[:, w, hf * 512:(hf + 1) * 512], op=ALU.mult),
                         reads=[('ps', pb), ('gate', w, hf)], writes=[('tmp', b, hf)])
                S.op('pool', lambda e, b=b: e.tensor_tensor(out=x_res[:, i, :], in0=x_res[:, i, :], in1=tmp[b], op=ALU.add),
                     reads=[('tmp', b, 0), ('tmp', b, 1), ('xres', i)], writes=[('xres', i)])

    hT_A = V(R1, [128, 8, NTOK], BF16)
    hT_B = V(R2, [128, 8, NTOK], BF16)
    o_tok = V(R2, [128, NT, D], BF16)
    OT = V(R1, [128, 8, NTOK], BF16)

    for i in range(NT):
        S.dma('sp', f'xld{i}', x_res[:, i, :], tok_rows(i), writes=[('xres', i)])
    import os
    STOP_L = int(os.environ.get('STOP_L', '0'))
    cur_l = [0]

    def chk(name):
        if stop == name and cur_l[0] == STOP_L:
            S.barrier()
            raise _Stop()

    try:
      for l in range(n_layers):
        ctx_out = l < n_layers - 1
        cur_l[0] = l
        adaln(l)
        S.barrier()
        chk('adaln')
        norm_to_hT(hT_A, GpM, 'GpM', 0, list(range(NT)), R2)
        if l > 0:
            pass
        if l > 0:
            for i in range(NT):
                S.dma('sp', f'xst{i}', xres_d[i * 128:(i + 1) * 128, :], x_res[:, i, :], reads=[('xres', i)])
        S.barrier()
        chk('normA')
        if l % 2 == 0:
            even_mixer(l // 2, hT_A, o_tok)
            tiles = list(range(NT))
        else:
            odd_mixer(l // 2, hT_A, o_tok, ctx_out)
            tiles = list(range(NT if ctx_out else 16))
        chk('mixer')
        transpose_otok(o_tok, OT, tiles)
        S.barrier()
        chk('tr')
        out_proj(about_d[l // 2] if l % 2 == 0 else swout_d[l // 2], OT, l, l == 0, tiles)
        S.barrier()
        norm_to_hT(hT_B, GpL, 'GpL', 2, tiles, R1)
        S.barrier()
        chk('norm2')
        mlp(l, hT_B, tiles)
        S.barrier()
    except _Stop:
        pass
    nf = V(R3T, [128, D], F32)
    junk = V(R1, [128, D], F32)
    yo = [V(R1 + 4 * KB, [128, D], F32), V(R1 + 8 * KB, [128, D], F32)]
    S.dma('sp', 'nf', nf, nfin_d.broadcast_to([128, D]), writes=['nf'])
    for i in range(16):
        S.op('act', lambda e, i=i: e.activation(out=junk, in_=x_res[:, i, :], func=AF.Square, accum_out=ssA[:, i:i + 1]),
             reads=[('xres', i)], writes=['junk', ('ssA', i)])
    S.op('act', lambda e: e.activation(out=rsA[:, 0:16], in_=ssA[:, 0:16], func=AF.Sqrt, scale=1.0 / D, bias=EPS),
         reads=[('ssA', i) for i in range(16)], writes=['rsA_t'])
    S.op('dve', lambda e: e.reciprocal(out=rsA[:, 0:16], in_=rsA[:, 0:16]), reads=['rsA_t'], writes=['rsA'])
    for i in range(16):
        b = S.rot('yo', 2)
        S.op('dve', lambda e, i=i, b=b: e.scalar_tensor_tensor(out=yo[b], in0=x_res[:, i, :], scalar=rsA[:, i:i + 1], in1=nf,
                                                             op0=ALU.mult, op1=ALU.mult), reads=[('xres', i), 'rsA', 'nf'], writes=[('yo', b)])
        S.dma('sp', f'ost{b}', out_d[i * 128:(i + 1) * 128, :], yo[b], reads=[('yo', b)])
    S.barrier()
    for c in reversed(ps_cm):
        c.__exit__(None, None, None)
    arena_cm.__exit__(None, None, None)
    S.close()
    return nc


def _prep_shared(inp):
    f = np.float32
    sh = {}
    sh["ada_w"] = np.ascontiguousarray(inp["ada_w"], dtype=f)
    ada_b = np.asarray(inp["ada_b"], dtype=f)
    sh["ada_bT"] = np.ascontiguousarray(ada_b.reshape(DEPTH, 48, 128).transpose(0, 2, 1))
    sh["ada_bf"] = np.ascontiguousarray(ada_b.reshape(DEPTH, 1, 6 * D))
    sh["nmix"] = np.ascontiguousarray(np.asarray(inp["norm_mix"], dtype=f).reshape(DEPTH, 8, 128).transpose(0, 2, 1))
    sh["nmlp"] = np.ascontiguousarray(np.asarray(inp["norm_mlp"], dtype=f).reshape(DEPTH, 8, 128).transpose(0, 2, 1))
    sh["nfin"] = np.ascontiguousarray(np.asarray(inp["norm_final"], dtype=f).reshape(1, D))
    sh["w1"] = np.ascontiguousarray(inp["mlp_w1"], dtype=f)
    sh["w2"] = np.ascontiguousarray(inp["mlp_w2"], dtype=f)
    sh["abin"] = np.ascontiguousarray(inp["ab_w_in"], dtype=f)
    sh["about"] = np.ascontiguousarray(inp["ab_w_out"], dtype=f)
    rpb = np.asarray(inp["na_rpb"], dtype=f)
    nab = np.empty((2, N_PAT, 128, 8, 128), dtype=f)
    for pi, idx in enumerate(NA_PATS):
        valid = idx >= 0
        ic = np.where(valid, idx, 0)
        for jl in range(2):
            flat = rpb[jl].reshape(8, 15 * 31)
            g = flat[:, ic]
            g = np.where(valid[None], g, f(-1e30))
            nab[jl, pi] = g.transpose(1, 0, 2)
    sh["nabias"] = np.ascontiguousarray(nab.reshape(2, N_PAT, 128, 1024))
    wa2 = np.asarray(inp["gla_wa2"], dtype=f)
    ba = np.asarray(inp["gla_ba"], dtype=f)
    wa2b = np.zeros((2, 33, 512), dtype=f)
    for jl in range(2):
        for d in range(2):
            wa2b[jl, d * 16:(d + 1) * 16, d * 256:(d + 1) * 256] = wa2[jl, d]
            wa2b[jl, 32, d * 256:(d + 1) * 256] = ba[jl, d]
    sh["wa2b"] = wa2b
    sh["gn"] = np.ascontiguousarray(np.asarray(inp["gla_gnorm"], dtype=f).reshape(2, 1, 512))
    sw = np.asarray(inp["swa_w_in"], dtype=f)
    qcols = np.arange(1024)
    swap = qcols ^ 1
    kdup = np.concatenate([np.concatenate([1024 + g * 64 + np.arange(64)] * 2) for g in range(4)])
    kdup_sw = np.concatenate([np.concatenate([1024 + g * 64 + (np.arange(64) ^ 1)] * 2) for g in range(4)])
    vcols = 1280 + np.arange(256)
    cols = np.concatenate([qcols, swap, kdup, kdup_sw, vcols])
    sh["swin"] = np.ascontiguousarray(sw[:, :, cols])
    sh["swout"] = np.ascontiguousarray(inp["swa_w_out"], dtype=f)
    sh["sink"] = np.ascontiguousarray(np.asarray(inp["swa_sink"], dtype=f).reshape(2, 1, 16))
    Ct, St = _rope_tables()
    sh["ropeC"], sh["ropeS"] = Ct, St
    return sh


_NC_CACHE = {}


def kernel(**inp):
    n_layers = int(inp.pop("_n_layers", DEPTH))
    f = np.float32
    sh = _prep_shared(inp)
    x = np.asarray(inp["x"], dtype=f); c = np.asarray(inp["c"], dtype=f)
    ctx = np.asarray(inp["ctx"], dtype=f); c_ctx = np.asarray(inp["c_ctx"], dtype=f)
    in_maps = []
    for b in range(8):
        m = dict(sh)
        m["x"] = np.ascontiguousarray(x[b]); m["ctx"] = np.ascontiguousarray(ctx[b])
        m["scin"] = np.ascontiguousarray(np.concatenate([c[b].reshape(8, 128).T, c_ctx.reshape(8, 128).T], axis=1))
        in_maps.append(m)
    if n_layers not in _NC_CACHE:
        _NC_CACHE[n_layers] = build(n_layers)
    nc = _NC_CACHE[n_layers]
    res = run_bass_kernel_spmd(nc, in_maps, core_ids=list(range(8)))
    return np.stack([r["out"] for r in res.results], axis=0).astype(f)
```

```python
import numpy as np
import concourse.bass as bass
import concourse.mybir as mybir
from concourse.bass_utils import run_bass_kernel_spmd

F32 = mybir.dt.float32
BF16 = mybir.dt.bfloat16
AF = mybir.ActivationFunctionType
ALU = mybir.AluOpType
AX = mybir.AxisListType

D = 1024
SEQ = 2048
CTX = 256
NT = 18
NTOK = NT * 128
DEPTH = 4
EPS = 1e-6
GROUPS = [(0, 512), (512, 1024), (1024, 1536), (1536, 2048), (2048, 2304)]


class Sched:
    def __init__(self, nc):
        self.nc = nc
        self.eng = {'pe': nc.tensor, 'act': nc.scalar, 'dve': nc.vector, 'pool': nc.gpsimd, 'sp': nc.sync}
        self.sem, self.cnt, self.ctx, self.semobj = {}, {}, [], {}
        for e in self.eng:
            cm = nc.semaphore('s_' + e)
            self.sem[e] = cm.__enter__(); self.ctx.append(cm)
            self.cnt[e] = 0
            self.semobj['s_' + e] = self.sem[e]
        self.dsem, self.dcnt = {}, {}
        self.seen = {e: {} for e in self.eng}
        self.lastw, self.readers = {}, {}
        self.rr = {}

    def rot(self, name, n):
        v = self.rr.get(name, 0)
        self.rr[name] = v + 1
        return v % n

    def dma_sem(self, name):
        if name not in self.dsem:
            cm = self.nc.semaphore('d_' + name)
            self.dsem[name] = cm.__enter__(); self.ctx.append(cm)
            self.dcnt[name] = 0
            self.semobj['d_' + name] = self.dsem[name]
        return self.dsem[name]

    def _wait(self, e, tok):
        if tok is None:
            return
        sname, val = tok
        if e == 'pe' and sname == 's_pe':
            return
        if self.seen[e].get(sname, 0) >= val:
            return
        self.eng[e].wait_ge(self.semobj[sname], val)
        self.seen[e][sname] = val

    def _deps(self, e, reads, writes):
        for k in reads:
            self._wait(e, self.lastw.get(k))
        for k in writes:
            self._wait(e, self.lastw.get(k))
            for t in self.readers.get(k, ()):
                self._wait(e, t)

    def _commit(self, tok, reads, writes):
        for k in reads:
            self.readers.setdefault(k, []).append(tok)
        for k in writes:
            self.lastw[k] = tok
            self.readers[k] = []

    def op(self, e, fn, reads=(), writes=(), inc=True):
        self._deps(e, reads, writes)
        inst = fn(self.eng[e])
        if inc:
            self.cnt[e] += 1
            inst.then_inc(self.sem[e], 1)
            tok = ('s_' + e, self.cnt[e])
        else:
            tok = ('s_' + e, self.cnt[e] + 1)
        self._commit(tok, reads, writes)
        return tok

    def dma(self, e, slot, out, in_, reads=(), writes=(), **kw):
        sem = self.dma_sem(slot)
        self._deps(e, reads, writes)
        inst = self.eng[e].dma_start(out=out, in_=in_, **kw)
        self.dcnt[slot] += 16
        inst.then_inc(sem, 16)
        tok = ('d_' + slot, self.dcnt[slot])
        self._commit(tok, reads, writes)
        return tok

    def wait_all(self, e):
        for f in self.eng:
            if self.cnt[f] > 0:
                self._wait(e, ('s_' + f, self.cnt[f]))
        for s in self.dsem:
            if self.dcnt[s] > 0:
                self._wait(e, ('d_' + s, self.dcnt[s]))

    def barrier(self):
        for e in self.eng:
            self.wait_all(e)
        self.lastw, self.readers = {}, {}

    def close(self):
        for cm in reversed(self.ctx):
            cm.__exit__(None, None, None)


def _na_patterns():
    pats, keymap, tilemap = [], {}, {}
    qi = np.arange(128)
    for i in range(16):
        r = 2 * i + qi // 64
        c = qi % 64
        r0 = np.clip(r - 4, 0, 24)
        w0 = np.clip(c - 8, 0, 48)
        lst = []
        for jt in range(16):
            kr = 2 * jt + qi // 64
            kc = qi % 64
            valid = ((kr[:, None] >= r0[None, :]) & (kr[:, None] < r0[None, :] + 8)
                     & (kc[:, None] >= w0[None, :]) & (kc[:, None] < w0[None, :] + 16))
            if not valid.any():
                continue
            ro = kr[:, None] - r[None, :] + 7
            co = kc[:, None] - c[None, :] + 15
            idx = np.where(valid, ro * 31 + co, -1)
            key = idx.tobytes()
            if key not in keymap:
                keymap[key] = len(pats)
                pats.append(idx)
            lst.append((jt, keymap[key]))
        tilemap[i] = lst
    return pats, tilemap


NA_PATS, NA_TILEMAP = _na_patterns()
N_PAT = len(NA_PATS)


def _rope_tables():
    t = np.arange(SEQ)
    row = (t // 64).astype(np.float32)
    col = (t % 64).astype(np.float32)
    inv = (np.float32(10000.0) ** (-np.arange(16, dtype=np.float32) / np.float32(16))).astype(np.float32)
    ang = np.concatenate([row[:, None] * inv, col[:, None] * inv], axis=-1).astype(np.float32)
    cos, sin = np.cos(ang).astype(np.float32), np.sin(ang).astype(np.float32)
    p = np.arange(128)
    d = p % 64
    m = d // 2
    Ct = cos[:, m].T.copy()
    St = (sin[:, m] * np.where(d % 2 == 0, -1.0, 1.0)[None, :]).T.astype(np.float32).copy()
    return Ct, St


class _Stop(Exception):
    pass


def build(n_layers=DEPTH, stop=None):
    nc = bass.Bass("TRN2", target_bir_lowering=False)

    def din(name, shape):
        return nc.dram_tensor(name, list(shape), F32, kind="ExternalInput").ap()

    x_d = din("x", [SEQ, D]); ctx_d = din("ctx", [CTX, D])
    scin_d = din("scin", [128, 16])
    adaw_d = din("ada_w", [DEPTH, D, 6 * D])
    adabT_d = din("ada_bT", [DEPTH, 128, 48]); adabf_d = din("ada_bf", [DEPTH, 1, 6 * D])
    nmix_d = din("nmix", [DEPTH, 128, 8]); nmlp_d = din("nmlp", [DEPTH, 128, 8]); nfin_d = din("nfin", [1, D])
    w1_d = din("w1", [DEPTH, D, 4 * D]); w2_d = din("w2", [DEPTH, 4 * D, D])
    abin_d = din("abin", [2, D, 3104]); about_d = din("about", [2, D, D])
    nab_d = din("nabias", [2, N_PAT, 128, 1024])
    wa2b_d = din("wa2b", [2, 33, 512]); gn_d = din("gn", [2, 1, 512])
    swin_d = din("swin", [2, D, 3328]); swout_d = din("swout", [2, D, D]); sink_d = din("sink", [2, 1, 16])
    ropeC_d = din("ropeC", [128, SEQ]); ropeS_d = din("ropeS", [128, SEQ])
    out_d = nc.dram_tensor("out", [SEQ, D], F32, kind="ExternalOutput").ap()
    xres_d = nc.dram_tensor("xres", [NTOK, D], F32, kind="Internal").ap()

    S = Sched(nc)
    ARENA_W = 52480
    arena_cm = nc.sbuf_tensor("arena", [128, ARENA_W], F32)
    arena = arena_cm.__enter__()
    ps_cm = [nc.psum_tensor(f"ps{i}", [128, 512], F32) for i in range(6)] + \
            [nc.psum_tensor(f"ps{i}", [128, 1024], BF16) for i in (6, 7)]
    PS = [c.__enter__() for c in ps_cm]

    def V(off, shape, dt, parts=128):
        n = int(np.prod(shape[1:]))
        assert off % 4 == 0
        if dt == F32:
            assert off // 4 + n <= ARENA_W, (off, shape)
            a = arena[0:parts, off // 4: off // 4 + n]
        else:
            assert n % 2 == 0 and off // 4 + n // 2 <= ARENA_W, (off, shape)
            a = arena[0:parts, off // 4: off // 4 + n // 2].bitcast(BF16)
        if len(shape) == 3:
            a = a.rearrange("p (a b) -> p a b", a=shape[1])
        elif len(shape) == 4:
            a = a.rearrange("p (a b c) -> p a b c", a=shape[1], b=shape[2])
        return a

    KB = 1024
    R0, R1, R2, R3 = 0, 72 * KB, 108 * KB, 144 * KB
    o = R3
    ident = V(o, [128, 128], BF16); o += 256
    Uf = V(o, [128, 128], F32); o += 512
    Ub = V(o, [128, 128], F32); o += 512
    Rf = V(o, [128, 128], F32); o += 512
    Rb = V(o, [128, 128], F32); o += 512
    mkf = V(o, [128, 128], F32); o += 512
    mkb = V(o, [128, 128], F32); o += 512
    bmp = V(o, [128, 128], BF16); o += 256
    bmn = V(o, [128, 128], BF16); o += 256
    scT = V(o, [128, 8, 2], BF16); o += 32
    scin = V(o, [128, 16], F32); o += 64
    ones_row = V(o, [128, 128], BF16); o += 256
    nmix = V(o, [128, DEPTH, 8], F32); o += 4 * DEPTH * 8
    nmlp = V(o, [128, DEPTH, 8], F32); o += 4 * DEPTH * 8
    adabT = V(o, [128, DEPTH, 48], F32); o += 4 * DEPTH * 48
    modT = V(o, [128, 4, 8, 2], F32); o += 256
    GpM = V(o, [128, 8, 2], F32); o += 64
    GpL = V(o, [128, 8, 2], F32); o += 64
    ssA = V(o, [128, 32], F32); o += 128
    rsA = V(o, [128, 32], F32); o += 128
    gate_bc = V(o, [128, 4, 1024], F32); o += 16 * KB
    R3T = o
    R3END = ARENA_W * 4

    def memset(e, ap, val, key):
        S.op(e, lambda en: en.memset(ap, val), writes=[key])

    def asel(ap, pattern, cm, base, op, key, fill=0.0):
        S.op('pool', lambda en: en.affine_select(out=ap, in_=ap, pattern=pattern, compare_op=op, fill=fill,
                                                 base=base, channel_multiplier=cm), reads=[key], writes=[key])

    memset('pool', ident, 1.0, 'ident')
    asel(ident, [[-1, 128]], 1, 0, ALU.is_equal, 'ident')
    memset('pool', ones_row, 1.0, 'ones_row')
    for (ap, key, pat, cm, base) in [(Uf, 'Uf', [[1, 128]], -1, 0), (Ub, 'Ub', [[-1, 128]], 1, 0),
                                     (Rf, 'Rf', [[-1, 128]], 1, -1), (Rb, 'Rb', [[1, 128]], -1, -1)]:
        memset('pool', ap, -1.0 / 16.0, key)
        asel(ap, pat, cm, base, ALU.is_ge, key)
    memset('pool', mkf, 1.0, 'mkf'); asel(mkf, [[1, 128]], -1, 0, ALU.is_ge, 'mkf')
    memset('pool', mkb, 1.0, 'mkb'); asel(mkb, [[-1, 128]], 1, 0, ALU.is_ge, 'mkb')
    memset('pool', bmp, 0.0, 'bmp'); asel(bmp, [[-1, 128]], 1, 0, ALU.is_ge, 'bmp', fill=-1e30)
    memset('pool', bmn, 0.0, 'bmn'); asel(bmn, [[1, 128]], -1, 0, ALU.is_ge, 'bmn', fill=-1e30)
    S.dma('sp', 'c0', scin, scin_d, writes=['scin'])
    S.dma('sp', 'c1', nmix, nmix_d.rearrange("l p c -> p l c"), writes=['nmix'])
    S.dma('sp', 'c2', nmlp, nmlp_d.rearrange("l p c -> p l c"), writes=['nmlp'])
    S.dma('sp', 'c3', adabT, adabT_d.rearrange("l p c -> p l c"), writes=['adabT'])
    S.op('act', lambda e: e.activation(out=scT.rearrange("p c w -> p w c"), in_=scin.rearrange("p (w c) -> p w c", w=2),
                                       func=AF.Silu), reads=['scin'], writes=['scT'])

    x_res = V(R0, [128, NT, D], F32)

    def tok_rows(i):
        return (x_d[i * 128:(i + 1) * 128, :] if i < 16 else ctx_d[(i - 16) * 128:(i - 15) * 128, :])

    def adaln(l):
        ob = R2
        adab = [V(ob, [128, 8, 1024], BF16), V(ob + 16 * KB, [128, 8, 1024], BF16)]
        ob += 32 * KB
        sc_rep = V(ob, [128, 8, 2, 128], BF16); ob += 4 * KB
        abf = V(R3T, [128, 2048], BF16)
        for kc in range(8):
            for w in range(2):
                S.op('dve', lambda e, kc=kc, w=w: e.tensor_copy(out=sc_rep[:, kc, w, :],
                                                                in_=scT[:, kc, w:w + 1].to_broadcast([128, 128])),
                     reads=['scT'], writes=[('sc_rep', kc, w)])
        S.dma('pool', 'abf0', abf[0:1, 0:1024], adabf_d[l, :, 2 * D:3 * D], writes=['abf0'])
        S.dma('pool', 'abf1', abf[0:1, 1024:2048], adabf_d[l, :, 5 * D:6 * D], writes=['abf1'])
        kind_of = {0: 0, 1: 1, 3: 2, 4: 3}
        for blk in range(6):
            bi = S.rot('adab', 2)
            buf = adab[bi]
            src = adaw_d[l, :, blk * D:(blk + 1) * D].rearrange("(c p) n -> p c n", p=128)
            for hq in range(2):
                S.dma('pool', f'adab{bi}_{hq}', buf[:, hq * 4:(hq + 1) * 4, :], src[:, hq * 4:(hq + 1) * 4, :],
                      writes=[('adab', bi, hq)])
            rk = [('adab', bi, 0), ('adab', bi, 1)]
            if blk in kind_of:
                pb = S.rot('psA', 2)
                ps = PS[pb]
                for j in range(8):
                    for kc in range(8):
                        S.op('pe', lambda e, j=j, kc=kc: e.matmul(ps[:, j * 2:(j + 1) * 2], lhsT=buf[:, kc, j * 128:(j + 1) * 128],
                                                                  rhs=scT[:, kc, :], start=(kc == 0), stop=(kc == 7)),
                             reads=rk + ['scT'], writes=[('ps', pb)])
                S.op('dve', lambda e, blk=blk: e.tensor_tensor(
                    out=modT[:, kind_of[blk], :, :], in0=ps[:, 0:16].rearrange("p (c w) -> p c w", w=2),
                    in1=adabT[:, l, blk * 8:(blk + 1) * 8].unsqueeze(2).to_broadcast([128, 8, 2]), op=ALU.add),
                    reads=[('ps', pb), 'adabT'], writes=[('modT', kind_of[blk])])
            else:
                gi = 0 if blk == 2 else 1
                for w in range(2):
                    for hf in range(2):
                        pb = S.rot('psA', 2)
                        ps = PS[pb]
                        for kc in range(8):
                            S.op('pe', lambda e, kc=kc, w=w, hf=hf: e.matmul(ps[:, :], lhsT=sc_rep[:, kc, w, :],
                                                                           rhs=buf[:, kc, hf * 512:(hf + 1) * 512],
                                                                           start=(kc == 0), stop=False),
                                 reads=rk + [('sc_rep', kc, w)], writes=[('ps', pb)])
                        S.op('pe', lambda e, hf=hf, gi=gi: e.matmul(ps[:, :], lhsT=ones_row[0:1, :],
                                                                  rhs=abf[0:1, gi * 1024 + hf * 512: gi * 1024 + (hf + 1) * 512],
                                                                  start=False, stop=True),
                             reads=['ones_row', 'abf0', 'abf1'], writes=[('ps', pb)])
                        S.op('act', lambda e, w=w, hf=hf, gi=gi: e.activation(out=gate_bc[:, gi * 2 + w, hf * 512:(hf + 1) * 512],
                                                                           in_=ps[:, :], func=AF.Copy),
                             reads=[('ps', pb)], writes=[('gate', gi * 2 + w, hf)])
        for (Gp, nrm, kind, key) in [(GpM, nmix, 1, 'GpM'), (GpL, nmlp, 3, 'GpL')]:
            S.op('dve', lambda e, Gp=Gp, nrm=nrm, kind=kind: e.scalar_tensor_tensor(
                out=Gp[:, :, :], in0=modT[:, kind, :, :], scalar=1.0,
                in1=nrm[:, l, :].unsqueeze(2).to_broadcast([128, 8, 2]), op0=ALU.add, op1=ALU.mult),
                reads=[('modT', kind), 'nmix', 'nmlp'], writes=[key])

    def norm_to_hT(hT, Gp, gkey, shift_kind, tiles, tmp_off):
        junk = V(tmp_off, [128, 1024], F32)
        xn = [V(tmp_off + 4 * KB, [128, 1024], BF16), V(tmp_off + 6 * KB, [128, 1024], BF16)]
        for i in tiles:
            S.op('act', lambda e, i=i: e.activation(out=junk, in_=x_res[:, i, :], func=AF.Square, accum_out=ssA[:, i:i + 1]),
                 reads=[('xres', i)], writes=['junk', ('ssA', i)])
        n = len(tiles)
        t0 = tiles[0]
        S.op('act', lambda e: e.activation(out=rsA[:, t0:t0 + n], in_=ssA[:, t0:t0 + n], func=AF.Sqrt, scale=1.0 / D, bias=EPS),
             reads=[('ssA', i) for i in tiles], writes=['rsA_t'])
        S.op('dve', lambda e: e.reciprocal(out=rsA[:, t0:t0 + n], in_=rsA[:, t0:t0 + n]), reads=['rsA_t'], writes=['rsA'])
        for i in tiles:
            b = S.rot('xn', 2)
            w = 0 if i < 16 else 1
            S.op('dve', lambda e, i=i, b=b: e.tensor_scalar(out=xn[b], in0=x_res[:, i, :], scalar1=rsA[:, i:i + 1], scalar2=None,
                                                          op0=ALU.mult), reads=[('xres', i), 'rsA'], writes=[('xn', b)])
            pb = 6 + S.rot('psT', 2)
            pst = PS[pb]
            for c in range(8):
                S.op('pe', lambda e, c=c, b=b: e.transpose(out=pst[:, c * 128:(c + 1) * 128], in_=xn[b][:, c * 128:(c + 1) * 128],
                                                         identity=ident), reads=[('xn', b), 'ident'], writes=[('ps', pb)])
            use_act = (S.rot('nev', 2) == 0)
            for c in range(8):
                if use_act:
                    S.op('act', lambda e, c=c, i=i, w=w: e.activation(
                        out=hT[:, c, i * 128:(i + 1) * 128], in_=pst[:, c * 128:(c + 1) * 128], func=AF.Identity,
                        scale=Gp[:, c, w:w + 1], bias=modT[:, shift_kind, c, w:w + 1]),
                        reads=[('ps', pb), gkey, ('modT', shift_kind)], writes=[('hT', c, i)])
                else:
                    S.op('dve', lambda e, c=c, i=i, w=w: e.tensor_scalar(
                        out=hT[:, c, i * 128:(i + 1) * 128], in0=pst[:, c * 128:(c + 1) * 128],
                        scalar1=Gp[:, c, w:w + 1], scalar2=modT[:, shift_kind, c, w:w + 1], op0=ALU.mult, op1=ALU.add),
                        reads=[('ps', pb), gkey, ('modT', shift_kind)], writes=[('hT', c, i)])

    def load_w(buf, src, ncols, key):
        sv = src.rearrange("(c p) n -> p c n", p=128)
        for hq in range(2):
            S.dma('pool', key[0] + str(key[1]) + '_' + str(hq), buf[:, hq * 4:(hq + 1) * 4, 0:ncols], sv[:, hq * 4:(hq + 1) * 4, :],
                  writes=[(key, hq)])
        return [(key, 0), (key, 1)]

    def proj_fm(hT, wbuf, wkeys, c0, M, evac, tiles_hi=NTOK):
        for (t0, t1) in GROUPS:
            if t0 >= tiles_hi:
                continue
            pb = S.rot('psP', 4)
            ps = PS[pb]
            for kc in range(8):
                S.op('pe', lambda e, kc=kc: e.matmul(ps[0:M, 0:t1 - t0], lhsT=wbuf[:, kc, c0:c0 + M], rhs=hT[:, kc, t0:t1],
                                                    start=(kc == 0), stop=(kc == 7)),
                     reads=wkeys + [('hT', kc, i) for i in range(t0 // 128, t1 // 128)], writes=[('ps', pb)])
            evac(ps, pb, t0, t1)

    def proj_tm(hT, wbuf, wkeys, c0, n, evac, tiles):
        for i in tiles:
            pb = S.rot('psP', 4)
            ps = PS[pb]
            for kc in range(8):
                S.op('pe', lambda e, kc=kc: e.matmul(ps[:, 0:n], lhsT=hT[:, kc, i * 128:(i + 1) * 128], rhs=wbuf[:, kc, c0:c0 + n],
                                                    start=(kc == 0), stop=(kc == 7)),
                     reads=wkeys + [('hT', kc, i)], writes=[('ps', pb)])
            evac(ps, pb, i)

    def ev_copy(eng, out_ap, in_ap, pb, wkey, scale=None):
        if eng == 'act':
            if scale is None:
                S.op('act', lambda e: e.activation(out=out_ap, in_=in_ap, func=AF.Copy), reads=[('ps', pb)], writes=[wkey])
            else:
                S.op('act', lambda e: e.activation(out=out_ap, in_=in_ap, func=AF.Copy, scale=scale), reads=[('ps', pb)], writes=[wkey])
        else:
            if scale is None:
                S.op(eng, lambda e: e.tensor_copy(out=out_ap, in_=in_ap), reads=[('ps', pb)], writes=[wkey])
            else:
                S.op(eng, lambda e: e.tensor_scalar(out=out_ap, in0=in_ap, scalar1=scale, scalar2=None, op0=ALU.mult),
                     reads=[('ps', pb)], writes=[wkey])

    def alt(name):
        return 'act' if S.rot(name, 2) == 0 else 'dve'

    def transpose_otok(o_tok, OT, tiles):
        for i in tiles:
            pb = 6 + S.rot('psT', 2)
            pst = PS[pb]
            for c in range(8):
                S.op('pe', lambda e, c=c: e.transpose(out=pst[:, c * 128:(c + 1) * 128], in_=o_tok[:, i, c * 128:(c + 1) * 128],
                                                    identity=ident), reads=[('otok', i), 'ident'], writes=[('ps', pb)])
            eng = alt('otev')
            ev_copy(eng, OT[:, :, i * 128:(i + 1) * 128], pst.rearrange("p (c t) -> p c t", c=8), pb, ('OT', i))

    def even_mixer(j, hT, o_tok):
        wb = [V(R3T, [128, 8, 512], BF16), V(R3T + 8 * KB, [128, 8, 512], BF16)]
        qaT = V(R0, [128, 4, NTOK], BF16)
        kaT = V(R0 + 18 * KB, [128, 4, NTOK], BF16)
        va = V(R0 + 36 * KB, [128, NT, 8, 65], BF16)
        S.op('pool', lambda e: e.memset(va[:, :, :, 64:65], 1.0), writes=['va1'])
        for blk in range(3):
            bi = S.rot('wb', 2)
            wk = load_w(wb[bi], abin_d[j, :, blk * 512:(blk + 1) * 512], 512, ('wb', bi))
            if blk < 2:
                dst = qaT if blk == 0 else kaT
                nm = 'qaT' if blk == 0 else 'kaT'
                sc = 0.125 if blk == 0 else None
                for c in range(4):
                    def evac(ps, pb, t0, t1, c=c, dst=dst, nm=nm, sc=sc):
                        ev_copy(alt('ev'), dst[:, c, t0:t1], ps[:, 0:t1 - t0], pb, (nm, c, t0), scale=sc)
                    proj_fm(hT, wb[bi], wk, c * 128, 128, evac)
            else:
                def evac(ps, pb, i):
                    ev_copy(alt('ev'), va[:, i, :, 0:64], ps[:, 0:512].rearrange("p (h d) -> p h d", h=8), pb, ('va', i))
                proj_tm(hT, wb[bi], wk, 0, 512, evac, range(NT))
        bt = [V(R3T + 16 * KB, [128, 5, 8, 128], BF16), V(R3T + 26 * KB, [128, 5, 8, 128], BF16)]
        pT = [V(R0 + 55 * KB, [128, 7, 128], BF16), V(R0 + 55 * KB + 1792, [128, 7, 128], BF16)]
        rec = V(R0 + 59 * KB, [128, 8, 1], F32)
        assert R3T + 36 * KB <= R3END
        PSF = [PS[0], PS[1], PS[2], PS[3], PS[4], PS[5], PS[6][:, :].bitcast(F32), PS[7][:, :].bitcast(F32)]
        rec2 = [rec, V(R0 + 59 * KB + 64, [128, 8, 1], F32)]
        tile_blocks, tile_bb = {}, {}

        def na_scores(i, h):
            if h == 0:
                if i < 16:
                    tile_blocks[i] = [(jt, pat) for (jt, pat) in NA_TILEMAP[i]] + [(16, None), (17, None)]
                    bb = S.rot('bt', 2)
                    tile_bb[i] = bb
                    for bi_, (jt, pat) in enumerate(NA_TILEMAP[i]):
                        S.dma('pool', f'bt{bb}_{bi_}', bt[bb][:, bi_, :, :], nab_d[j, pat, :, :].rearrange("k (h q) -> k h q", h=8),
                              writes=[('bt', bb, bi_)])
                else:
                    tile_blocks[i] = [(16, None), (17, None)]
                    tile_bb[i] = 0
            blocks, bb = tile_blocks[i], tile_bb[i]
            nb = len(blocks)
            p, pbs = h // 2, 64 * (h % 2)
            par = S.rot('nasc', 2)
            banks = [2 * par, 2 * par + 1]
            nw = sum(1 for (_, pat) in blocks if pat is not None)
            nA = min(nw, 4)
            if nA > 0:
                S.op('pe', lambda e: e.matmul(PS[banks[0]][:, 0:nA * 128].rearrange("p (b q) -> p b q", b=nA), lhsT=ident,
                                              rhs=bt[bb][:, 0:nA, h, :], start=True, stop=False),
                     reads=['ident'] + [('bt', bb, x) for x in range(nA)], writes=[('ps', banks[0])])
            for bi_, (jt, pat) in enumerate(blocks):
                bk = banks[bi_ // 4]
                dst = PS[bk][:, (bi_ % 4) * 128:(bi_ % 4 + 1) * 128]
                rk = [('kaT', p, t0) for (t0, t1) in GROUPS if t0 <= jt * 128 < t1] + \
                     [('qaT', p, t0) for (t0, t1) in GROUPS if t0 <= i * 128 < t1]
                inA = pat is not None and bi_ < 4
                if pat is not None and not inA:
                    S.op('pe', lambda e, dst=dst, bi_=bi_: e.matmul(dst, lhsT=ident, rhs=bt[bb][:, bi_, h, :], start=True, stop=False),
                         reads=['ident', ('bt', bb, bi_)], writes=[('ps', bk)])
                S.op('pe', lambda e, dst=dst, jt=jt, pat=pat, inA=inA, bi_=bi_: e.matmul(
                    dst, lhsT=kaT[pbs:pbs + 64, p, jt * 128:(jt + 1) * 128], rhs=qaT[pbs:pbs + 64, p, i * 128:(i + 1) * 128],
                    start=(pat is None), stop=((bi_ == nA - 1) if inA else True)), reads=rk, writes=[('ps', bk)])
            n0 = min(nb, 4)
            S.op('act', lambda e: e.activation(out=pT[par][:, 0:n0, :], in_=PS[2 * par][:, 0:n0 * 128].rearrange(
                "p (b q) -> p b q", b=n0), func=AF.Exp), reads=[('ps', 2 * par)], writes=[('pT', par, 0)])
            if nb > 4:
                n1 = nb - 4
                S.op('act', lambda e: e.activation(out=pT[par][:, 4:4 + n1, :], in_=PS[2 * par + 1][:, 0:n1 * 128].rearrange(
                    "p (b q) -> p b q", b=n1), func=AF.Exp), reads=[('ps', 2 * par + 1)], writes=[('pT', par, 1)])
            return par

        def na_pv(i, h, par):
            blocks = tile_blocks[i]
            nb = len(blocks)
            oset = 4 + 2 * (i % 2)
            ob = oset + h // 4
            od = PSF[ob][:, (h % 4) * 65:(h % 4) * 65 + 65]
            for bi_, (jt, pat) in enumerate(blocks):
                S.op('pe', lambda e, bi_=bi_, jt=jt: e.matmul(od, lhsT=pT[par][:, bi_, :], rhs=va[:, jt, h, :],
                                                            start=(bi_ == 0), stop=(bi_ == nb - 1)),
                     reads=[('pT', par, 0), ('pT', par, 1), ('va', jt), 'va1'], writes=[('ps', ob)])
            if h == 7:
                for hf in range(2):
                    ob2 = oset + hf
                    rc = rec2[i % 2]
                    ov = PSF[ob2][:, 0:260].rearrange("p (h e) -> p h e", e=65)
                    S.op('dve', lambda e, ov=ov, hf=hf, rc=rc: e.reciprocal(out=rc[:, hf * 4:(hf + 1) * 4, :], in_=ov[:, :, 64:65]),
                         reads=[('ps', ob2)], writes=[('rec', i % 2, hf)])
                    S.op('dve', lambda e, ov=ov, hf=hf, rc=rc: e.tensor_tensor(
                        out=o_tok[:, i, hf * 256:(hf + 1) * 256].rearrange("p (h d) -> p h d", h=4), in0=ov[:, :, 0:64],
                        in1=rc[:, hf * 4:(hf + 1) * 4, :].to_broadcast([128, 4, 64]), op=ALU.mult),
                        reads=[('ps', ob2), ('rec', i % 2, hf)], writes=[('otok', i)])

        items = [(i, h) for i in range(NT) for h in range(8)]
        pend = None
        for (i, h) in items:
            par = na_scores(i, h)
            if pend is not None:
                na_pv(*pend)
            pend = (i, h, par)
        na_pv(*pend)
        S.barrier()
        if stop == 'na':
            raise _Stop()
        qbT = V(R0, [128, 2, NTOK], BF16)
        kbT = V(R0 + 9 * KB, [128, 2, NTOK], BF16)
        kbk = V(R0 + 18 * KB, [128, NT, 256], BF16)
        vb = V(R0 + 27 * KB, [128, NT, 512], BF16)
        sgg = V(R0 + 45 * KB, [128, NT, 512], BF16)
        lrT1 = V(R0 + 63 * KB, [128, NTOK], BF16)
        W2b = V(R0 + 68 * KB, [128, 512], BF16)
        gnbc = V(R0 + 69 * KB, [128, 512], F32)
        S.dma('pool', 'w2b', W2b[0:33, :], wa2b_d[j], writes=['W2b'])
        S.dma('sp', 'gn', gnbc, gn_d[j].broadcast_to([128, 512]), writes=['gnbc'])
        S.op('pool', lambda e: e.memset(lrT1[32:33, :], 1.0), writes=['lr1'])
        sgt = [V(R3T + 16 * KB, [128, 512], F32), V(R3T + 18 * KB, [128, 512], F32)]
        bi = S.rot('wb', 2)
        wk = load_w(wb[bi], abin_d[j, :, 1536:2048], 512, ('wb', bi))
        for c in range(2):
            def evq(ps, pb, t0, t1, c=c):
                ev_copy(alt('ev'), qbT[:, c, t0:t1], ps[:, 0:t1 - t0], pb, ('qbT', c, t0), scale=0.125)
            proj_fm(hT, wb[bi], wk, c * 128, 128, evq)
            def evk(ps, pb, t0, t1, c=c):
                ev_copy(alt('ev'), kbT[:, c, t0:t1], ps[:, 0:t1 - t0], pb, ('kbT', c, t0))
            proj_fm(hT, wb[bi], wk, 256 + c * 128, 128, evk)
        def evkk(ps, pb, i):
            ev_copy(alt('ev'), kbk[:, i, :], ps[:, 0:256], pb, ('kbk', i))
        proj_tm(hT, wb[bi], wk, 256, 256, evkk, range(NT))
        bi = S.rot('wb', 2)
        wk = load_w(wb[bi], abin_d[j, :, 2048:2560], 512, ('wb', bi))
        def evv(ps, pb, i):
            ev_copy(alt('ev'), vb[:, i, :], ps[:, 0:512], pb, ('vb', i))
        proj_tm(hT, wb[bi], wk, 0, 512, evv, range(NT))
        bi = S.rot('wb', 2)
        wk = load_w(wb[bi], abin_d[j, :, 2560:3072], 512, ('wb', bi))
        def evg(ps, pb, i):
            b = S.rot('sgt', 2)
            S.op('act', lambda e: e.activation(out=sgt[b], in_=ps[:, 0:512], func=AF.Silu), reads=[('ps', pb)], writes=[('sgt', b)])
            S.op('pool', lambda e: e.tensor_tensor(out=sgg[:, i, :], in0=sgt[b], in1=gnbc, op=ALU.mult),
                 reads=[('sgt', b), 'gnbc'], writes=[('sgg', i)])
        proj_tm(hT, wb[bi], wk, 0, 512, evg, range(NT))
        bi = S.rot('wb', 2)
        wk = load_w(wb[bi], abin_d[j, :, 3072:3104], 32, ('wb', bi))
        def evl(ps, pb, t0, t1):
            ev_copy(alt('ev'), lrT1[0:32, t0:t1], ps[0:32, 0:t1 - t0], pb, ('lrT', t0))
        proj_fm(hT, wb[bi], wk, 0, 32, evl)
        S.barrier()
        if stop == 'glaproj':
            raise _Stop()
        oacc = V(R1, [128, NT, 512], F32)
        t = R3T
        TS = []
        for si in range(2):
            d_ = {}
            d_['E32'] = V(t, [128, 256], F32); t += KB
            d_['L32'] = V(t, [128, 256], F32); t += KB
            d_['eT'] = V(t, [128, 2, 128], F32); t += KB
            d_['enT'] = V(t, [128, 2, 128], F32); t += KB
            d_['krem'] = V(t, [128, 256], F32); t += KB
            d_['qtT'] = V(t, [128, 2, 128], BF16); t += 512
            d_['ktT'] = V(t, [128, 2, 128], BF16); t += 512
            d_['kend'] = V(t, [128, 256], BF16); t += 512
            d_['Abf'] = V(t, [128, 4, 128], BF16); t += KB
            TS.append(d_)
        Sst = [V(t, [128, 2, 128], F32), V(t + KB, [128, 2, 128], F32)]; t += 2 * KB
        Sbf = [V(t, [128, 2, 128], BF16), V(t + 512, [128, 2, 128], BF16)]; t += KB
        sq = V(t, [128, 512], F32); t += 2 * KB
        t1b = V(t, [128, 512], F32); t += 2 * KB
        ss4 = V(t, [128, 4], F32); t += 16
        rs4 = V(t, [128, 4], F32); t += 16
        assert t <= R3END
        PA = [PS[3], PS[6][:, :].bitcast(F32)]
        PO = [PS[4], PS[7][:, :].bitcast(F32)]
        pak, pok = [3, 6], [4, 7]
        for d in range(2):
            S.op('pool', lambda e, d=d: e.memset(Sst[d], 0.0), writes=[('Sst', d, 0), ('Sst', d, 1)])
            S.op('pool', lambda e, d=d: e.memset(Sbf[d], 0.0), writes=[('Sbf', d, 0), ('Sbf', d, 1)])
        orders = [[16, 17] + list(range(16)), [17, 16] + list(range(15, -1, -1))]
        visited = set()

        def gla_front(d, ti, si):
            T_ = TS[si]
            E32, L32, eT, enT, krem, qtT, ktT, kend, Abf = (T_[k] for k in ('E32', 'L32', 'eT', 'enT', 'krem', 'qtT', 'ktT', 'kend', 'Abf'))
            Ud, Rd, mk = (Uf, Rf, mkf) if d == 0 else (Ub, Rb, mkb)
            Uk, Rk, mkk = ('Uf', 'Rf', 'mkf') if d == 0 else ('Ub', 'Rb', 'mkb')
            tc0 = ti * 128
            tg = [t0 for (t0, t1_) in GROUPS if t0 <= tc0 < t1_][0]
            K = lambda n: (n, si)
            S.op('pe', lambda e: e.matmul(PS[0][:, 0:256], lhsT=lrT1[0:33, tc0:tc0 + 128], rhs=W2b[0:33, d * 256:(d + 1) * 256],
                                          start=True, stop=True), reads=[('lrT', tg), 'lr1', 'W2b'], writes=[('ps', 0)])
            S.op('act', lambda e: e.activation(out=E32, in_=PS[0][:, 0:256], func=AF.Exp, scale=-1.0), reads=[('ps', 0)], writes=[K('E32')])
            S.op('act', lambda e: e.activation(out=L32, in_=E32, func=AF.Ln, bias=1.0), reads=[K('E32')], writes=[K('L32')])
            for p in range(2):
                S.op('pe', lambda e, p=p: e.matmul(PS[1][:, p * 128:(p + 1) * 128], lhsT=L32[:, p * 128:(p + 1) * 128], rhs=Ud,
                                                  start=True, stop=True), reads=[K('L32'), Uk], writes=[('ps', 1)])
            S.op('pe', lambda e: e.matmul(PS[2][:, 0:256], lhsT=Rd, rhs=L32, start=True, stop=True), reads=[K('L32'), Rk], writes=[('ps', 2)])
            S.op('act', lambda e: e.activation(out=eT, in_=PS[1][:, 0:256].rearrange("p (a b) -> p a b", a=2), func=AF.Exp),
                 reads=[('ps', 1)], writes=[K('eT')])
            S.op('act', lambda e: e.activation(out=enT, in_=PS[1][:, 0:256].rearrange("p (a b) -> p a b", a=2), func=AF.Exp, scale=-1.0),
                 reads=[('ps', 1)], writes=[K('enT')])
            S.op('act', lambda e: e.activation(out=krem, in_=PS[2][:, 0:256], func=AF.Exp), reads=[('ps', 2)], writes=[K('krem')])
            S.op('dve', lambda e: e.tensor_tensor(out=qtT, in0=qbT[:, :, tc0:tc0 + 128], in1=eT, op=ALU.mult),
                 reads=[('qbT', 0, tg), ('qbT', 1, tg), K('eT')], writes=[K('qtT')])
            S.op('pool', lambda e: e.tensor_tensor(out=ktT, in0=kbT[:, :, tc0:tc0 + 128], in1=enT, op=ALU.mult),
                 reads=[('kbT', 0, tg), ('kbT', 1, tg), K('enT')], writes=[K('ktT')])
            S.op('pool', lambda e: e.tensor_tensor(out=kend, in0=kbk[:, ti, :], in1=krem, op=ALU.mult),
                 reads=[('kbk', ti), K('krem')], writes=[K('kend')])
            for h in range(4):
                p, pbs = h // 2, 64 * (h % 2)
                S.op('pe', lambda e, h=h, p=p, pbs=pbs: e.matmul(PA[h % 2][:, p * 128:(p + 1) * 128], lhsT=ktT[pbs:pbs + 64, p, :],
                                                              rhs=qtT[pbs:pbs + 64, p, :], start=True, stop=True),
                     reads=[K('ktT'), K('qtT')], writes=[('ps', pak[h % 2])])
            for h in range(4):
                S.op('dve', lambda e, h=h: e.tensor_tensor(out=Abf[:, h, :], in0=PA[h % 2][:, (h // 2) * 128:(h // 2 + 1) * 128],
                                                           in1=mk, op=ALU.mult),
                     reads=[('ps', pak[h % 2]), mkk], writes=[K('Abf')])

        def gla_back(d, ti, si):
            T_ = TS[si]
            eT, qtT, kend, Abf = T_['eT'], T_['qtT'], T_['kend'], T_['Abf']
            K = lambda n: (n, si)
            last = 127 if d == 0 else 0
            for h in range(4):
                p, pbs = h // 2, 64 * (h % 2)
                S.op('pe', lambda e, h=h, p=p: e.matmul(PO[h % 2][:, p * 128:(p + 1) * 128], lhsT=Abf[:, h, :], rhs=vb[:, ti, h * 128:(h + 1) * 128],
                                                       start=True, stop=False), reads=[K('Abf'), ('vb', ti)], writes=[('ps', pok[h % 2])])
                S.op('pe', lambda e, h=h, p=p, pbs=pbs: e.matmul(PO[h % 2][:, p * 128:(p + 1) * 128], lhsT=qtT[pbs:pbs + 64, p, :],
                                                              rhs=Sbf[d][pbs:pbs + 64, p, :], start=False, stop=True),
                     reads=[K('qtT'), ('Sbf', d, 0), ('Sbf', d, 1)], writes=[('ps', pok[h % 2])])
            first = ti not in visited
            visited.add(ti)
            for par in range(2):
                ov_ = oacc[:, ti, :].rearrange("p (a b v) -> p a b v", a=2, b=2)[:, :, par, :]
                pv_ = PO[par][:, 0:256].rearrange("p (a v) -> p a v", a=2)
                if first:
                    S.op('act', lambda e, ov_=ov_, pv_=pv_: e.activation(out=ov_, in_=pv_, func=AF.Copy),
                         reads=[('ps', pok[par])], writes=[('oacc', ti, par)])
                else:
                    S.op('dve', lambda e, ov_=ov_, pv_=pv_: e.tensor_tensor(out=ov_, in0=pv_, in1=ov_, op=ALU.add),
                         reads=[('ps', pok[par]), ('oacc', ti, par)], writes=[('oacc', ti, par)])
            for p in range(2):
                S.op('pe', lambda e, p=p: e.matmul(PS[5][:, p * 256:(p + 1) * 256], lhsT=kend[:, p * 128:(p + 1) * 128],
                                                  rhs=vb[:, ti, p * 256:(p + 1) * 256], start=True, stop=True),
                     reads=[K('kend'), ('vb', ti)], writes=[('ps', 5)])
            for p in range(2):
                for hp in range(2):
                    r0 = hp * 64
                    S.op('dve', lambda e, p=p, hp=hp, r0=r0: e.scalar_tensor_tensor(
                        out=Sst[d][r0:r0 + 64, p, :], in0=Sst[d][r0:r0 + 64, p, :], scalar=eT[r0:r0 + 64, p, last:last + 1],
                        in1=PS[5][r0:r0 + 64, p * 256 + hp * 128: p * 256 + (hp + 1) * 128], op0=ALU.mult, op1=ALU.add),
                        reads=[('ps', 5), K('eT'), ('Sst', d, hp)], writes=[('Sst', d, hp)])
            for hp in range(2):
                r0 = hp * 64
                S.op('act', lambda e, r0=r0: e.activation(out=Sbf[d][r0:r0 + 64, :, :], in_=Sst[d][r0:r0 + 64, :, :], func=AF.Copy),
                     reads=[('Sst', d, hp)], writes=[('Sbf', d, hp)])
            if not first:
                S.op('dve', lambda e: e.tensor_tensor(out=sq, in0=oacc[:, ti, :], in1=oacc[:, ti, :], op=ALU.mult),
                     reads=[('oacc', ti, 0), ('oacc', ti, 1)], writes=['sq'])
                S.op('dve', lambda e: e.tensor_reduce(out=ss4, in_=sq.rearrange("p (h v) -> p h v", h=4), axis=AX.X, op=ALU.add),
                     reads=['sq'], writes=['ss4'])
                S.op('act', lambda e: e.activation(out=rs4, in_=ss4, func=AF.Ln, scale=1.0 / 128.0, bias=EPS), reads=['ss4'], writes=['rs4t'])
                S.op('act', lambda e: e.activation(out=rs4, in_=rs4, func=AF.Exp, scale=-0.5), reads=['rs4t'], writes=['rs4'])
                S.op('dve', lambda e: e.tensor_tensor(out=t1b.rearrange("p (h v) -> p h v", h=4),
                                                      in0=oacc[:, ti, :].rearrange("p (h v) -> p h v", h=4),
                                                      in1=rs4.unsqueeze(2).to_broadcast([128, 4, 128]), op=ALU.mult),
                     reads=[('oacc', ti, 0), ('oacc', ti, 1), 'rs4'], writes=['t1b'])
                S.op('pool', lambda e: e.tensor_tensor(out=o_tok[:, ti, 512:1024], in0=t1b, in1=sgg[:, ti, :], op=ALU.mult),
                     reads=['t1b', ('sgg', ti)], writes=[('otok', ti)])

        gitems = []
        for k in range(NT):
            gitems.append((0, orders[0][k]))
            gitems.append((1, orders[1][k]))
        pend = None
        for n_, (d, ti) in enumerate(gitems):
            gla_front(d, ti, n_ % 2)
            if pend is not None:
                gla_back(*pend)
            pend = (d, ti, n_ % 2)
        gla_back(*pend)
        S.barrier()

    def odd_mixer(j, hT, o_tok, ctx_out):
        wb = [V(R3T, [128, 8, 512], BF16), V(R3T + 8 * KB, [128, 8, 512], BF16)]
        ropeC = V(R3T + 16 * KB, [128, SEQ], F32)
        ropeS = V(R3T + 24 * KB, [128, SEQ], F32)
        pT = V(R0 + 66 * KB, [128, 5, 4, 128], BF16)
        es = V(R3T + 32 * KB, [128, 16], F32)
        den = V(R3T + 32 * KB + 64, [128, 4, 1], F32)
        tq = [V(R0 + 62 * KB, [128, 512], F32), V(R0 + 64 * KB, [128, 512], F32)]
        assert R3T + 33 * KB <= R3END
        S.dma('sp', 'rc', ropeC, ropeC_d, writes=['ropeC'])
        S.dma('sp', 'rs', ropeS, ropeS_d, writes=['ropeS'])
        S.dma('sp', 'sk', es, sink_d[j].broadcast_to([128, 16]), writes=['es_raw'])
        S.op('act', lambda e: e.activation(out=es, in_=es, func=AF.Exp), reads=['es_raw'], writes=['es'])
        kT2 = V(R0, [128, 4, NTOK], BF16)
        krT2 = V(R0 + 18 * KB, [128, 4, SEQ], BF16)
        v65 = V(R0 + 34 * KB, [128, NT, 4, 65], BF16)
        qT = V(R0 + 44 * KB, [128, 2, NTOK], BF16)
        qrT = V(R0 + 53 * KB, [128, 2, SEQ], BF16)
        S.op('pool', lambda e: e.memset(v65[:, :, :, 64:65], 1.0), writes=['v1'])
        ntl = NT if ctx_out else 16

        def rope_evac(ps_a, pb_a, ps_b, pb_b, t0, t1, dst, key):
            n = t1 - t0
            b = S.rot('tq', 2)
            S.op('dve', lambda e: e.tensor_tensor(out=tq[b][:, 0:n], in0=ps_a[:, 0:n], in1=ropeC[:, t0:t1], op=ALU.mult),
                 reads=[('ps', pb_a), 'ropeC'], writes=[('tq', b)])
            b2 = S.rot('tq', 2)
            S.op('dve', lambda e: e.tensor_tensor(out=tq[b2][:, 0:n], in0=ps_b[:, 0:n], in1=ropeS[:, t0:t1], op=ALU.mult),
                 reads=[('ps', pb_b), 'ropeS'], writes=[('tq', b2)])
            S.op('pool', lambda e: e.tensor_tensor(out=dst, in0=tq[b][:, 0:n], in1=tq[b2][:, 0:n], op=ALU.add),
                 reads=[('tq', b), ('tq', b2)], writes=[key])

        def proj_pair(wbuf, wk, ca, cb, scale, dst_plain, dst_rope, nm, c):
            for (t0, t1) in GROUPS:
                n = t1 - t0
                pa = S.rot('psP', 4); psa = PS[pa]
                rk = wk + [('hT', kc, i) for kc in range(8) for i in range(t0 // 128, t1 // 128)]
                for kc in range(8):
                    S.op('pe', lambda e, kc=kc: e.matmul(psa[:, 0:n], lhsT=wbuf[:, kc, ca:ca + 128], rhs=hT[:, kc, t0:t1],
                                                        start=(kc == 0), stop=(kc == 7)), reads=rk, writes=[('ps', pa)])
                ev_copy('act', dst_plain[:, c, t0:t1], psa[:, 0:n], pa, (nm, c, t0), scale=scale)
                if t0 < SEQ:
                    pb2 = S.rot('psP', 4); psb = PS[pb2]
                    for kc in range(8):
                        S.op('pe', lambda e, kc=kc: e.matmul(psb[:, 0:n], lhsT=wbuf[:, kc, cb:cb + 128], rhs=hT[:, kc, t0:t1],
                                                            start=(kc == 0), stop=(kc == 7)), reads=rk, writes=[('ps', pb2)])
                    rope_evac(psa, pa, psb, pb2, t0, t1, dst_rope[:, c, t0:t1], (nm + 'r', c, t0))

        bi = S.rot('wb', 2); wk = load_w(wb[bi], swin_d[j, :, 2048:2560], 512, ('wb', bi))
        bi2 = S.rot('wb', 2); wk2 = load_w(wb[bi2], swin_d[j, :, 2560:3072], 512, ('wb', bi2))
        for g in range(4):
            for (t0, t1) in GROUPS:
                n = t1 - t0
                pa = S.rot('psP', 4); psa = PS[pa]
                rk = [('hT', kc, i) for kc in range(8) for i in range(t0 // 128, t1 // 128)]
                for kc in range(8):
                    S.op('pe', lambda e, kc=kc: e.matmul(psa[:, 0:n], lhsT=wb[bi][:, kc, g * 128:(g + 1) * 128], rhs=hT[:, kc, t0:t1],
                                                        start=(kc == 0), stop=(kc == 7)), reads=wk + rk, writes=[('ps', pa)])
                ev_copy('dve', kT2[:, g, t0:t1], psa[:, 0:n], pa, ('kT2', g, t0))
                if t0 < SEQ:
                    pb2 = S.rot('psP', 4); psb = PS[pb2]
                    for kc in range(8):
                        S.op('pe', lambda e, kc=kc: e.matmul(psb[:, 0:n], lhsT=wb[bi2][:, kc, g * 128:(g + 1) * 128], rhs=hT[:, kc, t0:t1],
                                                            start=(kc == 0), stop=(kc == 7)), reads=wk2 + rk, writes=[('ps', pb2)])
                    rope_evac(psa, pa, psb, pb2, t0, t1, krT2[:, g, t0:t1], ('krT2', g, t0))
        bi = S.rot('wb', 2); wk = load_w(wb[bi], swin_d[j, :, 3072:3328], 256, ('wb', bi))
        def evv(ps, pb, i):
            ev_copy(alt('ev'), v65[:, i, :, 0:64], ps[:, 0:256].rearrange("p (h d) -> p h d", h=4), pb, ('v65', i))
        proj_tm(hT, wb[bi], wk, 0, 256, evv, range(NT))

        for g in range(4):
            bi = S.rot('wb', 2); wk = load_w(wb[bi], swin_d[j, :, g * 256:(g + 1) * 256], 256, ('wb', bi))
            bi2 = S.rot('wb', 2); wk2 = load_w(wb[bi2], swin_d[j, :, 1024 + g * 256:1024 + (g + 1) * 256], 256, ('wb', bi2))
            for c in range(2):
                for (t0, t1) in GROUPS:
                    n = t1 - t0
                    pa = S.rot('psP', 4); psa = PS[pa]
                    rk = [('hT', kc, i) for kc in range(8) for i in range(t0 // 128, t1 // 128)]
                    for kc in range(8):
                        S.op('pe', lambda e, kc=kc: e.matmul(psa[:, 0:n], lhsT=wb[bi][:, kc, c * 128:(c + 1) * 128], rhs=hT[:, kc, t0:t1],
                                                            start=(kc == 0), stop=(kc == 7)), reads=wk + rk, writes=[('ps', pa)])
                    ev_copy('dve', qT[:, c, t0:t1], psa[:, 0:n], pa, ('qT', c, t0), scale=0.125)
                    if t0 < SEQ:
                        pb2 = S.rot('psP', 4); psb = PS[pb2]
                        for kc in range(8):
                            S.op('pe', lambda e, kc=kc: e.matmul(psb[:, 0:n], lhsT=wb[bi2][:, kc, c * 128:(c + 1) * 128], rhs=hT[:, kc, t0:t1],
                                                                start=(kc == 0), stop=(kc == 7)), reads=wk2 + rk, writes=[('ps', pb2)])
                        n_ = n
                        b = S.rot('tq', 2)
                        S.op('dve', lambda e, b=b: e.scalar_tensor_tensor(out=tq[b][:, 0:n_], in0=psa[:, 0:n_], scalar=0.125,
                                                                          in1=ropeC[:, t0:t1], op0=ALU.mult, op1=ALU.mult),
                             reads=[('ps', pa), 'ropeC'], writes=[('tq', b)])
                        b2 = S.rot('tq', 2)
                        S.op('dve', lambda e, b2=b2: e.scalar_tensor_tensor(out=tq[b2][:, 0:n_], in0=psb[:, 0:n_], scalar=0.125,
                                                                            in1=ropeS[:, t0:t1], op0=ALU.mult, op1=ALU.mult),
                             reads=[('ps', pb2), 'ropeS'], writes=[('tq', b2)])
                        S.op('pool', lambda e, b=b, b2=b2: e.tensor_tensor(out=qrT[:, c, t0:t1], in0=tq[b][:, 0:n_], in1=tq[b2][:, 0:n_], op=ALU.add),
                             reads=[('tq', b), ('tq', b2)], writes=[('qrT', c, t0)])
            pT2 = [pT, V(R3T + 33 * KB, [128, 5, 4, 128], BF16)]
            den2 = [den, V(R3T + 32 * KB + 128, [128, 4, 1], F32)]

            def sw_blocks(i):
                if i < 16:
                    blocks = []
                    if i > 0:
                        blocks.append((i - 1, 'p'))
                    blocks.append((i, 'l'))
                    if i < 15:
                        blocks.append((i + 1, 'n'))
                    return blocks + [(16, 'c'), (17, 'c')]
                return [(16, 'c'), (17, 'c')]

            def sw_scores(i, pi):
                blocks = sw_blocks(i)
                pTc = pT2[pi]
                tgq = [t0 for (t0, t1_) in GROUPS if t0 <= i * 128 < t1_][0]
                for bi_, (jt, kind) in enumerate(blocks):
                    tgk = [t0 for (t0, t1_) in GROUPS if t0 <= jt * 128 < t1_][0]
                    roped = kind in ('p', 'l', 'n')
                    ksrc = krT2 if roped else kT2
                    qsrc = qrT if roped else qT
                    kkey = ('krT2', g, tgk) if roped else ('kT2', g, tgk)
                    qn = 'qrT' if roped else 'qT'
                    par = S.rot('swsc', 2)
                    for half in range(2):
                        bk = 2 * par + half
                        r0 = half * 64
                        first = True
                        if kind in ('p', 'n'):
                            bm = bmp if kind == 'p' else bmn
                            S.op('pe', lambda e, bk=bk, bm=bm: e.matmul(PS[bk][:, 0:256].rearrange("p (h q) -> p h q", h=2), lhsT=ident,
                                                                      rhs=bm.unsqueeze(1).to_broadcast([128, 2, 128]), start=True, stop=False),
                                 reads=['ident', 'bmp', 'bmn'], writes=[('ps', bk)])
                            first = False
                        S.op('pe', lambda e, bk=bk, r0=r0, jt=jt, ksrc=ksrc, qsrc=qsrc, first=first: e.matmul(
                            PS[bk][:, 0:256].rearrange("p (c q) -> p c q", c=2), lhsT=ksrc[r0:r0 + 64, g, jt * 128:(jt + 1) * 128],
                            rhs=qsrc[r0:r0 + 64, :, i * 128:(i + 1) * 128], start=first, stop=True),
                            reads=[kkey, (qn, 0, tgq), (qn, 1, tgq)], writes=[('ps', bk)])
                        S.op('act', lambda e, bk=bk, bi_=bi_, half=half: e.activation(
                            out=pTc[:, bi_, :, :].rearrange("p (c f) q -> p c f q", c=2)[:, :, half, :],
                            in_=PS[bk][:, 0:256].rearrange("p (c q) -> p c q", c=2), func=AF.Exp),
                            reads=[('ps', bk)], writes=[('pT', pi, bi_, half)])

            def sw_pv(i, pi):
                blocks = sw_blocks(i)
                nb = len(blocks)
                pTc = pT2[pi]
                dn = den2[pi]
                ob = 4 + pi
                for hh in range(4):
                    od = PS[ob][:, hh * 65:(hh + 1) * 65]
                    for bi_, (jt, kind) in enumerate(blocks):
                        S.op('pe', lambda e, od=od, bi_=bi_, jt=jt, hh=hh: e.matmul(od, lhsT=pTc[:, bi_, hh, :], rhs=v65[:, jt, g, :],
                                                                                 start=(bi_ == 0), stop=(bi_ == nb - 1)),
                             reads=[('pT', pi, bi_, 0), ('pT', pi, bi_, 1), ('v65', jt), 'v1'], writes=[('ps', ob)])
                ov = PS[ob][:, 0:260].rearrange("p (h e) -> p h e", e=65)
                S.op('dve', lambda e: e.tensor_tensor(out=dn, in0=ov[:, :, 64:65], in1=es[:, g * 4:(g + 1) * 4].unsqueeze(2), op=ALU.add),
                     reads=[('ps', ob), 'es'], writes=[('den_t', pi)])
                S.op('dve', lambda e: e.reciprocal(out=dn, in_=dn), reads=[('den_t', pi)], writes=[('den', pi)])
                S.op('dve', lambda e: e.tensor_tensor(
                    out=o_tok[:, i, g * 256:(g + 1) * 256].rearrange("p (h d) -> p h d", h=4), in0=ov[:, :, 0:64],
                    in1=dn.to_broadcast([128, 4, 64]), op=ALU.mult), reads=[('ps', ob), ('den', pi)], writes=[('otok', i)])

            pend = None
            for n_, i in enumerate(range(ntl)):
                sw_scores(i, n_ % 2)
                if pend is not None:
                    sw_pv(*pend)
                pend = (i, n_ % 2)
            sw_pv(*pend)
        S.barrier()

    def out_proj(wsrc, OT, l, first_layer, tiles):
        wout = V(R3T, [128, 8, 1024], BF16)
        tmp = [V(R3T + 16 * KB, [128, 1024], F32), V(R3T + 20 * KB, [128, 1024], F32)]
        wk = []
        sv = wsrc.rearrange("(c p) n -> p c n", p=128)
        for hq in range(4):
            S.dma('pool', f'wout{hq}', wout[:, hq * 2:(hq + 1) * 2, :], sv[:, hq * 2:(hq + 1) * 2, :], writes=[('wout', hq)])
            wk.append(('wout', hq))
        for i in tiles:
            src = tok_rows(i) if first_layer else xres_d[i * 128:(i + 1) * 128, :]
            S.dma('sp', f'xld{i}', x_res[:, i, :], src, writes=[('xres', i)])
        for i in tiles:
            w = 0 if i < 16 else 1
            b = S.rot('tmp', 2)
            for hf in range(2):
                pb = 4 + hf
                for kc in range(8):
                    S.op('pe', lambda e, kc=kc, hf=hf, pb=pb: e.matmul(PS[pb][:, :], lhsT=OT[:, kc, i * 128:(i + 1) * 128],
                                                                     rhs=wout[:, kc, hf * 512:(hf + 1) * 512], start=(kc == 0), stop=(kc == 7)),
                         reads=wk + [('OT', i)], writes=[('ps', pb)])
                S.op('dve', lambda e, hf=hf, pb=pb, w=w, b=b: e.tensor_tensor(out=tmp[b][:, hf * 512:(hf + 1) * 512], in0=PS[pb][:, :],
                                                                             in1=gate_bc[:, w, hf * 512:(hf + 1) * 512], op=ALU.mult),
                     reads=[('ps', pb), ('gate', w, hf)], writes=[('tmp', b, hf)])
            S.op('pool', lambda e, b=b: e.tensor_tensor(out=x_res[:, i, :], in0=x_res[:, i, :], in1=tmp[b], op=ALU.add),
                 reads=[('tmp', b, 0), ('tmp', b, 1), ('xres', i)], writes=[('xres', i)])

    def mlp(l, hT, tiles):
        uT = V(R1, [128, 4, NTOK], BF16)
        tmp = [V(R1 + 18 * KB, [128, 1024], F32), V(R1 + 22 * KB, [128, 1024], F32)]
        rl = [V(R1 + 26 * KB, [128, 512], F32), V(R1 + 28 * KB, [128, 512], F32)]
        W1 = [V(R3T, [128, 8, 512], BF16), V(R3T + 8 * KB, [128, 8, 512], BF16)]
        W2 = [V(R3T + 16 * KB, [128, 4, 1024], BF16), V(R3T + 24 * KB, [128, 4, 1024], BF16)]
        assert R3T + 32 * KB <= R3END
        thi = (max(tiles) + 1) * 128
        for blk in range(8):
            bi = S.rot('mlpw', 2)
            w1v = w1_d[l, :, blk * 512:(blk + 1) * 512].rearrange("(c p) n -> p c n", p=128)
            w2v = w2_d[l, blk * 512:(blk + 1) * 512, :].rearrange("(c p) n -> p c n", p=128)
            k1, k2 = [], []
            for hq in range(2):
                S.dma('pool', f'w1_{bi}_{hq}', W1[bi][:, hq * 4:(hq + 1) * 4, :], w1v[:, hq * 4:(hq + 1) * 4, :], writes=[('W1', bi, hq)])
                k1.append(('W1', bi, hq))
            for hq in range(2):
                S.dma('pool', f'w2_{bi}_{hq}', W2[bi][:, hq * 2:(hq + 1) * 2, :], w2v[:, hq * 2:(hq + 1) * 2, :], writes=[('W2', bi, hq)])
                k2.append(('W2', bi, hq))
            for (t0, t1) in GROUPS:
                if t0 >= thi:
                    continue
                n = t1 - t0
                for fc in range(4):
                    pb = S.rot('psU', 4)
                    for kc in range(8):
                        S.op('pe', lambda e, kc=kc, fc=fc, pb=pb: e.matmul(PS[pb][:, 0:n], lhsT=W1[bi][:, kc, fc * 128:(fc + 1) * 128],
                                                                         rhs=hT[:, kc, t0:t1], start=(kc == 0), stop=(kc == 7)),
                             reads=k1 + [('hT', kc, i) for i in range(t0 // 128, t1 // 128)], writes=[('ps', pb)])
                    rb = S.rot('rl', 2)
                    S.op('act', lambda e, pb=pb, rb=rb: e.activation(out=rl[rb][:, 0:n], in_=PS[pb][:, 0:n], func=AF.Relu),
                         reads=[('ps', pb)], writes=[('rl', rb)])
                    S.op('pool', lambda e, rb=rb, fc=fc: e.tensor_tensor(out=uT[:, fc, t0:t1], in0=rl[rb][:, 0:n], in1=rl[rb][:, 0:n], op=ALU.mult),
                         reads=[('rl', rb)], writes=[('uT', fc, t0)])
            for i in tiles:
                w = 2 + (0 if i < 16 else 1)
                tg = [t0 for (t0, t1_) in GROUPS if t0 <= i * 128 < t1_][0]
                b = S.rot('tmp', 2)
                for hf in range(2):
                    pb = 4 + hf
                    for fc in range(4):
                        S.op('pe', lambda e, fc=fc, hf=hf, pb=pb: e.matmul(PS[pb][:, :], lhsT=uT[:, fc, i * 128:(i + 1) * 128],
                                                                         rhs=W2[bi][:, fc, hf * 512:(hf + 1) * 512], start=(fc == 0), stop=(fc == 3)),
                             reads=k2 + [('uT', fc, tg)], writes=[('ps', pb)])
                    S.op('dve', lambda e, hf=hf, pb=pb, w=w, b=b: e.tensor_tensor(out=tmp[b][:, hf * 512:(hf + 1) * 512], in0=PS[pb][:, :],
                                                                                 in1=gate_bc[:, w, hf * 512:(hf + 1) * 512], op=ALU.mult),
                         reads=[('ps', pb), ('gate', w, hf)], writes=[('tmp', b, hf)])
                S.op('pool', lambda e, b=b: e.tensor_tensor(out=x_res[:, i, :], in0=x_res[:, i, :], in1=tmp[b], op=ALU.add),
                     reads=[('tmp', b, 0), ('tmp', b, 1), ('xres', i)], writes=[('xres', i)])

    hT_A = V(R1, [128, 8, NTOK], BF16)
    hT_B = V(R2, [128, 8, NTOK], BF16)
    o_tok = V(R2, [128, NT, D], BF16)
    OT = V(R1, [128, 8, NTOK], BF16)

    for i in range(NT):
        S.dma('sp', f'xld{i}', x_res[:, i, :], tok_rows(i), writes=[('xres', i)])
    import os
    STOP_L = int(os.environ.get('STOP_L', '0'))
    cur_l = [0]

    def chk(name):
        if stop == name and cur_l[0] == STOP_L:
            S.barrier()
            raise _Stop()

    try:
      for l in range(n_layers):
        ctx_out = l < n_layers - 1
        cur_l[0] = l
        adaln(l)
        S.barrier()
        chk('adaln')
        norm_to_hT(hT_A, GpM, 'GpM', 0, list(range(NT)), R2)
        if l > 0:
            pass
        if l > 0:
            for i in range(NT):
                S.dma('sp', f'xst{i}', xres_d[i * 128:(i + 1) * 128, :], x_res[:, i, :], reads=[('xres', i)])
        S.barrier()
        chk('normA')
        if l % 2 == 0:
            even_mixer(l // 2, hT_A, o_tok)
            tiles = list(range(NT))
        else:
            odd_mixer(l // 2, hT_A, o_tok, ctx_out)
            tiles = list(range(NT if ctx_out else 16))
        chk('mixer')
        transpose_otok(o_tok, OT, tiles)
        S.barrier()
        chk('tr')
        out_proj(about_d[l // 2] if l % 2 == 0 else swout_d[l // 2], OT, l, l == 0, tiles)
        S.barrier()
        norm_to_hT(hT_B, GpL, 'GpL', 2, tiles, R1)
        S.barrier()
        chk('norm2')
        mlp(l, hT_B, tiles)
        S.barrier()
    except _Stop:
        pass
    nf = V(R3T, [128, D], F32)
    junk = V(R1, [128, D], F32)
    yo = [V(R1 + 4 * KB, [128, D], F32), V(R1 + 8 * KB, [128, D], F32)]
    S.dma('sp', 'nf', nf, nfin_d.broadcast_to([128, D]), writes=['nf'])
    for i in range(16):
        S.op('act', lambda e, i=i: e.activation(out=junk, in_=x_res[:, i, :], func=AF.Square, accum_out=ssA[:, i:i + 1]),
             reads=[('xres', i)], writes=['junk', ('ssA', i)])
    S.op('act', lambda e: e.activation(out=rsA[:, 0:16], in_=ssA[:, 0:16], func=AF.Sqrt, scale=1.0 / D, bias=EPS),
         reads=[('ssA', i) for i in range(16)], writes=['rsA_t'])
    S.op('dve', lambda e: e.reciprocal(out=rsA[:, 0:16], in_=rsA[:, 0:16]), reads=['rsA_t'], writes=['rsA'])
    for i in range(16):
        b = S.rot('yo', 2)
        S.op('dve', lambda e, i=i, b=b: e.scalar_tensor_tensor(out=yo[b], in0=x_res[:, i, :], scalar=rsA[:, i:i + 1], in1=nf,
                                                             op0=ALU.mult, op1=ALU.mult), reads=[('xres', i), 'rsA', 'nf'], writes=[('yo', b)])
        S.dma('sp', f'ost{b}', out_d[i * 128:(i + 1) * 128, :], yo[b], reads=[('yo', b)])
    S.barrier()
    for c in reversed(ps_cm):
        c.__exit__(None, None, None)
    arena_cm.__exit__(None, None, None)
    S.close()
    return nc


def _prep_shared(inp):
    f = np.float32
    sh = {}
    sh["ada_w"] = np.ascontiguousarray(inp["ada_w"], dtype=f)
    ada_b = np.asarray(inp["ada_b"], dtype=f)
    sh["ada_bT"] = np.ascontiguousarray(ada_b.reshape(DEPTH, 48, 128).transpose(0, 2, 1))
    sh["ada_bf"] = np.ascontiguousarray(ada_b.reshape(DEPTH, 1, 6 * D))
    sh["nmix"] = np.ascontiguousarray(np.asarray(inp["norm_mix"], dtype=f).reshape(DEPTH, 8, 128).transpose(0, 2, 1))
    sh["nmlp"] = np.ascontiguousarray(np.asarray(inp["norm_mlp"], dtype=f).reshape(DEPTH, 8, 128).transpose(0, 2, 1))
    sh["nfin"] = np.ascontiguousarray(np.asarray(inp["norm_final"], dtype=f).reshape(1, D))
    sh["w1"] = np.ascontiguousarray(inp["mlp_w1"], dtype=f)
    sh["w2"] = np.ascontiguousarray(inp["mlp_w2"], dtype=f)
    sh["abin"] = np.ascontiguousarray(inp["ab_w_in"], dtype=f)
    sh["about"] = np.ascontiguousarray(inp["ab_w_out"], dtype=f)
    rpb = np.asarray(inp["na_rpb"], dtype=f)
    nab = np.empty((2, N_PAT, 128, 8, 128), dtype=f)
    for pi, idx in enumerate(NA_PATS):
        valid = idx >= 0
        ic = np.where(valid, idx, 0)
        for jl in range(2):
            flat = rpb[jl].reshape(8, 15 * 31)
            g = flat[:, ic]
            g = np.where(valid[None], g, f(-1e30))
            nab[jl, pi] = g.transpose(1, 0, 2)
    sh["nabias"] = np.ascontiguousarray(nab.reshape(2, N_PAT, 128, 1024))
    wa2 = np.asarray(inp["gla_wa2"], dtype=f)
    ba = np.asarray(inp["gla_ba"], dtype=f)
    wa2b = np.zeros((2, 33, 512), dtype=f)
    for jl in range(2):
        for d in range(2):
            wa2b[jl, d * 16:(d + 1) * 16, d * 256:(d + 1) * 256] = wa2[jl, d]
            wa2b[jl, 32, d * 256:(d + 1) * 256] = ba[jl, d]
    sh["wa2b"] = wa2b
    sh["gn"] = np.ascontiguousarray(np.asarray(inp["gla_gnorm"], dtype=f).reshape(2, 1, 512))
    sw = np.asarray(inp["swa_w_in"], dtype=f)
    qcols = np.arange(1024)
    swap = qcols ^ 1
    kdup = np.concatenate([np.concatenate([1024 + g * 64 + np.arange(64)] * 2) for g in range(4)])
    kdup_sw = np.concatenate([np.concatenate([1024 + g * 64 + (np.arange(64) ^ 1)] * 2) for g in range(4)])
    vcols = 1280 + np.arange(256)
    cols = np.concatenate([qcols, swap, kdup, kdup_sw, vcols])
    sh["swin"] = np.ascontiguousarray(sw[:, :, cols])
    sh["swout"] = np.ascontiguousarray(inp["swa_w_out"], dtype=f)
    sh["sink"] = np.ascontiguousarray(np.asarray(inp["swa_sink"], dtype=f).reshape(2, 1, 16))
    Ct, St = _rope_tables()
    sh["ropeC"], sh["ropeS"] = Ct, St
    return sh


_NC_CACHE = {}


def kernel(**inp):
    n_layers = int(inp.pop("_n_layers", DEPTH))
    f = np.float32
    sh = _prep_shared(inp)
    x = np.asarray(inp["x"], dtype=f); c = np.asarray(inp["c"], dtype=f)
    ctx = np.asarray(inp["ctx"], dtype=f); c_ctx = np.asarray(inp["c_ctx"], dtype=f)
    in_maps = []
    for b in range(8):
        m = dict(sh)
        m["x"] = np.ascontiguousarray(x[b]); m["ctx"] = np.ascontiguousarray(ctx[b])
        m["scin"] = np.ascontiguousarray(np.concatenate([c[b].reshape(8, 128).T, c_ctx.reshape(8, 128).T], axis=1))
        in_maps.append(m)
    if n_layers not in _NC_CACHE:
        _NC_CACHE[n_layers] = build(n_layers)
    nc = _NC_CACHE[n_layers]
    res = run_bass_kernel_spmd(nc, in_maps, core_ids=list(range(8)))
    return np.stack([r["out"] for r in res.results], axis=0).astype(f)
```

```python
import numpy as np
import concourse.bass as bass
import concourse.mybir as mybir
from concourse.bass_utils import run_bass_kernel_spmd

F32 = mybir.dt.float32
BF16 = mybir.dt.bfloat16
AF = mybir.ActivationFunctionType
ALU = mybir.AluOpType
AX = mybir.AxisListType

D = 1024
SEQ = 2048
CTX = 256
NT = 18
NTOK = NT * 128
DEPTH = 4
EPS = 1e-6
GROUPS = [(0, 512), (512, 1024), (1024, 1536), (1536, 2048), (2048, 2304)]


class Sched:
    def __init__(self, nc):
        self.nc = nc
        self.eng = {'pe': nc.tensor, 'act': nc.scalar, 'dve': nc.vector, 'pool': nc.gpsimd, 'sp': nc.sync}
        self.sem, self.cnt, self.ctx, self.semobj = {}, {}, [], {}
        for e in self.eng:
            cm = nc.semaphore('s_' + e)
            self.sem[e] = cm.__enter__(); self.ctx.append(cm)
            self.cnt[e] = 0
            self.semobj['s_' + e] = self.sem[e]
        self.dsem, self.dcnt = {}, {}
        self.seen = {e: {} for e in self.eng}
        self.lastw, self.readers = {}, {}
        self.rr = {}

    def rot(self, name, n):
        v = self.rr.get(name, 0)
        self.rr[name] = v + 1
        return v % n

    def dma_sem(self, name):
        if name not in self.dsem:
            cm = self.nc.semaphore('d_' + name)
            self.dsem[name] = cm.__enter__(); self.ctx.append(cm)
            self.dcnt[name] = 0
            self.semobj['d_' + name] = self.dsem[name]
        return self.dsem[name]

    def _wait(self, e, tok):
        if tok is None:
            return
        sname, val = tok
        if e == 'pe' and sname == 's_pe':
            return
        if self.seen[e].get(sname, 0) >= val:
            return
        self.eng[e].wait_ge(self.semobj[sname], val)
        self.seen[e][sname] = val

    def _deps(self, e, reads, writes):
        for k in reads:
            self._wait(e, self.lastw.get(k))
        for k in writes:
            self._wait(e, self.lastw.get(k))
            for t in self.readers.get(k, ()):
                self._wait(e, t)

    def _commit(self, tok, reads, writes):
        for k in reads:
            self.readers.setdefault(k, []).append(tok)
        for k in writes:
            self.lastw[k] = tok
            self.readers[k] = []

    def op(self, e, fn, reads=(), writes=(), inc=True):
        self._deps(e, reads, writes)
        inst = fn(self.eng[e])
        if inc:
            self.cnt[e] += 1
            inst.then_inc(self.sem[e], 1)
            tok = ('s_' + e, self.cnt[e])
        else:
            tok = ('s_' + e, self.cnt[e] + 1)
        self._commit(tok, reads, writes)
        return tok

    def dma(self, e, slot, out, in_, reads=(), writes=(), **kw):
        sem = self.dma_sem(slot)
        self._deps(e, reads, writes)
        inst = self.eng[e].dma_start(out=out, in_=in_, **kw)
        self.dcnt[slot] += 16
        inst.then_inc(sem, 16)
        tok = ('d_' + slot, self.dcnt[slot])
        self._commit(tok, reads, writes)
        return tok

    def wait_all(self, e):
        for f in self.eng:
            if self.cnt[f] > 0:
                self._wait(e, ('s_' + f, self.cnt[f]))
        for s in self.dsem:
            if self.dcnt[s] > 0:
                self._wait(e, ('d_' + s, self.dcnt[s]))

    def barrier(self):
        for e in self.eng:
            self.wait_all(e)
        self.lastw, self.readers = {}, {}

    def close(self):
        for cm in reversed(self.ctx):
            cm.__exit__(None, None, None)


def _na_patterns():
    pats, keymap, tilemap = [], {}, {}
    qi = np.arange(128)
    for i in range(16):
        r = 2 * i + qi // 64
        c = qi % 64
        r0 = np.clip(r - 4, 0, 24)
        w0 = np.clip(c - 8, 0, 48)
        lst = []
        for jt in range(16):
            kr = 2 * jt + qi // 64
            kc = qi % 64
            valid = ((kr[:, None] >= r0[None, :]) & (kr[:, None] < r0[None, :] + 8)
                     & (kc[:, None] >= w0[None, :]) & (kc[:, None] < w0[None, :] + 16))
            if not valid.any():
                continue
            ro = kr[:, None] - r[None, :] + 7
            co = kc[:, None] - c[None, :] + 15
            idx = np.where(valid, ro * 31 + co, -1)
            key = idx.tobytes()
            if key not in keymap:
                keymap[key] = len(pats)
                pats.append(idx)
            lst.append((jt, keymap[key]))
        tilemap[i] = lst
    return pats, tilemap


NA_PATS, NA_TILEMAP = _na_patterns()
N_PAT = len(NA_PATS)


def _rope_tables():
    t = np.arange(SEQ)
    row = (t // 64).astype(np.float32)
    col = (t % 64).astype(np.float32)
    inv = (np.float32(10000.0) ** (-np.arange(16, dtype=np.float32) / np.float32(16))).astype(np.float32)
    ang = np.concatenate([row[:, None] * inv, col[:, None] * inv], axis=-1).astype(np.float32)
    cos, sin = np.cos(ang).astype(np.float32), np.sin(ang).astype(np.float32)
    p = np.arange(128)
    d = p % 64
    m = d // 2
    Ct = cos[:, m].T.copy()
    St = (sin[:, m] * np.where(d % 2 == 0, -1.0, 1.0)[None, :]).T.astype(np.float32).copy()
    return Ct, St


class _Stop(Exception):
    pass


def build(n_layers=DEPTH, stop=None):
    nc = bass.Bass("TRN2", target_bir_lowering=False)

    def din(name, shape):
        return nc.dram_tensor(name, list(shape), F32, kind="ExternalInput").ap()

    x_d = din("x", [SEQ, D]); ctx_d = din("ctx", [CTX, D])
    scin_d = din("scin", [128, 16])
    adaw_d = din("ada_w", [DEPTH, D, 6 * D])
    adabT_d = din("ada_bT", [DEPTH, 128, 48]); adabf_d = din("ada_bf", [DEPTH, 1, 6 * D])
    nmix_d = din("nmix", [DEPTH, 128, 8]); nmlp_d = din("nmlp", [DEPTH, 128, 8]); nfin_d = din("nfin", [1, D])
    w1_d = din("w1", [DEPTH, D, 4 * D]); w2_d = din("w2", [DEPTH, 4 * D, D])
    abin_d = din("abin", [2, D, 3104]); about_d = din("about", [2, D, D])
    nab_d = din("nabias", [2, N_PAT, 128, 1024])
    wa2b_d = din("wa2b", [2, 33, 512]); gn_d = din("gn", [2, 1, 512])
    swin_d = din("swin", [2, D, 3328]); swout_d = din("swout", [2, D, D]); sink_d = din("sink", [2, 1, 16])
    ropeC_d = din("ropeC", [128, SEQ]); ropeS_d = din("ropeS", [128, SEQ])
    out_d = nc.dram_tensor("out", [SEQ, D], F32, kind="ExternalOutput").ap()
    xres_d = nc.dram_tensor("xres", [NTOK, D], F32, kind="Internal").ap()

    S = Sched(nc)
    ARENA_W = 52480
    arena_cm = nc.sbuf_tensor("arena", [128, ARENA_W], F32)
    arena = arena_cm.__enter__()
    ps_cm = [nc.psum_tensor(f"ps{i}", [128, 512], F32) for i in range(6)] + \
            [nc.psum_tensor(f"ps{i}", [128, 1024], BF16) for i in (6, 7)]
    PS = [c.__enter__() for c in ps_cm]

    def V(off, shape, dt, parts=128):
        n = int(np.prod(shape[1:]))
        assert off % 4 == 0
        if dt == F32:
            assert off // 4 + n <= ARENA_W, (off, shape)
            a = arena[0:parts, off // 4: off // 4 + n]
        else:
            assert n % 2 == 0 and off // 4 + n // 2 <= ARENA_W, (off, shape)
            a = arena[0:parts, off // 4: off // 4 + n // 2].bitcast(BF16)
        if len(shape) == 3:
            a = a.rearrange("p (a b) -> p a b", a=shape[1])
        elif len(shape) == 4:
            a = a.rearrange("p (a b c) -> p a b c", a=shape[1], b=shape[2])
        return a

    KB = 1024
    R0, R1, R2, R3 = 0, 72 * KB, 108 * KB, 144 * KB
    o = R3
    ident = V(o, [128, 128], BF16); o += 256
    Uf = V(o, [128, 128], F32); o += 512
    Ub = V(o, [128, 128], F32); o += 512
    Rf = V(o, [128, 128], F32); o += 512
    Rb = V(o, [128, 128], F32); o += 512
    mkf = V(o, [128, 128], F32); o += 512
    mkb = V(o, [128, 128], F32); o += 512
    bmp = V(o, [128, 128], BF16); o += 256
    bmn = V(o, [128, 128], BF16); o += 256
    scT = V(o, [128, 8, 2], BF16); o += 32
    scin = V(o, [128, 16], F32); o += 64
    ones_row = V(o, [128, 128], BF16); o += 256
    nmix = V(o, [128, DEPTH, 8], F32); o += 4 * DEPTH * 8
    nmlp = V(o, [128, DEPTH, 8], F32); o += 4 * DEPTH * 8
    adabT = V(o, [128, DEPTH, 48], F32); o += 4 * DEPTH * 48
    modT = V(o, [128, 4, 8, 2], F32); o += 256
    GpM = V(o, [128, 8, 2], F32); o += 64
    GpL = V(o, [128, 8, 2], F32); o += 64
    ssA = V(o, [128, 32], F32); o += 128
    rsA = V(o, [128, 32], F32); o += 128
    gate_bc = V(o, [128, 4, 1024], F32); o += 16 * KB
    R3T = o
    R3END = ARENA_W * 4

    def memset(e, ap, val, key):
        S.op(e, lambda en: en.memset(ap, val), writes=[key])

    def asel(ap, pattern, cm, base, op, key, fill=0.0):
        S.op('pool', lambda en: en.affine_select(out=ap, in_=ap, pattern=pattern, compare_op=op, fill=fill,
                                                 base=base, channel_multiplier=cm), reads=[key], writes=[key])

    memset('pool', ident, 1.0, 'ident')
    asel(ident, [[-1, 128]], 1, 0, ALU.is_equal, 'ident')
    memset('pool', ones_row, 1.0, 'ones_row')
    for (ap, key, pat, cm, base) in [(Uf, 'Uf', [[1, 128]], -1, 0), (Ub, 'Ub', [[-1, 128]], 1, 0),
                                     (Rf, 'Rf', [[-1, 128]], 1, -1), (Rb, 'Rb', [[1, 128]], -1, -1)]:
        memset('pool', ap, -1.0 / 16.0, key)
        asel(ap, pat, cm, base, ALU.is_ge, key)
    memset('pool', mkf, 1.0, 'mkf'); asel(mkf, [[1, 128]], -1, 0, ALU.is_ge, 'mkf')
    memset('pool', mkb, 1.0, 'mkb'); asel(mkb, [[-1, 128]], 1, 0, ALU.is_ge, 'mkb')
    memset('pool', bmp, 0.0, 'bmp'); asel(bmp, [[-1, 128]], 1, 0, ALU.is_ge, 'bmp', fill=-1e30)
    memset('pool', bmn, 0.0, 'bmn'); asel(bmn, [[1, 128]], -1, 0, ALU.is_ge, 'bmn', fill=-1e30)
    S.dma('sp', 'c0', scin, scin_d, writes=['scin'])
    S.dma('sp', 'c1', nmix, nmix_d.rearrange("l p c -> p l c"), writes=['nmix'])
    S.dma('sp', 'c2', nmlp, nmlp_d.rearrange("l p c -> p l c"), writes=['nmlp'])
    S.dma('sp', 'c3', adabT, adabT_d.rearrange("l p c -> p l c"), writes=['adabT'])
    S.op('act', lambda e: e.activation(out=scT.rearrange("p c w -> p w c"), in_=scin.rearrange("p (w c) -> p w c", w=2),
                                       func=AF.Silu), reads=['scin'], writes=['scT'])

    x_res = V(R0, [128, NT, D], F32)

    def tok_rows(i):
        return (x_d[i * 128:(i + 1) * 128, :] if i < 16 else ctx_d[(i - 16) * 128:(i - 15) * 128, :])

    def adaln(l):
        ob = R2
        adab = [V(ob, [128, 8, 1024], BF16), V(ob + 16 * KB, [128, 8, 1024], BF16)]
        ob += 32 * KB
        sc_rep = V(ob, [128, 8, 2, 128], BF16); ob += 4 * KB
        abf = V(R3T, [128, 2048], BF16)
        for kc in range(8):
            for w in range(2):
                S.op('dve', lambda e, kc=kc, w=w: e.tensor_copy(out=sc_rep[:, kc, w, :],
                                                                in_=scT[:, kc, w:w + 1].to_broadcast([128, 128])),
                     reads=['scT'], writes=[('sc_rep', kc, w)])
        S.dma('pool', 'abf0', abf[0:1, 0:1024], adabf_d[l, :, 2 * D:3 * D], writes=['abf0'])
        S.dma('pool', 'abf1', abf[0:1, 1024:2048], adabf_d[l, :, 5 * D:6 * D], writes=['abf1'])
        kind_of = {0: 0, 1: 1, 3: 2, 4: 3}
        for blk in range(6):
            bi = S.rot('adab', 2)
            buf = adab[bi]
            src = adaw_d[l, :, blk * D:(blk + 1) * D].rearrange("(c p) n -> p c n", p=128)
            for hq in range(2):
                S.dma('pool', f'adab{bi}_{hq}', buf[:, hq * 4:(hq + 1) * 4, :], src[:, hq * 4:(hq + 1) * 4, :],
                      writes=[('adab', bi, hq)])
            rk = [('adab', bi, 0), ('adab', bi, 1)]
            if blk in kind_of:
                pb = S.rot('psA', 2)
                ps = PS[pb]
                for j in range(8):
                    for kc in range(8):
                        S.op('pe', lambda e, j=j, kc=kc: e.matmul(ps[:, j * 2:(j + 1) * 2], lhsT=buf[:, kc, j * 128:(j + 1) * 128],
                                                                  rhs=scT[:, kc, :], start=(kc == 0), stop=(kc == 7)),
                             reads=rk + ['scT'], writes=[('ps', pb)])
                S.op('dve', lambda e, blk=blk: e.tensor_tensor(
                    out=modT[:, kind_of[blk], :, :], in0=ps[:, 0:16].rearrange("p (c w) -> p c w", w=2),
                    in1=adabT[:, l, blk * 8:(blk + 1) * 8].unsqueeze(2).to_broadcast([128, 8, 2]), op=ALU.add),
                    reads=[('ps', pb), 'adabT'], writes=[('modT', kind_of[blk])])
            else:
                gi = 0 if blk == 2 else 1
                for w in range(2):
                    for hf in range(2):
                        pb = S.rot('psA', 2)
                        ps = PS[pb]
                        for kc in range(8):
                            S.op('pe', lambda e, kc=kc, w=w, hf=hf: e.matmul(ps[:, :], lhsT=sc_rep[:, kc, w, :],
                                                                           rhs=buf[:, kc, hf * 512:(hf + 1) * 512],
                                                                           start=(kc == 0), stop=False),
                                 reads=rk + [('sc_rep', kc, w)], writes=[('ps', pb)])
                        S.op('pe', lambda e, hf=hf, gi=gi: e.matmul(ps[:, :], lhsT=ones_row[0:1, :],
                                                                  rhs=abf[0:1, gi * 1024 + hf * 512: gi * 1024 + (hf + 1) * 512],
                                                                  start=False, stop=True),
                             reads=['ones_row', 'abf0', 'abf1'], writes=[('ps', pb)])
                        S.op('act', lambda e, w=w, hf=hf, gi=gi: e.activation(out=gate_bc[:, gi * 2 + w, hf * 512:(hf + 1) * 512],
                                                                           in_=ps[:, :], func=AF.Copy),
                             reads=[('ps', pb)], writes=[('gate', gi * 2 + w, hf)])
        for (Gp, nrm, kind, key) in [(GpM, nmix, 1, 'GpM'), (GpL, nmlp, 3, 'GpL')]:
            S.op('dve', lambda e, Gp=Gp, nrm=nrm, kind=kind: e.scalar_tensor_tensor(
                out=Gp[:, :, :], in0=modT[:, kind, :, :], scalar=1.0,
                in1=nrm[:, l, :].unsqueeze(2).to_broadcast([128, 8, 2]), op0=ALU.add, op1=ALU.mult),
                reads=[('modT', kind), 'nmix', 'nmlp'], writes=[key])

    def norm_to_hT(hT, Gp, gkey, shift_kind, tiles, tmp_off):
        junk = V(tmp_off, [128, 1024], F32)
        xn = [V(tmp_off + 4 * KB, [128, 1024], BF16), V(tmp_off + 6 * KB, [128, 1024], BF16)]
        for i in tiles:
            S.op('act', lambda e, i=i: e.activation(out=junk, in_=x_res[:, i, :], func=AF.Square, accum_out=ssA[:, i:i + 1]),
                 reads=[('xres', i)], writes=['junk', ('ssA', i)])
        n = len(tiles)
        t0 = tiles[0]
        S.op('act', lambda e: e.activation(out=rsA[:, t0:t0 + n], in_=ssA[:, t0:t0 + n], func=AF.Sqrt, scale=1.0 / D, bias=EPS),
             reads=[('ssA', i) for i in tiles], writes=['rsA_t'])
        S.op('dve', lambda e: e.reciprocal(out=rsA[:, t0:t0 + n], in_=rsA[:, t0:t0 + n]), reads=['rsA_t'], writes=['rsA'])
        for i in tiles:
            b = S.rot('xn', 2)
            w = 0 if i < 16 else 1
            S.op('dve', lambda e, i=i, b=b: e.tensor_scalar(out=xn[b], in0=x_res[:, i, :], scalar1=rsA[:, i:i + 1], scalar2=None,
                                                          op0=ALU.mult), reads=[('xres', i), 'rsA'], writes=[('xn', b)])
            pb = 6 + S.rot('psT', 2)
            pst = PS[pb]
            for c in range(8):
                S.op('pe', lambda e, c=c, b=b: e.transpose(out=pst[:, c * 128:(c + 1) * 128], in_=xn[b][:, c * 128:(c + 1) * 128],
                                                         identity=ident), reads=[('xn', b), 'ident'], writes=[('ps', pb)])
            use_act = (S.rot('nev', 2) == 0)
            for c in range(8):
                if use_act:
                    S.op('act', lambda e, c=c, i=i, w=w: e.activation(
                        out=hT[:, c, i * 128:(i + 1) * 128], in_=pst[:, c * 128:(c + 1) * 128], func=AF.Identity,
                        scale=Gp[:, c, w:w + 1], bias=modT[:, shift_kind, c, w:w + 1]),
                        reads=[('ps', pb), gkey, ('modT', shift_kind)], writes=[('hT', c, i)])
                else:
                    S.op('dve', lambda e, c=c, i=i, w=w: e.tensor_scalar(
                        out=hT[:, c, i * 128:(i + 1) * 128], in0=pst[:, c * 128:(c + 1) * 128],
                        scalar1=Gp[:, c, w:w + 1], scalar2=modT[:, shift_kind, c, w:w + 1], op0=ALU.mult, op1=ALU.add),
                        reads=[('ps', pb), gkey, ('modT', shift_kind)], writes=[('hT', c, i)])

    def load_w(buf, src, ncols, key):
        sv = src.rearrange("(c p) n -> p c n", p=128)
        for hq in range(2):
            S.dma('pool', key[0] + str(key[1]) + '_' + str(hq), buf[:, hq * 4:(hq + 1) * 4, 0:ncols], sv[:, hq * 4:(hq + 1) * 4, :],
                  writes=[(key, hq)])
        return [(key, 0), (key, 1)]

    def proj_fm(hT, wbuf, wkeys, c0, M, evac, tiles_hi=NTOK):
        for (t0, t1) in GROUPS:
            if t0 >= tiles_hi:
                continue
            pb = S.rot('psP', 4)
            ps = PS[pb]
            for kc in range(8):
                S.op('pe', lambda e, kc=kc: e.matmul(ps[0:M, 0:t1 - t0], lhsT=wbuf[:, kc, c0:c0 + M], rhs=hT[:, kc, t0:t1],
                                                    start=(kc == 0), stop=(kc == 7)),
                     reads=wkeys + [('hT', kc, i) for i in range(t0 // 128, t1 // 128)], writes=[('ps', pb)])
            evac(ps, pb, t0, t1)

    def proj_tm(hT, wbuf, wkeys, c0, n, evac, tiles):
        for i in tiles:
            pb = S.rot('psP', 4)
            ps = PS[pb]
            for kc in range(8):
                S.op('pe', lambda e, kc=kc: e.matmul(ps[:, 0:n], lhsT=hT[:, kc, i * 128:(i + 1) * 128], rhs=wbuf[:, kc, c0:c0 + n],
                                                    start=(kc == 0), stop=(kc == 7)),
                     reads=wkeys + [('hT', kc, i)], writes=[('ps', pb)])
            evac(ps, pb, i)

    def ev_copy(eng, out_ap, in_ap, pb, wkey, scale=None):
        if eng == 'act':
            if scale is None:
                S.op('act', lambda e: e.activation(out=out_ap, in_=in_ap, func=AF.Copy), reads=[('ps', pb)], writes=[wkey])
            else:
                S.op('act', lambda e: e.activation(out=out_ap, in_=in_ap, func=AF.Copy, scale=scale), reads=[('ps', pb)], writes=[wkey])
        else:
            if scale is None:
                S.op(eng, lambda e: e.tensor_copy(out=out_ap, in_=in_ap), reads=[('ps', pb)], writes=[wkey])
            else:
                S.op(eng, lambda e: e.tensor_scalar(out=out_ap, in0=in_ap, scalar1=scale, scalar2=None, op0=ALU.mult),
                     reads=[('ps', pb)], writes=[wkey])

    def alt(name):
        return 'act' if S.rot(name, 2) == 0 else 'dve'

    def transpose_otok(o_tok, OT, tiles):
        for i in tiles:
            pb = 6 + S.rot('psT', 2)
            pst = PS[pb]
            for c in range(8):
                S.op('pe', lambda e, c=c: e.transpose(out=pst[:, c * 128:(c + 1) * 128], in_=o_tok[:, i, c * 128:(c + 1) * 128],
                                                    identity=ident), reads=[('otok', i), 'ident'], writes=[('ps', pb)])
            eng = alt('otev')
            ev_copy(eng, OT[:, :, i * 128:(i + 1) * 128], pst.rearrange("p (c t) -> p c t", c=8), pb, ('OT', i))

    def even_mixer(j, hT, o_tok):
        wb = [V(R3T, [128, 8, 512], BF16), V(R3T + 8 * KB, [128, 8, 512], BF16)]
        qaT = V(R0, [128, 4, NTOK], BF16)
        kaT = V(R0 + 18 * KB, [128, 4, NTOK], BF16)
        va = V(R0 + 36 * KB, [128, NT, 8, 65], BF16)
        S.op('pool', lambda e: e.memset(va[:, :, :, 64:65], 1.0), writes=['va1'])
        for blk in range(3):
            bi = S.rot('wb', 2)
            wk = load_w(wb[bi], abin_d[j, :, blk * 512:(blk + 1) * 512], 512, ('wb', bi))
            if blk < 2:
                dst = qaT if blk == 0 else kaT
                nm = 'qaT' if blk == 0 else 'kaT'
                sc = 0.125 if blk == 0 else None
                for c in range(4):
                    def evac(ps, pb, t0, t1, c=c, dst=dst, nm=nm, sc=sc):
                        ev_copy(alt('ev'), dst[:, c, t0:t1], ps[:, 0:t1 - t0], pb, (nm, c, t0), scale=sc)
                    proj_fm(hT, wb[bi], wk, c * 128, 128, evac)
            else:
                def evac(ps, pb, i):
                    ev_copy(alt('ev'), va[:, i, :, 0:64], ps[:, 0:512].rearrange("p (h d) -> p h d", h=8), pb, ('va', i))
                proj_tm(hT, wb[bi], wk, 0, 512, evac, range(NT))
        bt = [V(R3T + 16 * KB, [128, 5, 8, 128], BF16), V(R3T + 26 * KB, [128, 5, 8, 128], BF16)]
        pT = [V(R0 + 55 * KB, [128, 7, 128], BF16), V(R0 + 55 * KB + 1792, [128, 7, 128], BF16)]
        rec = V(R0 + 59 * KB, [128, 8, 1], F32)
        assert R3T + 36 * KB <= R3END
        PSF = [PS[0], PS[1], PS[2], PS[3], PS[4], PS[5], PS[6][:, :].bitcast(F32), PS[7][:, :].bitcast(F32)]
        rec2 = [rec, V(R0 + 59 * KB + 64, [128, 8, 1], F32)]
        tile_blocks, tile_bb = {}, {}

        def na_scores(i, h):
            if h == 0:
                if i < 16:
                    tile_blocks[i] = [(jt, pat) for (jt, pat) in NA_TILEMAP[i]] + [(16, None), (17, None)]
                    bb = S.rot('bt', 2)
                    tile_bb[i] = bb
                    for bi_, (jt, pat) in enumerate(NA_TILEMAP[i]):
                        S.dma('pool', f'bt{bb}_{bi_}', bt[bb][:, bi_, :, :], nab_d[j, pat, :, :].rearrange("k (h q) -> k h q", h=8),
                              writes=[('bt', bb, bi_)])
                else:
                    tile_blocks[i] = [(16, None), (17, None)]
                    tile_bb[i] = 0
            blocks, bb = tile_blocks[i], tile_bb[i]
            nb = len(blocks)
            p, pbs = h // 2, 64 * (h % 2)
            par = S.rot('nasc', 2)
            banks = [2 * par, 2 * par + 1]
            nw = sum(1 for (_, pat) in blocks if pat is not None)
            nA = min(nw, 4)
            if nA > 0:
                S.op('pe', lambda e: e.matmul(PS[banks[0]][:, 0:nA * 128].rearrange("p (b q) -> p b q", b=nA), lhsT=ident,
                                              rhs=bt[bb][:, 0:nA, h, :], start=True, stop=False),
                     reads=['ident'] + [('bt', bb, x) for x in range(nA)], writes=[('ps', banks[0])])
            for bi_, (jt, pat) in enumerate(blocks):
                bk = banks[bi_ // 4]
                dst = PS[bk][:, (bi_ % 4) * 128:(bi_ % 4 + 1) * 128]
                rk = [('kaT', p, t0) for (t0, t1) in GROUPS if t0 <= jt * 128 < t1] + \
                     [('qaT', p, t0) for (t0, t1) in GROUPS if t0 <= i * 128 < t1]
                inA = pat is not None and bi_ < 4
                if pat is not None and not inA:
                    S.op('pe', lambda e, dst=dst, bi_=bi_: e.matmul(dst, lhsT=ident, rhs=bt[bb][:, bi_, h, :], start=True, stop=False),
                         reads=['ident', ('bt', bb, bi_)], writes=[('ps', bk)])
                S.op('pe', lambda e, dst=dst, jt=jt, pat=pat, inA=inA, bi_=bi_: e.matmul(
                    dst, lhsT=kaT[pbs:pbs + 64, p, jt * 128:(jt + 1) * 128], rhs=qaT[pbs:pbs + 64, p, i * 128:(i + 1) * 128],
                    start=(pat is None), stop=((bi_ == nA - 1) if inA else True)), reads=rk, writes=[('ps', bk)])
            n0 = min(nb, 4)
            S.op('act', lambda e: e.activation(out=pT[par][:, 0:n0, :], in_=PS[2 * par][:, 0:n0 * 128].rearrange(
                "p (b q) -> p b q", b=n0), func=AF.Exp), reads=[('ps', 2 * par)], writes=[('pT', par, 0)])
            if nb > 4:
                n1 = nb - 4
                S.op('act', lambda e: e.activation(out=pT[par][:, 4:4 + n1, :], in_=PS[2 * par + 1][:, 0:n1 * 128].rearrange(
                    "p (b q) -> p b q", b=n1), func=AF.Exp), reads=[('ps', 2 * par + 1)], writes=[('pT', par, 1)])
            return par

        def na_pv(i, h, par):
            blocks = tile_blocks[i]
            nb = len(blocks)
            oset = 4 + 2 * (i % 2)
            ob = oset + h // 4
            od = PSF[ob][:, (h % 4) * 65:(h % 4) * 65 + 65]
            for bi_, (jt, pat) in enumerate(blocks):
                S.op('pe', lambda e, bi_=bi_, jt=jt: e.matmul(od, lhsT=pT[par][:, bi_, :], rhs=va[:, jt, h, :],
                                                            start=(bi_ == 0), stop=(bi_ == nb - 1)),
                     reads=[('pT', par, 0), ('pT', par, 1), ('va', jt), 'va1'], writes=[('ps', ob)])
            if h == 7:
                for hf in range(2):
                    ob2 = oset + hf
                    rc = rec2[i % 2]
                    ov = PSF[ob2][:, 0:260].rearrange("p (h e) -> p h e", e=65)
                    S.op('dve', lambda e, ov=ov, hf=hf, rc=rc: e.reciprocal(out=rc[:, hf * 4:(hf + 1) * 4, :], in_=ov[:, :, 64:65]),
                         reads=[('ps', ob2)], writes=[('rec', i % 2, hf)])
                    S.op('dve', lambda e, ov=ov, hf=hf, rc=rc: e.tensor_tensor(
                        out=o_tok[:, i, hf * 256:(hf + 1) * 256].rearrange("p (h d) -> p h d", h=4), in0=ov[:, :, 0:64],
                        in1=rc[:, hf * 4:(hf + 1) * 4, :].to_broadcast([128, 4, 64]), op=ALU.mult),
                        reads=[('ps', ob2), ('rec', i % 2, hf)], writes=[('otok', i)])

        items = [(i, h) for i in range(NT) for h in range(8)]
        pend = None
        for (i, h) in items:
            par = na_scores(i, h)
            if pend is not None:
                na_pv(*pend)
            pend = (i, h, par)
        na_pv(*pend)
        S.barrier()
        if stop == 'na':
            raise _Stop()
        qbT = V(R0, [128, 2, NTOK], BF16)
        kbT = V(R0 + 9 * KB, [128, 2, NTOK], BF16)
        kbk = V(R0 + 18 * KB, [128, NT, 256], BF16)
        vb = V(R0 + 27 * KB, [128, NT, 512], BF16)
        sgg = V(R0 + 45 * KB, [128, NT, 512], BF16)
        lrT1 = V(R0 + 63 * KB, [128, NTOK], BF16)
        W2b = V(R0 + 68 * KB, [128, 512], BF16)
        gnbc = V(R0 + 69 * KB, [128, 512], F32)
        S.dma('pool', 'w2b', W2b[0:33, :], wa2b_d[j], writes=['W2b'])
        S.dma('sp', 'gn', gnbc, gn_d[j].broadcast_to([128, 512]), writes=['gnbc'])
        S.op('pool', lambda e: e.memset(lrT1[32:33, :], 1.0), writes=['lr1'])
        sgt = [V(R3T + 16 * KB, [128, 512], F32), V(R3T + 18 * KB, [128, 512], F32)]
        bi = S.rot('wb', 2)
        wk = load_w(wb[bi], abin_d[j, :, 1536:2048], 512, ('wb', bi))
        for c in range(2):
            def evq(ps, pb, t0, t1, c=c):
                ev_copy(alt('ev'), qbT[:, c, t0:t1], ps[:, 0:t1 - t0], pb, ('qbT', c, t0), scale=0.125)
            proj_fm(hT, wb[bi], wk, c * 128, 128, evq)
            def evk(ps, pb, t0, t1, c=c):
                ev_copy(alt('ev'), kbT[:, c, t0:t1], ps[:, 0:t1 - t0], pb, ('kbT', c, t0))
            proj_fm(hT, wb[bi], wk, 256 + c * 128, 128, evk)
        def evkk(ps, pb, i):
            ev_copy(alt('ev'), kbk[:, i, :], ps[:, 0:256], pb, ('kbk', i))
        proj_tm(hT, wb[bi], wk, 256, 256, evkk, range(NT))
        bi = S.rot('wb', 2)
        wk = load_w(wb[bi], abin_d[j, :, 2048:2560], 512, ('wb', bi))
        def evv(ps, pb, i):
            ev_copy(alt('ev'), vb[:, i, :], ps[:, 0:512], pb, ('vb', i))
        proj_tm(hT, wb[bi], wk, 0, 512, evv, range(NT))
        bi = S.rot('wb', 2)
        wk = load_w(wb[bi], abin_d[j, :, 2560:3072], 512, ('wb', bi))
        def evg(ps, pb, i):
            b = S.rot('sgt', 2)
            S.op('act', lambda e: e.activation(out=sgt[b], in_=ps[:, 0:512], func=AF.Silu), reads=[('ps', pb)], writes=[('sgt', b)])
            S.op('pool', lambda e: e.tensor_tensor(out=sgg[:, i, :], in0=sgt[b], in1=gnbc, op=ALU.mult),
                 reads=[('sgt', b), 'gnbc'], writes=[('sgg', i)])
        proj_tm(hT, wb[bi], wk, 0, 512, evg, range(NT))
        bi = S.rot('wb', 2)
        wk = load_w(wb[bi], abin_d[j, :, 3072:3104], 32, ('wb', bi))
        def evl(ps, pb, t0, t1):
            ev_copy(alt('ev'), lrT1[0:32, t0:t1], ps[0:32, 0:t1 - t0], pb, ('lrT', t0))
        proj_fm(hT, wb[bi], wk, 0, 32, evl)
        S.barrier()
        if stop == 'glaproj':
            raise _Stop()
        oacc = V(R1, [128, NT, 512], F32)
        t = R3T
        TS = []
        for si in range(2):
            d_ = {}
            d_['E32'] = V(t, [128, 256], F32); t += KB
            d_['L32'] = V(t, [128, 256], F32); t += KB
            d_['eT'] = V(t, [128, 2, 128], F32); t += KB
            d_['enT'] = V(t, [128, 2, 128], F32); t += KB
            d_['krem'] = V(t, [128, 256], F32); t += KB
            d_['qtT'] = V(t, [128, 2, 128], BF16); t += 512
            d_['ktT'] = V(t, [128, 2, 128], BF16); t += 512
            d_['kend'] = V(t, [128, 256], BF16); t += 512
            d_['Abf'] = V(t, [128, 4, 128], BF16); t += KB
            TS.append(d_)
        Sst = [V(t, [128, 2, 128], F32), V(t + KB, [128, 2, 128], F32)]; t += 2 * KB
        Sbf = [V(t, [128, 2, 128], BF16), V(t + 512, [128, 2, 128], BF16)]; t += KB
        sq = V(t, [128, 512], F32); t += 2 * KB
        t1b = V(t, [128, 512], F32); t += 2 * KB
        ss4 = V(t, [128, 4], F32); t += 16
        rs4 = V(t, [128, 4], F32); t += 16
        assert t <= R3END
        PA = [PS[3], PS[6][:, :].bitcast(F32)]
        PO = [PS[4], PS[7][:, :].bitcast(F32)]
        pak, pok = [3, 6], [4, 7]
        for d in range(2):
            S.op('pool', lambda e, d=d: e.memset(Sst[d], 0.0), writes=[('Sst', d, 0), ('Sst', d, 1)])
            S.op('pool', lambda e, d=d: e.memset(Sbf[d], 0.0), writes=[('Sbf', d, 0), ('Sbf', d, 1)])
        orders = [[16, 17] + list(range(16)), [17, 16] + list(range(15, -1, -1))]
        visited = set()

        def gla_front(d, ti, si):
            T_ = TS[si]
            E32, L32, eT, enT, krem, qtT, ktT, kend, Abf = (T_[k] for k in ('E32', 'L32', 'eT', 'enT', 'krem', 'qtT', 'ktT', 'kend', 'Abf'))
            Ud, Rd, mk = (Uf, Rf, mkf) if d == 0 else (Ub, Rb, mkb)
            Uk, Rk, mkk = ('Uf', 'Rf', 'mkf') if d == 0 else ('Ub', 'Rb', 'mkb')
            tc0 = ti * 128
            tg = [t0 for (t0, t1_) in GROUPS if t0 <= tc0 < t1_][0]
            K = lambda n: (n, si)
            S.op('pe', lambda e: e.matmul(PS[0][:, 0:256], lhsT=lrT1[0:33, tc0:tc0 + 128], rhs=W2b[0:33, d * 256:(d + 1) * 256],
                                          start=True, stop=True), reads=[('lrT', tg), 'lr1', 'W2b'], writes=[('ps', 0)])
            S.op('act', lambda e: e.activation(out=E32, in_=PS[0][:, 0:256], func=AF.Exp, scale=-1.0), reads=[('ps', 0)], writes=[K('E32')])
            S.op('act', lambda e: e.activation(out=L32, in_=E32, func=AF.Ln, bias=1.0), reads=[K('E32')], writes=[K('L32')])
            for p in range(2):
                S.op('pe', lambda e, p=p: e.matmul(PS[1][:, p * 128:(p + 1) * 128], lhsT=L32[:, p * 128:(p + 1) * 128], rhs=Ud,
                                                  start=True, stop=True), reads=[K('L32'), Uk], writes=[('ps', 1)])
            S.op('pe', lambda e: e.matmul(PS[2][:, 0:256], lhsT=Rd, rhs=L32, start=True, stop=True), reads=[K('L32'), Rk], writes=[('ps', 2)])
            S.op('act', lambda e: e.activation(out=eT, in_=PS[1][:, 0:256].rearrange("p (a b) -> p a b", a=2), func=AF.Exp),
                 reads=[('ps', 1)], writes=[K('eT')])
            S.op('act', lambda e: e.activation(out=enT, in_=PS[1][:, 0:256].rearrange("p (a b) -> p a b", a=2), func=AF.Exp, scale=-1.0),
                 reads=[('ps', 1)], writes=[K('enT')])
            S.op('act', lambda e: e.activation(out=krem, in_=PS[2][:, 0:256], func=AF.Exp), reads=[('ps', 2)], writes=[K('krem')])
            S.op('dve', lambda e: e.tensor_tensor(out=qtT, in0=qbT[:, :, tc0:tc0 + 128], in1=eT, op=ALU.mult),
                 reads=[('qbT', 0, tg), ('qbT', 1, tg), K('eT')], writes=[K('qtT')])
            S.op('pool', lambda e: e.tensor_tensor(out=ktT, in0=kbT[:, :, tc0:tc0 + 128], in1=enT, op=ALU.mult),
                 reads=[('kbT', 0, tg), ('kbT', 1, tg), K('enT')], writes=[K('ktT')])
            S.op('pool', lambda e: e.tensor_tensor(out=kend, in0=kbk[:, ti, :], in1=krem, op=ALU.mult),
                 reads=[('kbk', ti), K('krem')], writes=[K('kend')])
            for h in range(4):
                p, pbs = h // 2, 64 * (h % 2)
                S.op('pe', lambda e, h=h, p=p, pbs=pbs: e.matmul(PA[h % 2][:, p * 128:(p + 1) * 128], lhsT=ktT[pbs:pbs + 64, p, :],
                                                              rhs=qtT[pbs:pbs + 64, p, :], start=True, stop=True),
                     reads=[K('ktT'), K('qtT')], writes=[('ps', pak[h % 2])])
            for h in range(4):
                S.op('dve', lambda e, h=h: e.tensor_tensor(out=Abf[:, h, :], in0=PA[h % 2][:, (h // 2) * 128:(h // 2 + 1) * 128],
                                                           in1=mk, op=ALU.mult),
                     reads=[('ps', pak[h % 2]), mkk], writes=[K('Abf')])

        def gla_back(d, ti, si):
            T_ = TS[si]
            eT, qtT, kend, Abf = T_['eT'], T_['qtT'], T_['kend'], T_['Abf']
            K = lambda n: (n, si)
            last = 127 if d == 0 else 0
            for h in range(4):
                p, pbs = h // 2, 64 * (h % 2)
                S.op('pe', lambda e, h=h, p=p: e.matmul(PO[h % 2][:, p * 128:(p + 1) * 128], lhsT=Abf[:, h, :], rhs=vb[:, ti, h * 128:(h + 1) * 128],
                                                       start=True, stop=False), reads=[K('Abf'), ('vb', ti)], writes=[('ps', pok[h % 2])])
                S.op('pe', lambda e, h=h, p=p, pbs=pbs: e.matmul(PO[h % 2][:, p * 128:(p + 1) * 128], lhsT=qtT[pbs:pbs + 64, p, :],
                                                              rhs=Sbf[d][pbs:pbs + 64, p, :], start=False, stop=True),
                     reads=[K('qtT'), ('Sbf', d, 0), ('Sbf', d, 1)], writes=[('ps', pok[h % 2])])
            first = ti not in visited
            visited.add(ti)
            for par in range(2):
                ov_ = oacc[:, ti, :].rearrange("p (a b v) -> p a b v", a=2, b=2)[:, :, par, :]
                pv_ = PO[par][:, 0:256].rearrange("p (a v) -> p a v", a=2)
                if first:
                    S.op('act', lambda e, ov_=ov_, pv_=pv_: e.activation(out=ov_, in_=pv_, func=AF.Copy),
                         reads=[('ps', pok[par])], writes=[('oacc', ti, par)])
                else:
                    S.op('dve', lambda e, ov_=ov_, pv_=pv_: e.tensor_tensor(out=ov_, in0=pv_, in1=ov_, op=ALU.add),
                         reads=[('ps', pok[par]), ('oacc', ti, par)], writes=[('oacc', ti, par)])
            for p in range(2):
                S.op('pe', lambda e, p=p: e.matmul(PS[5][:, p * 256:(p + 1) * 256], lhsT=kend[:, p * 128:(p + 1) * 128],
                                                  rhs=vb[:, ti, p * 256:(p + 1) * 256], start=True, stop=True),
                     reads=[K('kend'), ('vb', ti)], writes=[('ps', 5)])
            for p in range(2):
                for hp in range(2):
                    r0 = hp * 64
                    S.op('dve', lambda e, p=p, hp=hp, r0=r0: e.scalar_tensor_tensor(
                        out=Sst[d][r0:r0 + 64, p, :], in0=Sst[d][r0:r0 + 64, p, :], scalar=eT[r0:r0 + 64, p, last:last + 1],
                        in1=PS[5][r0:r0 + 64, p * 256 + hp * 128: p * 256 + (hp + 1) * 128], op0=ALU.mult, op1=ALU.add),
                        reads=[('ps', 5), K('eT'), ('Sst', d, hp)], writes=[('Sst', d, hp)])
            for hp in range(2):
                r0 = hp * 64
                S.op('act', lambda e, r0=r0: e.activation(out=Sbf[d][r0:r0 + 64, :, :], in_=Sst[d][r0:r0 + 64, :, :], func=AF.Copy),
                     reads=[('Sst', d, hp)], writes=[('Sbf', d, hp)])
            if not first:
                S.op('dve', lambda e: e.tensor_tensor(out=sq, in0=oacc[:, ti, :], in1=oacc[:, ti, :], op=ALU.mult),
                     reads=[('oacc', ti, 0), ('oacc', ti, 1)], writes=['sq'])
                S.op('dve', lambda e: e.tensor_reduce(out=ss4, in_=sq.rearrange("p (h v) -> p h v", h=4), axis=AX.X, op=ALU.add),
                     reads=['sq'], writes=['ss4'])
                S.op('act', lambda e: e.activation(out=rs4, in_=ss4, func=AF.Ln, scale=1.0 / 128.0, bias=EPS), reads=['ss4'], writes=['rs4t'])
                S.op('act', lambda e: e.activation(out=rs4, in_=rs4, func=AF.Exp, scale=-0.5), reads=['rs4t'], writes=['rs4'])
                S.op('dve', lambda e: e.tensor_tensor(out=t1b.rearrange("p (h v) -> p h v", h=4),
                                                      in0=oacc[:, ti, :].rearrange("p (h v) -> p h v", h=4),
                                                      in1=rs4.unsqueeze(2).to_broadcast([128, 4, 128]), op=ALU.mult),
                     reads=[('oacc', ti, 0), ('oacc', ti, 1), 'rs4'], writes=['t1b'])
                S.op('pool', lambda e: e.tensor_tensor(out=o_tok[:, ti, 512:1024], in0=t1b, in1=sgg[:, ti, :], op=ALU.mult),
                     reads=['t1b', ('sgg', ti)], writes=[('otok', ti)])

        gitems = []
        for k in range(NT):
            gitems.append((0, orders[0][k]))
            gitems.append((1, orders[1][k]))
        pend = None
        for n_, (d, ti) in enumerate(gitems):
            gla_front(d, ti, n_ % 2)
            if pend is not None:
                gla_back(*pend)
            pend = (d, ti, n_ % 2)
        gla_back(*pend)
        S.barrier()

    def odd_mixer(j, hT, o_tok, ctx_out):
        wb = [V(R3T, [128, 8, 512], BF16), V(R3T + 8 * KB, [128, 8, 512], BF16)]
        ropeC = V(R3T + 16 * KB, [128, SEQ], F32)
        ropeS = V(R3T + 24 * KB, [128, SEQ], F32)
        pT = V(R0 + 66 * KB, [128, 5, 4, 128], BF16)
        es = V(R3T + 32 * KB, [128, 16], F32)
        den = V(R3T + 32 * KB + 64, [128, 4, 1], F32)
        tq = [V(R0 + 62 * KB, [128, 512], F32), V(R0 + 64 * KB, [128, 512], F32)]
        assert R3T + 33 * KB <= R3END
        S.dma('sp', 'rc', ropeC, ropeC_d, writes=['ropeC'])
        S.dma('sp', 'rs', ropeS, ropeS_d, writes=['ropeS'])
        S.dma('sp', 'sk', es, sink_d[j].broadcast_to([128, 16]), writes=['es_raw'])
        S.op('act', lambda e: e.activation(out=es, in_=es, func=AF.Exp), reads=['es_raw'], writes=['es'])
        kT2 = V(R0, [128, 4, NTOK], BF16)
        krT2 = V(R0 + 18 * KB, [128, 4, SEQ], BF16)
        v65 = V(R0 + 34 * KB, [128, NT, 4, 65], BF16)
        qT = V(R0 + 44 * KB, [128, 2, NTOK], BF16)
        qrT = V(R0 + 53 * KB, [128, 2, SEQ], BF16)
        S.op('pool', lambda e: e.memset(v65[:, :, :, 64:65], 1.0), writes=['v1'])
        ntl = NT if ctx_out else 16

        def rope_evac(ps_a, pb_a, ps_b, pb_b, t0, t1, dst, key):
            n = t1 - t0
            b = S.rot('tq', 2)
            S.op('dve', lambda e: e.tensor_tensor(out=tq[b][:, 0:n], in0=ps_a[:, 0:n], in1=ropeC[:, t0:t1], op=ALU.mult),
                 reads=[('ps', pb_a), 'ropeC'], writes=[('tq', b)])
            b2 = S.rot('tq', 2)
            S.op('dve', lambda e: e.tensor_tensor(out=tq[b2][:, 0:n], in0=ps_b[:, 0:n], in1=ropeS[:, t0:t1], op=ALU.mult),
                 reads=[('ps', pb_b), 'ropeS'], writes=[('tq', b2)])
            S.op('pool', lambda e: e.tensor_tensor(out=dst, in0=tq[b][:, 0:n], in1=tq[b2][:, 0:n], op=ALU.add),
                 reads=[('tq', b), ('tq', b2)], writes=[key])

        def proj_pair(wbuf, wk, ca, cb, scale, dst_plain, dst_rope, nm, c):
            for (t0, t1) in GROUPS:
                n = t1 - t0
                pa = S.rot('psP', 4); psa = PS[pa]
                rk = wk + [('hT', kc, i) for kc in range(8) for i in range(t0 // 128, t1 // 128)]
                for kc in range(8):
                    S.op('pe', lambda e, kc=kc: e.matmul(psa[:, 0:n], lhsT=wbuf[:, kc, ca:ca + 128], rhs=hT[:, kc, t0:t1],
                                                        start=(kc == 0), stop=(kc == 7)), reads=rk, writes=[('ps', pa)])
                ev_copy('act', dst_plain[:, c, t0:t1], psa[:, 0:n], pa, (nm, c, t0), scale=scale)
                if t0 < SEQ:
                    pb2 = S.rot('psP', 4); psb = PS[pb2]
                    for kc in range(8):
                        S.op('pe', lambda e, kc=kc: e.matmul(psb[:, 0:n], lhsT=wbuf[:, kc, cb:cb + 128], rhs=hT[:, kc, t0:t1],
                                                            start=(kc == 0), stop=(kc == 7)), reads=rk, writes=[('ps', pb2)])
                    rope_evac(psa, pa, psb, pb2, t0, t1, dst_rope[:, c, t0:t1], (nm + 'r', c, t0))

        bi = S.rot('wb', 2); wk = load_w(wb[bi], swin_d[j, :, 2048:2560], 512, ('wb', bi))
        bi2 = S.rot('wb', 2); wk2 = load_w(wb[bi2], swin_d[j, :, 2560:3072], 512, ('wb', bi2))
        for g in range(4):
            for (t0, t1) in GROUPS:
                n = t1 - t0
                pa = S.rot('psP', 4); psa = PS[pa]
                rk = [('hT', kc, i) for kc in range(8) for i in range(t0 // 128, t1 // 128)]
                for kc in range(8):
                    S.op('pe', lambda e, kc=kc: e.matmul(psa[:, 0:n], lhsT=wb[bi][:, kc, g * 128:(g + 1) * 128], rhs=hT[:, kc, t0:t1],
                                                        start=(kc == 0), stop=(kc == 7)), reads=wk + rk, writes=[('ps', pa)])
                ev_copy('dve', kT2[:, g, t0:t1], psa[:, 0:n], pa, ('kT2', g, t0))
                if t0 < SEQ:
                    pb2 = S.rot('psP', 4); psb = PS[pb2]
                    for kc in range(8):
                        S.op('pe', lambda e, kc=kc: e.matmul(psb[:, 0:n], lhsT=wb[bi2][:, kc, g * 128:(g + 1) * 128], rhs=hT[:, kc, t0:t1],
                                                            start=(kc == 0), stop=(kc == 7)), reads=wk2 + rk, writes=[('ps', pb2)])
                    rope_evac(psa, pa, psb, pb2, t0, t1, krT2[:, g, t0:t1], ('krT2', g, t0))
        bi = S.rot('wb', 2); wk = load_w(wb[bi], swin_d[j, :, 3072:3328], 256, ('wb', bi))
        def evv(ps, pb, i):
            ev_copy(alt('ev'), v65[:, i, :, 0:64], ps[:, 0:256].rearrange("p (h d) -> p h d", h=4), pb, ('v65', i))
        proj_tm(hT, wb[bi], wk, 0, 256, evv, range(NT))

        for g in range(4):
            bi = S.rot('wb', 2); wk = load_w(wb[bi], swin_d[j, :, g * 256:(g + 1) * 256], 256, ('wb', bi))
            bi2 = S.rot('wb', 2); wk2 = load_w(wb[bi2], swin_d[j, :, 1024 + g * 256:1024 + (g + 1) * 256], 256, ('wb', bi2))
            for c in range(2):
                for (t0, t1) in GROUPS:
                    n = t1 - t0
                    pa = S.rot('psP', 4); psa = PS[pa]
                    rk = [('hT', kc, i) for kc in range(8) for i in range(t0 // 128, t1 // 128)]
                    for kc in range(8):
                        S.op('pe', lambda e, kc=kc: e.matmul(psa[:, 0:n], lhsT=wb[bi][:, kc, c * 128:(c + 1) * 128], rhs=hT[:, kc, t0:t1],
                                                            start=(kc == 0), stop=(kc == 7)), reads=wk + rk, writes=[('ps', pa)])
                    ev_copy('dve', qT[:, c, t0:t1], psa[:, 0:n], pa, ('qT', c, t0), scale=0.125)
                    if t0 < SEQ:
                        pb2 = S.rot('psP', 4); psb = PS[pb2]
                        for kc in range(8):
                            S.op('pe', lambda e, kc=kc: e.matmul(psb[:, 0:n], lhsT=wb[bi2][:, kc, c * 128:(c + 1) * 128], rhs=hT[:, kc, t0:t1],
                                                                start=(kc == 0), stop=(kc == 7)), reads=wk2 + rk, writes=[('ps', pb2)])
                        n_ = n
                        b = S.rot('tq', 2)
                        S.op('dve', lambda e, b=b: e.scalar_tensor_tensor(out=tq[b][:, 0:n_], in0=psa[:, 0:n_], scalar=0.125,
                                                                          in1=ropeC[:, t0:t1], op0=ALU.mult, op1=ALU.mult),
                             reads=[('ps', pa), 'ropeC'], writes=[('tq', b)])
                        b2 = S.rot('tq', 2)
                        S.op('dve', lambda e, b2=b2: e.scalar_tensor_tensor(out=tq[b2][:, 0:n_], in0=psb[:, 0:n_], scalar=0.125,
                                                                            in1=ropeS[:, t0:t1], op0=ALU.mult, op1=ALU.mult),
                             reads=[('ps', pb2), 'ropeS'], writes=[('tq', b2)])
                        S.op('pool', lambda e, b=b, b2=b2: e.tensor_tensor(out=qrT[:, c, t0:t1], in0=tq[b][:, 0:n_], in1=tq[b2][:, 0:n_], op=ALU.add),
                             reads=[('tq', b), ('tq', b2)], writes=[('qrT', c, t0)])
            pT2 = [pT, V(R3T + 33 * KB, [128, 5, 4, 128], BF16)]
            den2 = [den, V(R3T + 32 * KB + 128, [128, 4, 1], F32)]

            def sw_blocks(i):
                if i < 16:
                    blocks = []
                    if i > 0:
                        blocks.append((i - 1, 'p'))
                    blocks.append((i, 'l'))
                    if i < 15:
                        blocks.append((i + 1, 'n'))
                    return blocks + [(16, 'c'), (17, 'c')]
                return [(16, 'c'), (17, 'c')]

            def sw_scores(i, pi):
                blocks = sw_blocks(i)
                pTc = pT2[pi]
                tgq = [t0 for (t0, t1_) in GROUPS if t0 <= i * 128 < t1_][0]
                for bi_, (jt, kind) in enumerate(blocks):
                    tgk = [t0 for (t0, t1_) in GROUPS if t0 <= jt * 128 < t1_][0]
                    roped = kind in ('p', 'l', 'n')
                    ksrc = krT2 if roped else kT2
                    qsrc = qrT if roped else qT
                    kkey = ('krT2', g, tgk) if roped else ('kT2', g, tgk)
                    qn = 'qrT' if roped else 'qT'
                    par = S.rot('swsc', 2)
                    for half in range(2):
                        bk = 2 * par + half
                        r0 = half * 64
                        first = True
                        if kind in ('p', 'n'):
                            bm = bmp if kind == 'p' else bmn
                            S.op('pe', lambda e, bk=bk, bm=bm: e.matmul(PS[bk][:, 0:256].rearrange("p (h q) -> p h q", h=2), lhsT=ident,
                                                                      rhs=bm.unsqueeze(1).to_broadcast([128, 2, 128]), start=True, stop=False),
                                 reads=['ident', 'bmp', 'bmn'], writes=[('ps', bk)])
                            first = False
                        S.op('pe', lambda e, bk=bk, r0=r0, jt=jt, ksrc=ksrc, qsrc=qsrc, first=first: e.matmul(
                            PS[bk][:, 0:256].rearrange("p (c q) -> p c q", c=2), lhsT=ksrc[r0:r0 + 64, g, jt * 128:(jt + 1) * 128],
                            rhs=qsrc[r0:r0 + 64, :, i * 128:(i + 1) * 128], start=first, stop=True),
                            reads=[kkey, (qn, 0, tgq), (qn, 1, tgq)], writes=[('ps', bk)])
                        S.op('act', lambda e, bk=bk, bi_=bi_, half=half: e.activation(
                            out=pTc[:, bi_, :, :].rearrange("p (c f) q -> p c f q", c=2)[:, :, half, :],
                            in_=PS[bk][:, 0:256].rearrange("p (c q) -> p c q", c=2), func=AF.Exp),
                            reads=[('ps', bk)], writes=[('pT', pi, bi_, half)])

            def sw_pv(i, pi):
                blocks = sw_blocks(i)
                nb = len(blocks)
                pTc = pT2[pi]
                dn = den2[pi]
                ob = 4 + pi
                for hh in range(4):
                    od = PS[ob][:, hh * 65:(hh + 1) * 65]
                    for bi_, (jt, kind) in enumerate(blocks):
                        S.op('pe', lambda e, od=od, bi_=bi_, jt=jt, hh=hh: e.matmul(od, lhsT=pTc[:, bi_, hh, :], rhs=v65[:, jt, g, :],
                                                                                 start=(bi_ == 0), stop=(bi_ == nb - 1)),
                             reads=[('pT', pi, bi_, 0), ('pT', pi, bi_, 1), ('v65', jt), 'v1'], writes=[('ps', ob)])
                ov = PS[ob][:, 0:260].rearrange("p (h e) -> p h e", e=65)
                S.op('dve', lambda e: e.tensor_tensor(out=dn, in0=ov[:, :, 64:65], in1=es[:, g * 4:(g + 1) * 4].unsqueeze(2), op=ALU.add),
                     reads=[('ps', ob), 'es'], writes=[('den_t', pi)])
                S.op('dve', lambda e: e.reciprocal(out=dn, in_=dn), reads=[('den_t', pi)], writes=[('den', pi)])
                S.op('dve', lambda e: e.tensor_tensor(
                    out=o_tok[:, i, g * 256:(g + 1) * 256].rearrange("p (h d) -> p h d", h=4), in0=ov[:, :, 0:64],
                    in1=dn.to_broadcast([128, 4, 64]), op=ALU.mult), reads=[('ps', ob), ('den', pi)], writes=[('otok', i)])

            pend = None
            for n_, i in enumerate(range(ntl)):
                sw_scores(i, n_ % 2)
                if pend is not None:
                    sw_pv(*pend)
                pend = (i, n_ % 2)
            sw_pv(*pend)
        S.barrier()

    def out_proj(wsrc, OT, l, first_layer, tiles):
        wout = V(R3T, [128, 8, 1024], BF16)
        tmp = [V(R3T + 16 * KB, [128, 1024], F32), V(R3T + 20 * KB, [128, 1024], F32)]
        wk = []
        sv = wsrc.rearrange("(c p) n -> p c n", p=128)
        for hq in range(4):
            S.dma('pool', f'wout{hq}', wout[:, hq * 2:(hq + 1) * 2, :], sv[:, hq * 2:(hq + 1) * 2, :], writes=[('wout', hq)])
            wk.append(('wout', hq))
        for i in tiles:
            src = tok_rows(i) if first_layer else xres_d[i * 128:(i + 1) * 128, :]
            S.dma('sp', f'xld{i}', x_res[:, i, :], src, writes=[('xres', i)])
        for i in tiles:
            w = 0 if i < 16 else 1
            b = S.rot('tmp', 2)
            PY = [PS[4], PS[5], PS[6][:, :].bitcast(F32), PS[7][:, :].bitcast(F32)]
            ys = S.rot('psY', 2)
            for hf in range(2):
                pb = 4 + 2 * ys + hf
                for kc in range(8):
                    S.op('pe', lambda e, kc=kc, hf=hf, pb=pb: e.matmul(PY[pb - 4][:, :], lhsT=OT[:, kc, i * 128:(i + 1) * 128],
                                                                     rhs=wout[:, kc, hf * 512:(hf + 1) * 512], start=(kc == 0), stop=(kc == 7)),
                         reads=wk + [('OT', i)], writes=[('ps', pb)])
                S.op('dve', lambda e, hf=hf, pb=pb, w=w, b=b: e.tensor_tensor(out=tmp[b][:, hf * 512:(hf + 1) * 512], in0=PY[pb - 4][:, :],
                                                                             in1=gate_bc[:, w, hf * 512:(hf + 1) * 512], op=ALU.mult),
                     reads=[('ps', pb), ('gate', w, hf)], writes=[('tmp', b, hf)])
            S.op('pool', lambda e, b=b: e.tensor_tensor(out=x_res[:, i, :], in0=x_res[:, i, :], in1=tmp[b], op=ALU.add),
                 reads=[('tmp', b, 0), ('tmp', b, 1), ('xres', i)], writes=[('xres', i)])

    def mlp(l, hT, tiles):
        uT = V(R1, [128, 4, NTOK], BF16)
        tmp = [V(R1 + 18 * KB, [128, 1024], F32), V(R1 + 22 * KB, [128, 1024], F32)]
        rl = [V(R1 + 26 * KB, [128, 512], F32), V(R1 + 28 * KB, [128, 512], F32)]
        W1 = [V(R3T, [128, 8, 512], BF16), V(R3T + 8 * KB, [128, 8, 512], BF16)]
        W2 = [V(R3T + 16 * KB, [128, 4, 1024], BF16), V(R3T + 24 * KB, [128, 4, 1024], BF16)]
        assert R3T + 32 * KB <= R3END
        thi = (max(tiles) + 1) * 128
        def issue_w(blk):
            bi = blk % 2
            w1v = w1_d[l, :, blk * 512:(blk + 1) * 512].rearrange("(c p) n -> p c n", p=128)
            w2v = w2_d[l, blk * 512:(blk + 1) * 512, :].rearrange("(c p) n -> p c n", p=128)
            k1, k2 = [], []
            for hq in range(2):
                S.dma('pool', f'w1_{bi}_{hq}', W1[bi][:, hq * 4:(hq + 1) * 4, :], w1v[:, hq * 4:(hq + 1) * 4, :], writes=[('W1', bi, hq)])
                k1.append(('W1', bi, hq))
            for hq in range(2):
                S.dma('pool', f'w2_{bi}_{hq}', W2[bi][:, hq * 2:(hq + 1) * 2, :], w2v[:, hq * 2:(hq + 1) * 2, :], writes=[('W2', bi, hq)])
                k2.append(('W2', bi, hq))
            return bi, k1, k2

        nxt = issue_w(0)
        for blk in range(8):
            bi, k1, k2 = nxt
            if blk + 1 < 8:
                nxt = issue_w(blk + 1)
            PY = [PS[4], PS[5], PS[6][:, :].bitcast(F32), PS[7][:, :].bitcast(F32)]

            def u_phase(t0, t1):
                n = t1 - t0
                for fc in range(4):
                    pb = S.rot('psU', 4)
                    for kc in range(8):
                        S.op('pe', lambda e, kc=kc, fc=fc, pb=pb: e.matmul(PS[pb][:, 0:n], lhsT=W1[bi][:, kc, fc * 128:(fc + 1) * 128],
                                                                         rhs=hT[:, kc, t0:t1], start=(kc == 0), stop=(kc == 7)),
                             reads=k1 + [('hT', kc, i) for i in range(t0 // 128, t1 // 128)], writes=[('ps', pb)])
                    rb = S.rot('rl', 2)
                    S.op('act', lambda e, pb=pb, rb=rb: e.activation(out=rl[rb][:, 0:n], in_=PS[pb][:, 0:n], func=AF.Relu),
                         reads=[('ps', pb)], writes=[('rl', rb)])
                    S.op('dve', lambda e, rb=rb, fc=fc: e.tensor_tensor(out=uT[:, fc, t0:t1], in0=rl[rb][:, 0:n], in1=rl[rb][:, 0:n], op=ALU.mult),
                         reads=[('rl', rb)], writes=[('uT', fc, t0)])

            def y_phase(t0, t1):
                for i in range(t0 // 128, t1 // 128):
                    if i not in tiles:
                        continue
                    w = 2 + (0 if i < 16 else 1)
                    b = S.rot('tmp', 2)
                    ys = S.rot('psY', 2)
                    for hf in range(2):
                        pb = 4 + 2 * ys + hf
                        for fc in range(4):
                            S.op('pe', lambda e, fc=fc, hf=hf, pb=pb: e.matmul(PY[pb - 4][:, :], lhsT=uT[:, fc, i * 128:(i + 1) * 128],
                                                                             rhs=W2[bi][:, fc, hf * 512:(hf + 1) * 512], start=(fc == 0), stop=(fc == 3)),
                                 reads=k2 + [('uT', fc, t0)], writes=[('ps', pb)])
                        S.op('dve', lambda e, hf=hf, pb=pb, w=w, b=b: e.tensor_tensor(out=tmp[b][:, hf * 512:(hf + 1) * 512], in0=PY[pb - 4][:, :],
                                                                                     in1=gate_bc[:, w, hf * 512:(hf + 1) * 512], op=ALU.mult),
                             reads=[('ps', pb), ('gate', w, hf)], writes=[('tmp', b, hf)])
                    S.op('pool', lambda e, b=b, i=i: e.tensor_tensor(out=x_res[:, i, :], in0=x_res[:, i, :], in1=tmp[b], op=ALU.add),
                         reads=[('tmp', b, 0), ('tmp', b, 1), ('xres', i)], writes=[('xres', i)])

            grp = [(t0, t1) for (t0, t1) in GROUPS if t0 < thi]
            prev = None
            for gidx, (t0, t1) in enumerate(grp):
                u_phase(t0, t1)
                if prev is not None:
                    y_phase(*prev)
                prev = (t0, t1)
            y_phase(*prev)

    hT_A = V(R1, [128, 8, NTOK], BF16)
    hT_B = V(R2, [128, 8, NTOK], BF16)
    o_tok = V(R2, [128, NT, D], BF16)
    OT = V(R1, [128, 8, NTOK], BF16)

    for i in range(NT):
        S.dma('sp', f'xld{i}', x_res[:, i, :], tok_rows(i), writes=[('xres', i)])
    import os
    STOP_L = int(os.environ.get('STOP_L', '0'))
    cur_l = [0]

    def chk(name):
        if stop == name and cur_l[0] == STOP_L:
            S.barrier()
            raise _Stop()

    try:
      for l in range(n_layers):
        ctx_out = l < n_layers - 1
        cur_l[0] = l
        adaln(l)
        S.barrier()
        chk('adaln')
        norm_to_hT(hT_A, GpM, 'GpM', 0, list(range(NT)), R2)
        if l > 0:
            pass
        if l > 0:
            for i in range(NT):
                S.dma('sp', f'xst{i}', xres_d[i * 128:(i + 1) * 128, :], x_res[:, i, :], reads=[('xres', i)])
        S.barrier()
        chk('normA')
        if l % 2 == 0:
            even_mixer(l // 2, hT_A, o_tok)
            tiles = list(range(NT))
        else:
            odd_mixer(l // 2, hT_A, o_tok, ctx_out)
            tiles = list(range(NT if ctx_out else 16))
        chk('mixer')
        transpose_otok(o_tok, OT, tiles)
        S.barrier()
        chk('tr')
        out_proj(about_d[l // 2] if l % 2 == 0 else swout_d[l // 2], OT, l, l == 0, tiles)
        S.barrier()
        norm_to_hT(hT_B, GpL, 'GpL', 2, tiles, R1)
        S.barrier()
        chk('norm2')
        mlp(l, hT_B, tiles)
        S.barrier()
    except _Stop:
        pass
    nf = V(R3T, [128, D], F32)
    junk = V(R1, [128, D], F32)
    yo = [V(R1 + 4 * KB, [128, D], F32), V(R1 + 8 * KB, [128, D], F32)]
    S.dma('sp', 'nf', nf, nfin_d.broadcast_to([128, D]), writes=['nf'])
    for i in range(16):
        S.op('act', lambda e, i=i: e.activation(out=junk, in_=x_res[:, i, :], func=AF.Square, accum_out=ssA[:, i:i + 1]),
             reads=[('xres', i)], writes=['junk', ('ssA', i)])
    S.op('act', lambda e: e.activation(out=rsA[:, 0:16], in_=ssA[:, 0:16], func=AF.Sqrt, scale=1.0 / D, bias=EPS),
         reads=[('ssA', i) for i in range(16)], writes=['rsA_t'])
    S.op('dve', lambda e: e.reciprocal(out=rsA[:, 0:16], in_=rsA[:, 0:16]), reads=['rsA_t'], writes=['rsA'])
    for i in range(16):
        b = S.rot('yo', 2)
        S.op('dve', lambda e, i=i, b=b: e.scalar_tensor_tensor(out=yo[b], in0=x_res[:, i, :], scalar=rsA[:, i:i + 1], in1=nf,
                                                             op0=ALU.mult, op1=ALU.mult), reads=[('xres', i), 'rsA', 'nf'], writes=[('yo', b)])
        S.dma('sp', f'ost{b}', out_d[i * 128:(i + 1) * 128, :], yo[b], reads=[('yo', b)])
    S.barrier()
    for c in reversed(ps_cm):
        c.__exit__(None, None, None)
    arena_cm.__exit__(None, None, None)
    S.close()
    return nc


def _prep_shared(inp):
    f = np.float32
    sh = {}
    sh["ada_w"] = np.ascontiguousarray(inp["ada_w"], dtype=f)
    ada_b = np.asarray(inp["ada_b"], dtype=f)
    sh["ada_bT"] = np.ascontiguousarray(ada_b.reshape(DEPTH, 48, 128).transpose(0, 2, 1))
    sh["ada_bf"] = np.ascontiguousarray(ada_b.reshape(DEPTH, 1, 6 * D))
    sh["nmix"] = np.ascontiguousarray(np.asarray(inp["norm_mix"], dtype=f).reshape(DEPTH, 8, 128).transpose(0, 2, 1))
    sh["nmlp"] = np.ascontiguousarray(np.asarray(inp["norm_mlp"], dtype=f).reshape(DEPTH, 8, 128).transpose(0, 2, 1))
    sh["nfin"] = np.ascontiguousarray(np.asarray(inp["norm_final"], dtype=f).reshape(1, D))
    sh["w1"] = np.ascontiguousarray(inp["mlp_w1"], dtype=f)
    sh["w2"] = np.ascontiguousarray(inp["mlp_w2"], dtype=f)
    sh["abin"] = np.ascontiguousarray(inp["ab_w_in"], dtype=f)
    sh["about"] = np.ascontiguousarray(inp["ab_w_out"], dtype=f)
    rpb = np.asarray(inp["na_rpb"], dtype=f)
    nab = np.empty((2, N_PAT, 128, 8, 128), dtype=f)
    for pi, idx in enumerate(NA_PATS):
        valid = idx >= 0
        ic = np.where(valid, idx, 0)
        for jl in range(2):
            flat = rpb[jl].reshape(8, 15 * 31)
            g = flat[:, ic]
            g = np.where(valid[None], g, f(-1e30))
            nab[jl, pi] = g.transpose(1, 0, 2)
    sh["nabias"] = np.ascontiguousarray(nab.reshape(2, N_PAT, 128, 1024))
    wa2 = np.asarray(inp["gla_wa2"], dtype=f)
    ba = np.asarray(inp["gla_ba"], dtype=f)
    wa2b = np.zeros((2, 33, 512), dtype=f)
    for jl in range(2):
        for d in range(2):
            wa2b[jl, d * 16:(d + 1) * 16, d * 256:(d + 1) * 256] = wa2[jl, d]
            wa2b[jl, 32, d * 256:(d + 1) * 256] = ba[jl, d]
    sh["wa2b"] = wa2b
    sh["gn"] = np.ascontiguousarray(np.asarray(inp["gla_gnorm"], dtype=f).reshape(2, 1, 512))
    sw = np.asarray(inp["swa_w_in"], dtype=f)
    qcols = np.arange(1024)
    swap = qcols ^ 1
    kdup = np.concatenate([np.concatenate([1024 + g * 64 + np.arange(64)] * 2) for g in range(4)])
    kdup_sw = np.concatenate([np.concatenate([1024 + g * 64 + (np.arange(64) ^ 1)] * 2) for g in range(4)])
    vcols = 1280 + np.arange(256)
    cols = np.concatenate([qcols, swap, kdup, kdup_sw, vcols])
    sh["swin"] = np.ascontiguousarray(sw[:, :, cols])
    sh["swout"] = np.ascontiguousarray(inp["swa_w_out"], dtype=f)
    sh["sink"] = np.ascontiguousarray(np.asarray(inp["swa_sink"], dtype=f).reshape(2, 1, 16))
    Ct, St = _rope_tables()
    sh["ropeC"], sh["ropeS"] = Ct, St
    return sh


_NC_CACHE = {}


def kernel(**inp):
    n_layers = int(inp.pop("_n_layers", DEPTH))
    f = np.float32
    sh = _prep_shared(inp)
    x = np.asarray(inp["x"], dtype=f); c = np.asarray(inp["c"], dtype=f)
    ctx = np.asarray(inp["ctx"], dtype=f); c_ctx = np.asarray(inp["c_ctx"], dtype=f)
    in_maps = []
    for b in range(8):
        m = dict(sh)
        m["x"] = np.ascontiguousarray(x[b]); m["ctx"] = np.ascontiguousarray(ctx[b])
        m["scin"] = np.ascontiguousarray(np.concatenate([c[b].reshape(8, 128).T, c_ctx.reshape(8, 128).T], axis=1))
        in_maps.append(m)
    if n_layers not in _NC_CACHE:
        _NC_CACHE[n_layers] = build(n_layers)
    nc = _NC_CACHE[n_layers]
    res = run_bass_kernel_spmd(nc, in_maps, core_ids=list(range(8)))
    return np.stack([r["out"] for r in res.results], axis=0).astype(f)
```

```python
import numpy as np
import concourse.bass as bass
import concourse.mybir as mybir
from concourse.bass_utils import run_bass_kernel_spmd

F32 = mybir.dt.float32
BF16 = mybir.dt.bfloat16
AF = mybir.ActivationFunctionType
ALU = mybir.AluOpType
AX = mybir.AxisListType

D = 1024
SEQ = 2048
CTX = 256
NT = 18
NTOK = NT * 128
DEPTH = 4
EPS = 1e-6
GROUPS = [(0, 512), (512, 1024), (1024, 1536), (1536, 2048), (2048, 2304)]


class Sched:
    def __init__(self, nc):
        self.nc = nc
        self.eng = {'pe': nc.tensor, 'act': nc.scalar, 'dve': nc.vector, 'pool': nc.gpsimd, 'sp': nc.sync}
        self.sem, self.cnt, self.ctx, self.semobj = {}, {}, [], {}
        for e in self.eng:
            cm = nc.semaphore('s_' + e)
            self.sem[e] = cm.__enter__(); self.ctx.append(cm)
            self.cnt[e] = 0
            self.semobj['s_' + e] = self.sem[e]
        self.dsem, self.dcnt = {}, {}
        self.seen = {e: {} for e in self.eng}
        self.lastw, self.readers = {}, {}
        self.rr = {}

    def rot(self, name, n):
        v = self.rr.get(name, 0)
        self.rr[name] = v + 1
        return v % n

    def dma_sem(self, name):
        if name not in self.dsem:
            cm = self.nc.semaphore('d_' + name)
            self.dsem[name] = cm.__enter__(); self.ctx.append(cm)
            self.dcnt[name] = 0
            self.semobj['d_' + name] = self.dsem[name]
        return self.dsem[name]

    def _wait(self, e, tok):
        if tok is None:
            return
        sname, val = tok
        if e == 'pe' and sname == 's_pe':
            return
        if self.seen[e].get(sname, 0) >= val:
            return
        self.eng[e].wait_ge(self.semobj[sname], val)
        self.seen[e][sname] = val

    def _deps(self, e, reads, writes):
        for k in reads:
            self._wait(e, self.lastw.get(k))
        for k in writes:
            self._wait(e, self.lastw.get(k))
            for t in self.readers.get(k, ()):
                self._wait(e, t)

    def _commit(self, tok, reads, writes):
        for k in reads:
            self.readers.setdefault(k, []).append(tok)
        for k in writes:
            self.lastw[k] = tok
            self.readers[k] = []

    def op(self, e, fn, reads=(), writes=(), inc=True):
        self._deps(e, reads, writes)
        inst = fn(self.eng[e])
        if inc:
            self.cnt[e] += 1
            inst.then_inc(self.sem[e], 1)
            tok = ('s_' + e, self.cnt[e])
        else:
            tok = ('s_' + e, self.cnt[e] + 1)
        self._commit(tok, reads, writes)
        return tok

    def dma(self, e, slot, out, in_, reads=(), writes=(), **kw):
        sem = self.dma_sem(slot)
        self._deps(e, reads, writes)
        inst = self.eng[e].dma_start(out=out, in_=in_, **kw)
        self.dcnt[slot] += 16
        inst.then_inc(sem, 16)
        tok = ('d_' + slot, self.dcnt[slot])
        self._commit(tok, reads, writes)
        return tok

    def wait_all(self, e):
        for f in self.eng:
            if self.cnt[f] > 0:
                self._wait(e, ('s_' + f, self.cnt[f]))
        for s in self.dsem:
            if self.dcnt[s] > 0:
                self._wait(e, ('d_' + s, self.dcnt[s]))

    def barrier(self):
        for e in self.eng:
            self.wait_all(e)
        self.lastw, self.readers = {}, {}

    def close(self):
        for cm in reversed(self.ctx):
            cm.__exit__(None, None, None)


def _na_patterns():
    pats, keymap, tilemap = [], {}, {}
    qi = np.arange(128)
    for i in range(16):
        r = 2 * i + qi // 64
        c = qi % 64
        r0 = np.clip(r - 4, 0, 24)
        w0 = np.clip(c - 8, 0, 48)
        lst = []
        for jt in range(16):
            kr = 2 * jt + qi // 64
            kc = qi % 64
            valid = ((kr[:, None] >= r0[None, :]) & (kr[:, None] < r0[None, :] + 8)
                     & (kc[:, None] >= w0[None, :]) & (kc[:, None] < w0[None, :] + 16))
            if not valid.any():
                continue
            ro = kr[:, None] - r[None, :] + 7
            co = kc[:, None] - c[None, :] + 15
            idx = np.where(valid, ro * 31 + co, -1)
            key = idx.tobytes()
            if key not in keymap:
                keymap[key] = len(pats)
                pats.append(idx)
            lst.append((jt, keymap[key]))
        tilemap[i] = lst
    return pats, tilemap


NA_PATS, NA_TILEMAP = _na_patterns()
N_PAT = len(NA_PATS)


def _rope_tables():
    t = np.arange(SEQ)
    row = (t // 64).astype(np.float32)
    col = (t % 64).astype(np.float32)
    inv = (np.float32(10000.0) ** (-np.arange(16, dtype=np.float32) / np.float32(16))).astype(np.float32)
    ang = np.concatenate([row[:, None] * inv, col[:, None] * inv], axis=-1).astype(np.float32)
    cos, sin = np.cos(ang).astype(np.float32), np.sin(ang).astype(np.float32)
    p = np.arange(128)
    d = p % 64
    m = d // 2
    Ct = cos[:, m].T.copy()
    St = (sin[:, m] * np.where(d % 2 == 0, -1.0, 1.0)[None, :]).T.astype(np.float32).copy()
    return Ct, St


class _Stop(Exception):
    pass


def build(n_layers=DEPTH, stop=None):
    nc = bass.Bass("TRN2", target_bir_lowering=False)

    def din(name, shape):
        return nc.dram_tensor(name, list(shape), F32, kind="ExternalInput").ap()

    x_d = din("x", [SEQ, D]); ctx_d = din("ctx", [CTX, D])
    scin_d = din("scin", [128, 16])
    adaw_d = din("ada_w", [DEPTH, D, 6 * D])
    adabT_d = din("ada_bT", [DEPTH, 128, 48]); adabf_d = din("ada_bf", [DEPTH, 1, 6 * D])
    nmix_d = din("nmix", [DEPTH, 128, 8]); nmlp_d = din("nmlp", [DEPTH, 128, 8]); nfin_d = din("nfin", [1, D])
    w1_d = din("w1", [DEPTH, D, 4 * D]); w2_d = din("w2", [DEPTH, 4 * D, D])
    abin_d = din("abin", [2, D, 3104]); about_d = din("about", [2, D, D])
    nab_d = din("nabias", [2, N_PAT, 128, 1024])
    wa2b_d = din("wa2b", [2, 33, 512]); gn_d = din("gn", [2, 1, 512])
    swin_d = din("swin", [2, D, 3328]); swout_d = din("swout", [2, D, D]); sink_d = din("sink", [2, 1, 16])
    ropeC_d = din("ropeC", [128, SEQ]); ropeS_d = din("ropeS", [128, SEQ])
    out_d = nc.dram_tensor("out", [SEQ, D], F32, kind="ExternalOutput").ap()
    xres_d = nc.dram_tensor("xres", [NTOK, D], F32, kind="Internal").ap()

    S = Sched(nc)
    ARENA_W = 52480
    arena_cm = nc.sbuf_tensor("arena", [128, ARENA_W], F32)
    arena = arena_cm.__enter__()
    ps_cm = [nc.psum_tensor(f"ps{i}", [128, 512], F32) for i in range(6)] + \
            [nc.psum_tensor(f"ps{i}", [128, 1024], BF16) for i in (6, 7)]
    PS = [c.__enter__() for c in ps_cm]

    def V(off, shape, dt, parts=128):
        n = int(np.prod(shape[1:]))
        assert off % 4 == 0
        if dt == F32:
            assert off // 4 + n <= ARENA_W, (off, shape)
            a = arena[0:parts, off // 4: off // 4 + n]
        else:
            assert n % 2 == 0 and off // 4 + n // 2 <= ARENA_W, (off, shape)
            a = arena[0:parts, off // 4: off // 4 + n // 2].bitcast(BF16)
        if len(shape) == 3:
            a = a.rearrange("p (a b) -> p a b", a=shape[1])
        elif len(shape) == 4:
            a = a.rearrange("p (a b c) -> p a b c", a=shape[1], b=shape[2])
        return a

    KB = 1024
    R0, R1, R2, R3 = 0, 72 * KB, 108 * KB, 144 * KB
    o = R3
    ident = V(o, [128, 128], BF16); o += 256
    Uf = V(o, [128, 128], F32); o += 512
    Ub = V(o, [128, 128], F32); o += 512
    Rf = V(o, [128, 128], F32); o += 512
    Rb = V(o, [128, 128], F32); o += 512
    mkf = V(o, [128, 128], F32); o += 512
    mkb = V(o, [128, 128], F32); o += 512
    bmp = V(o, [128, 128], BF16); o += 256
    bmn = V(o, [128, 128], BF16); o += 256
    scT = V(o, [128, 8, 2], BF16); o += 32
    scin = V(o, [128, 16], F32); o += 64
    ones_row = V(o, [128, 128], BF16); o += 256
    nmix = V(o, [128, DEPTH, 8], F32); o += 4 * DEPTH * 8
    nmlp = V(o, [128, DEPTH, 8], F32); o += 4 * DEPTH * 8
    adabT = V(o, [128, DEPTH, 48], F32); o += 4 * DEPTH * 48
    modT = V(o, [128, 4, 8, 2], F32); o += 256
    GpM = V(o, [128, 8, 2], F32); o += 64
    GpL = V(o, [128, 8, 2], F32); o += 64
    ssA = V(o, [128, 32], F32); o += 128
    rsA = V(o, [128, 32], F32); o += 128
    gate_bc = V(o, [128, 4, 1024], F32); o += 16 * KB
    R3T = o
    R3END = ARENA_W * 4

    def memset(e, ap, val, key):
        S.op(e, lambda en: en.memset(ap, val), writes=[key])

    def asel(ap, pattern, cm, base, op, key, fill=0.0):
        S.op('pool', lambda en: en.affine_select(out=ap, in_=ap, pattern=pattern, compare_op=op, fill=fill,
                                                 base=base, channel_multiplier=cm), reads=[key], writes=[key])

    memset('pool', ident, 1.0, 'ident')
    asel(ident, [[-1, 128]], 1, 0, ALU.is_equal, 'ident')
    memset('pool', ones_row, 1.0, 'ones_row')
    for (ap, key, pat, cm, base) in [(Uf, 'Uf', [[1, 128]], -1, 0), (Ub, 'Ub', [[-1, 128]], 1, 0),
                                     (Rf, 'Rf', [[-1, 128]], 1, -1), (Rb, 'Rb', [[1, 128]], -1, -1)]:
        memset('pool', ap, -1.0 / 16.0, key)
        asel(ap, pat, cm, base, ALU.is_ge, key)
    memset('pool', mkf, 1.0, 'mkf'); asel(mkf, [[1, 128]], -1, 0, ALU.is_ge, 'mkf')
    memset('pool', mkb, 1.0, 'mkb'); asel(mkb, [[-1, 128]], 1, 0, ALU.is_ge, 'mkb')
    memset('pool', bmp, 0.0, 'bmp'); asel(bmp, [[-1, 128]], 1, 0, ALU.is_ge, 'bmp', fill=-1e30)
    memset('pool', bmn, 0.0, 'bmn'); asel(bmn, [[1, 128]], -1, 0, ALU.is_ge, 'bmn', fill=-1e30)
    S.dma('sp', 'c0', scin, scin_d, writes=['scin'])
    S.dma('sp', 'c1', nmix, nmix_d.rearrange("l p c -> p l c"), writes=['nmix'])
    S.dma('sp', 'c2', nmlp, nmlp_d.rearrange("l p c -> p l c"), writes=['nmlp'])
    S.dma('sp', 'c3', adabT, adabT_d.rearrange("l p c -> p l c"), writes=['adabT'])
    S.op('act', lambda e: e.activation(out=scT.rearrange("p c w -> p w c"), in_=scin.rearrange("p (w c) -> p w c", w=2),
                                       func=AF.Silu), reads=['scin'], writes=['scT'])

    x_res = V(R0, [128, NT, D], F32)

    def tok_rows(i):
        return (x_d[i * 128:(i + 1) * 128, :] if i < 16 else ctx_d[(i - 16) * 128:(i - 15) * 128, :])

    def adaln(l):
        ob = R2
        adab = [V(ob, [128, 8, 1024], BF16), V(ob + 16 * KB, [128, 8, 1024], BF16)]
        ob += 32 * KB
        sc_rep = V(ob, [128, 8, 2, 128], BF16); ob += 4 * KB
        abf = V(R3T, [128, 2048], BF16)
        for kc in range(8):
            for w in range(2):
                S.op('dve', lambda e, kc=kc, w=w: e.tensor_copy(out=sc_rep[:, kc, w, :],
                                                                in_=scT[:, kc, w:w + 1].to_broadcast([128, 128])),
                     reads=['scT'], writes=[('sc_rep', kc, w)])
        S.dma('pool', 'abf0', abf[0:1, 0:1024], adabf_d[l, :, 2 * D:3 * D], writes=['abf0'])
        S.dma('pool', 'abf1', abf[0:1, 1024:2048], adabf_d[l, :, 5 * D:6 * D], writes=['abf1'])
        kind_of = {0: 0, 1: 1, 3: 2, 4: 3}
        for blk in range(6):
            bi = S.rot('adab', 2)
            buf = adab[bi]
            src = adaw_d[l, :, blk * D:(blk + 1) * D].rearrange("(c p) n -> p c n", p=128)
            for hq in range(2):
                S.dma('pool', f'adab{bi}_{hq}', buf[:, hq * 4:(hq + 1) * 4, :], src[:, hq * 4:(hq + 1) * 4, :],
                      writes=[('adab', bi, hq)])
            rk = [('adab', bi, 0), ('adab', bi, 1)]
            if blk in kind_of:
                pb = S.rot('psA', 2)
                ps = PS[pb]
                for j in range(8):
                    for kc in range(8):
                        S.op('pe', lambda e, j=j, kc=kc: e.matmul(ps[:, j * 2:(j + 1) * 2], lhsT=buf[:, kc, j * 128:(j + 1) * 128],
                                                                  rhs=scT[:, kc, :], start=(kc == 0), stop=(kc == 7)),
                             reads=rk + ['scT'], writes=[('ps', pb)])
                S.op('dve', lambda e, blk=blk: e.tensor_tensor(
                    out=modT[:, kind_of[blk], :, :], in0=ps[:, 0:16].rearrange("p (c w) -> p c w", w=2),
                    in1=adabT[:, l, blk * 8:(blk + 1) * 8].unsqueeze(2).to_broadcast([128, 8, 2]), op=ALU.add),
                    reads=[('ps', pb), 'adabT'], writes=[('modT', kind_of[blk])])
            else:
                gi = 0 if blk == 2 else 1
                for w in range(2):
                    for hf in range(2):
                        pb = S.rot('psA', 2)
                        ps = PS[pb]
                        for kc in range(8):
                            S.op('pe', lambda e, kc=kc, w=w, hf=hf: e.matmul(ps[:, :], lhsT=sc_rep[:, kc, w, :],
                                                                           rhs=buf[:, kc, hf * 512:(hf + 1) * 512],
                                                                           start=(kc == 0), stop=False),
                                 reads=rk + [('sc_rep', kc, w)], writes=[('ps', pb)])
                        S.op('pe', lambda e, hf=hf, gi=gi: e.matmul(ps[:, :], lhsT=ones_row[0:1, :],
                                                                  rhs=abf[0:1, gi * 1024 + hf * 512: gi * 1024 + (hf + 1) * 512],
                                                                  start=False, stop=True),
                             reads=['ones_row', 'abf0', 'abf1'], writes=[('ps', pb)])
                        S.op('act', lambda e, w=w, hf=hf, gi=gi: e.activation(out=gate_bc[:, gi * 2 + w, hf * 512:(hf + 1) * 512],
                                                                           in_=ps[:, :], func=AF.Copy),
                             reads=[('ps', pb)], writes=[('gate', gi * 2 + w, hf)])
        for (Gp, nrm, kind, key) in [(GpM, nmix, 1, 'GpM'), (GpL, nmlp, 3, 'GpL')]:
            S.op('dve', lambda e, Gp=Gp, nrm=nrm, kind=kind: e.scalar_tensor_tensor(
                out=Gp[:, :, :], in0=modT[:, kind, :, :], scalar=1.0,
                in1=nrm[:, l, :].unsqueeze(2).to_broadcast([128, 8, 2]), op0=ALU.add, op1=ALU.mult),
                reads=[('modT', kind), 'nmix', 'nmlp'], writes=[key])

    def norm_to_hT(hT, Gp, gkey, shift_kind, tiles, tmp_off):
        junk = V(tmp_off, [128, 1024], F32)
        xn = [V(tmp_off + 4 * KB, [128, 1024], BF16), V(tmp_off + 6 * KB, [128, 1024], BF16)]
        for i in tiles:
            S.op('act', lambda e, i=i: e.activation(out=junk, in_=x_res[:, i, :], func=AF.Square, accum_out=ssA[:, i:i + 1]),
                 reads=[('xres', i)], writes=['junk', ('ssA', i)])
        n = len(tiles)
        t0 = tiles[0]
        S.op('act', lambda e: e.activation(out=rsA[:, t0:t0 + n], in_=ssA[:, t0:t0 + n], func=AF.Sqrt, scale=1.0 / D, bias=EPS),
             reads=[('ssA', i) for i in tiles], writes=['rsA_t'])
        S.op('dve', lambda e: e.reciprocal(out=rsA[:, t0:t0 + n], in_=rsA[:, t0:t0 + n]), reads=['rsA_t'], writes=['rsA'])
        for i in tiles:
            b = S.rot('xn', 2)
            w = 0 if i < 16 else 1
            S.op('dve', lambda e, i=i, b=b: e.tensor_scalar(out=xn[b], in0=x_res[:, i, :], scalar1=rsA[:, i:i + 1], scalar2=None,
                                                          op0=ALU.mult), reads=[('xres', i), 'rsA'], writes=[('xn', b)])
            pb = 6 + S.rot('psT', 2)
            pst = PS[pb]
            for c in range(8):
                S.op('pe', lambda e, c=c, b=b: e.transpose(out=pst[:, c * 128:(c + 1) * 128], in_=xn[b][:, c * 128:(c + 1) * 128],
                                                         identity=ident), reads=[('xn', b), 'ident'], writes=[('ps', pb)])
            use_act = (S.rot('nev', 2) == 0)
            for c in range(8):
                if use_act:
                    S.op('act', lambda e, c=c, i=i, w=w: e.activation(
                        out=hT[:, c, i * 128:(i + 1) * 128], in_=pst[:, c * 128:(c + 1) * 128], func=AF.Identity,
                        scale=Gp[:, c, w:w + 1], bias=modT[:, shift_kind, c, w:w + 1]),
                        reads=[('ps', pb), gkey, ('modT', shift_kind)], writes=[('hT', c, i)])
                else:
                    S.op('dve', lambda e, c=c, i=i, w=w: e.tensor_scalar(
                        out=hT[:, c, i * 128:(i + 1) * 128], in0=pst[:, c * 128:(c + 1) * 128],
                        scalar1=Gp[:, c, w:w + 1], scalar2=modT[:, shift_kind, c, w:w + 1], op0=ALU.mult, op1=ALU.add),
                        reads=[('ps', pb), gkey, ('modT', shift_kind)], writes=[('hT', c, i)])

    def load_w(buf, src, ncols, key):
        sv = src.rearrange("(c p) n -> p c n", p=128)
        for hq in range(2):
            S.dma('pool', key[0] + str(key[1]) + '_' + str(hq), buf[:, hq * 4:(hq + 1) * 4, 0:ncols], sv[:, hq * 4:(hq + 1) * 4, :],
                  writes=[(key, hq)])
        return [(key, 0), (key, 1)]

    def proj_fm(hT, wbuf, wkeys, c0, M, evac, tiles_hi=NTOK):
        for (t0, t1) in GROUPS:
            if t0 >= tiles_hi:
                continue
            pb = S.rot('psP', 4)
            ps = PS[pb]
            for kc in range(8):
                S.op('pe', lambda e, kc=kc: e.matmul(ps[0:M, 0:t1 - t0], lhsT=wbuf[:, kc, c0:c0 + M], rhs=hT[:, kc, t0:t1],
                                                    start=(kc == 0), stop=(kc == 7)),
                     reads=wkeys + [('hT', kc, i) for i in range(t0 // 128, t1 // 128)], writes=[('ps', pb)])
            evac(ps, pb, t0, t1)

    def proj_tm(hT, wbuf, wkeys, c0, n, evac, tiles):
        for i in tiles:
            pb = S.rot('psP', 4)
            ps = PS[pb]
            for kc in range(8):
                S.op('pe', lambda e, kc=kc: e.matmul(ps[:, 0:n], lhsT=hT[:, kc, i * 128:(i + 1) * 128], rhs=wbuf[:, kc, c0:c0 + n],
                                                    start=(kc == 0), stop=(kc == 7)),
                     reads=wkeys + [('hT', kc, i)], writes=[('ps', pb)])
            evac(ps, pb, i)

    def ev_copy(eng, out_ap, in_ap, pb, wkey, scale=None):
        if eng == 'act':
            if scale is None:
                S.op('act', lambda e: e.activation(out=out_ap, in_=in_ap, func=AF.Copy), reads=[('ps', pb)], writes=[wkey])
            else:
                S.op('act', lambda e: e.activation(out=out_ap, in_=in_ap, func=AF.Copy, scale=scale), reads=[('ps', pb)], writes=[wkey])
        else:
            if scale is None:
                S.op(eng, lambda e: e.tensor_copy(out=out_ap, in_=in_ap), reads=[('ps', pb)], writes=[wkey])
            else:
                S.op(eng, lambda e: e.tensor_scalar(out=out_ap, in0=in_ap, scalar1=scale, scalar2=None, op0=ALU.mult),
                     reads=[('ps', pb)], writes=[wkey])

    def alt(name):
        return 'act' if S.rot(name, 2) == 0 else 'dve'

    def transpose_otok(o_tok, OT, tiles):
        for i in tiles:
            pb = 6 + S.rot('psT', 2)
            pst = PS[pb]
            for c in range(8):
                S.op('pe', lambda e, c=c: e.transpose(out=pst[:, c * 128:(c + 1) * 128], in_=o_tok[:, i, c * 128:(c + 1) * 128],
                                                    identity=ident), reads=[('otok', i), 'ident'], writes=[('ps', pb)])
            eng = alt('otev')
            ev_copy(eng, OT[:, :, i * 128:(i + 1) * 128], pst.rearrange("p (c t) -> p c t", c=8), pb, ('OT', i))

    def even_mixer(j, hT, o_tok):
        wb = [V(R3T, [128, 8, 512], BF16), V(R3T + 8 * KB, [128, 8, 512], BF16)]
        qaT = V(R0, [128, 4, NTOK], BF16)
        kaT = V(R0 + 18 * KB, [128, 4, NTOK], BF16)
        va = V(R0 + 36 * KB, [128, NT, 8, 65], BF16)
        S.op('pool', lambda e: e.memset(va[:, :, :, 64:65], 1.0), writes=['va1'])
        for blk in range(3):
            bi = S.rot('wb', 2)
            wk = load_w(wb[bi], abin_d[j, :, blk * 512:(blk + 1) * 512], 512, ('wb', bi))
            if blk < 2:
                dst = qaT if blk == 0 else kaT
                nm = 'qaT' if blk == 0 else 'kaT'
                sc = 0.125 if blk == 0 else None
                for c in range(4):
                    def evac(ps, pb, t0, t1, c=c, dst=dst, nm=nm, sc=sc):
                        ev_copy(alt('ev'), dst[:, c, t0:t1], ps[:, 0:t1 - t0], pb, (nm, c, t0), scale=sc)
                    proj_fm(hT, wb[bi], wk, c * 128, 128, evac)
            else:
                def evac(ps, pb, i):
                    ev_copy(alt('ev'), va[:, i, :, 0:64], ps[:, 0:512].rearrange("p (h d) -> p h d", h=8), pb, ('va', i))
                proj_tm(hT, wb[bi], wk, 0, 512, evac, range(NT))
        bt = [V(R3T + 16 * KB, [128, 5, 8, 128], BF16), V(R3T + 26 * KB, [128, 5, 8, 128], BF16)]
        pT = [V(R0 + 55 * KB, [128, 7, 128], BF16), V(R0 + 55 * KB + 1792, [128, 7, 128], BF16)]
        rec = V(R0 + 59 * KB, [128, 8, 1], F32)
        assert R3T + 36 * KB <= R3END
        PSF = [PS[0], PS[1], PS[2], PS[3], PS[4], PS[5], PS[6][:, :].bitcast(F32), PS[7][:, :].bitcast(F32)]
        rec2 = [rec, V(R0 + 59 * KB + 64, [128, 8, 1], F32)]
        tile_blocks, tile_bb = {}, {}

        def na_scores(i, h):
            if h == 0:
                if i < 16:
                    tile_blocks[i] = [(jt, pat) for (jt, pat) in NA_TILEMAP[i]] + [(16, None), (17, None)]
                    bb = S.rot('bt', 2)
                    tile_bb[i] = bb
                    for bi_, (jt, pat) in enumerate(NA_TILEMAP[i]):
                        S.dma('pool', f'bt{bb}_{bi_}', bt[bb][:, bi_, :, :], nab_d[j, pat, :, :].rearrange("k (h q) -> k h q", h=8),
                              writes=[('bt', bb, bi_)])
                else:
                    tile_blocks[i] = [(16, None), (17, None)]
                    tile_bb[i] = 0
            blocks, bb = tile_blocks[i], tile_bb[i]
            nb = len(blocks)
            p, pbs = h // 2, 64 * (h % 2)
            par = S.rot('nasc', 2)
            banks = [2 * par, 2 * par + 1]
            nw = sum(1 for (_, pat) in blocks if pat is not None)
            nA = min(nw, 4)
            if nA > 0:
                S.op('pe', lambda e: e.matmul(PS[banks[0]][:, 0:nA * 128].rearrange("p (b q) -> p b q", b=nA), lhsT=ident,
                                              rhs=bt[bb][:, 0:nA, h, :], start=True, stop=False),
                     reads=['ident'] + [('bt', bb, x) for x in range(nA)], writes=[('ps', banks[0])])
            for bi_, (jt, pat) in enumerate(blocks):
                bk = banks[bi_ // 4]
                dst = PS[bk][:, (bi_ % 4) * 128:(bi_ % 4 + 1) * 128]
                rk = [('kaT', p, t0) for (t0, t1) in GROUPS if t0 <= jt * 128 < t1] + \
                     [('qaT', p, t0) for (t0, t1) in GROUPS if t0 <= i * 128 < t1]
                inA = pat is not None and bi_ < 4
                if pat is not None and not inA:
                    S.op('pe', lambda e, dst=dst, bi_=bi_: e.matmul(dst, lhsT=ident, rhs=bt[bb][:, bi_, h, :], start=True, stop=False),
                         reads=['ident', ('bt', bb, bi_)], writes=[('ps', bk)])
                S.op('pe', lambda e, dst=dst, jt=jt, pat=pat, inA=inA, bi_=bi_: e.matmul(
                    dst, lhsT=kaT[pbs:pbs + 64, p, jt * 128:(jt + 1) * 128], rhs=qaT[pbs:pbs + 64, p, i * 128:(i + 1) * 128],
                    start=(pat is None), stop=((bi_ == nA - 1) if inA else True)), reads=rk, writes=[('ps', bk)])
            n0 = min(nb, 4)
            S.op('act', lambda e: e.activation(out=pT[par][:, 0:n0, :], in_=PS[2 * par][:, 0:n0 * 128].rearrange(
                "p (b q) -> p b q", b=n0), func=AF.Exp), reads=[('ps', 2 * par)], writes=[('pT', par, 0)])
            if nb > 4:
                n1 = nb - 4
                S.op('act', lambda e: e.activation(out=pT[par][:, 4:4 + n1, :], in_=PS[2 * par + 1][:, 0:n1 * 128].rearrange(
                    "p (b q) -> p b q", b=n1), func=AF.Exp), reads=[('ps', 2 * par + 1)], writes=[('pT', par, 1)])
            return par

        def na_pv(i, h, par):
            blocks = tile_blocks[i]
            nb = len(blocks)
            oset = 4 + 2 * (i % 2)
            ob = oset + h // 4
            od = PSF[ob][:, (h % 4) * 65:(h % 4) * 65 + 65]
            for bi_, (jt, pat) in enumerate(blocks):
                S.op('pe', lambda e, bi_=bi_, jt=jt: e.matmul(od, lhsT=pT[par][:, bi_, :], rhs=va[:, jt, h, :],
                                                            start=(bi_ == 0), stop=(bi_ == nb - 1)),
                     reads=[('pT', par, 0), ('pT', par, 1), ('va', jt), 'va1'], writes=[('ps', ob)])
            if h == 7:
                for hf in range(2):
                    ob2 = oset + hf
                    rc = rec2[i % 2]
                    ov = PSF[ob2][:, 0:260].rearrange("p (h e) -> p h e", e=65)
                    S.op('dve', lambda e, ov=ov, hf=hf, rc=rc: e.reciprocal(out=rc[:, hf * 4:(hf + 1) * 4, :], in_=ov[:, :, 64:65]),
                         reads=[('ps', ob2)], writes=[('rec', i % 2, hf)])
                    S.op('dve', lambda e, ov=ov, hf=hf, rc=rc: e.tensor_tensor(
                        out=o_tok[:, i, hf * 256:(hf + 1) * 256].rearrange("p (h d) -> p h d", h=4), in0=ov[:, :, 0:64],
                        in1=rc[:, hf * 4:(hf + 1) * 4, :].to_broadcast([128, 4, 64]), op=ALU.mult),
                        reads=[('ps', ob2), ('rec', i % 2, hf)], writes=[('otok', i)])

        items = [(i, h) for i in range(NT) for h in range(8)]
        pend = None
        for (i, h) in items:
            par = na_scores(i, h)
            if pend is not None:
                na_pv(*pend)
            pend = (i, h, par)
        na_pv(*pend)
        S.barrier()
        if stop == 'na':
            raise _Stop()
        qbT = V(R0, [128, 2, NTOK], BF16)
        kbT = V(R0 + 9 * KB, [128, 2, NTOK], BF16)
        kbk = V(R0 + 18 * KB, [128, NT, 256], BF16)
        vb = V(R0 + 27 * KB, [128, NT, 512], BF16)
        sgg = V(R0 + 45 * KB, [128, NT, 512], BF16)
        lrT1 = V(R0 + 63 * KB, [128, NTOK], BF16)
        W2b = V(R0 + 68 * KB, [128, 512], BF16)
        gnbc = V(R0 + 69 * KB, [128, 512], F32)
        S.dma('pool', 'w2b', W2b[0:33, :], wa2b_d[j], writes=['W2b'])
        S.dma('sp', 'gn', gnbc, gn_d[j].broadcast_to([128, 512]), writes=['gnbc'])
        S.op('pool', lambda e: e.memset(lrT1[32:33, :], 1.0), writes=['lr1'])
        sgt = [V(R3T + 16 * KB, [128, 512], F32), V(R3T + 18 * KB, [128, 512], F32)]
        bi = S.rot('wb', 2)
        wk = load_w(wb[bi], abin_d[j, :, 1536:2048], 512, ('wb', bi))
        for c in range(2):
            def evq(ps, pb, t0, t1, c=c):
                ev_copy(alt('ev'), qbT[:, c, t0:t1], ps[:, 0:t1 - t0], pb, ('qbT', c, t0), scale=0.125)
            proj_fm(hT, wb[bi], wk, c * 128, 128, evq)
            def evk(ps, pb, t0, t1, c=c):
                ev_copy(alt('ev'), kbT[:, c, t0:t1], ps[:, 0:t1 - t0], pb, ('kbT', c, t0))
            proj_fm(hT, wb[bi], wk, 256 + c * 128, 128, evk)
        def evkk(ps, pb, i):
            ev_copy(alt('ev'), kbk[:, i, :], ps[:, 0:256], pb, ('kbk', i))
        proj_tm(hT, wb[bi], wk, 256, 256, evkk, range(NT))
        bi = S.rot('wb', 2)
        wk = load_w(wb[bi], abin_d[j, :, 2048:2560], 512, ('wb', bi))
        def evv(ps, pb, i):
            ev_copy(alt('ev'), vb[:, i, :], ps[:, 0:512], pb, ('vb', i))
        proj_tm(hT, wb[bi], wk, 0, 512, evv, range(NT))
        bi = S.rot('wb', 2)
        wk = load_w(wb[bi], abin_d[j, :, 2560:3072], 512, ('wb', bi))
        def evg(ps, pb, i):
            b = S.rot('sgt', 2)
            S.op('act', lambda e: e.activation(out=sgt[b], in_=ps[:, 0:512], func=AF.Silu), reads=[('ps', pb)], writes=[('sgt', b)])
            S.op('pool', lambda e: e.tensor_tensor(out=sgg[:, i, :], in0=sgt[b], in1=gnbc, op=ALU.mult),
                 reads=[('sgt', b), 'gnbc'], writes=[('sgg', i)])
        proj_tm(hT, wb[bi], wk, 0, 512, evg, range(NT))
        bi = S.rot('wb', 2)
        wk = load_w(wb[bi], abin_d[j, :, 3072:3104], 32, ('wb', bi))
        def evl(ps, pb, t0, t1):
            ev_copy(alt('ev'), lrT1[0:32, t0:t1], ps[0:32, 0:t1 - t0], pb, ('lrT', t0))
        proj_fm(hT, wb[bi], wk, 0, 32, evl)
        S.barrier()
        if stop == 'glaproj':
            raise _Stop()
        oacc = V(R1, [128, NT, 512], F32)
        t = R3T
        TS = []
        for si in range(2):
            d_ = {}
            d_['E32'] = V(t, [128, 256], F32); t += KB
            d_['L32'] = V(t, [128, 256], F32); t += KB
            d_['eT'] = V(t, [128, 2, 128], F32); t += KB
            d_['enT'] = V(t, [128, 2, 128], F32); t += KB
            d_['krem'] = V(t, [128, 256], F32); t += KB
            d_['qtT'] = V(t, [128, 2, 128], BF16); t += 512
            d_['ktT'] = V(t, [128, 2, 128], BF16); t += 512
            d_['kend'] = V(t, [128, 256], BF16); t += 512
            d_['Abf'] = V(t, [128, 4, 128], BF16); t += KB
            TS.append(d_)
        Sst = [V(t, [128, 2, 128], F32), V(t + KB, [128, 2, 128], F32)]; t += 2 * KB
        Sbf = [V(t, [128, 2, 128], BF16), V(t + 512, [128, 2, 128], BF16)]; t += KB
        sq = V(t, [128, 512], F32); t += 2 * KB
        t1b = V(t, [128, 512], F32); t += 2 * KB
        ss4 = V(t, [128, 4], F32); t += 16
        rs4 = V(t, [128, 4], F32); t += 16
        assert t <= R3END
        PA = [PS[3], PS[6][:, :].bitcast(F32)]
        PO = [PS[4], PS[7][:, :].bitcast(F32)]
        pak, pok = [3, 6], [4, 7]
        for d in range(2):
            S.op('pool', lambda e, d=d: e.memset(Sst[d], 0.0), writes=[('Sst', d, 0), ('Sst', d, 1)])
            S.op('pool', lambda e, d=d: e.memset(Sbf[d], 0.0), writes=[('Sbf', d, 0), ('Sbf', d, 1)])
        orders = [[16, 17] + list(range(16)), [17, 16] + list(range(15, -1, -1))]
        visited = set()

        def gla_front(d, ti, si):
            T_ = TS[si]
            E32, L32, eT, enT, krem, qtT, ktT, kend, Abf = (T_[k] for k in ('E32', 'L32', 'eT', 'enT', 'krem', 'qtT', 'ktT', 'kend', 'Abf'))
            Ud, Rd, mk = (Uf, Rf, mkf) if d == 0 else (Ub, Rb, mkb)
            Uk, Rk, mkk = ('Uf', 'Rf', 'mkf') if d == 0 else ('Ub', 'Rb', 'mkb')
            tc0 = ti * 128
            tg = [t0 for (t0, t1_) in GROUPS if t0 <= tc0 < t1_][0]
            K = lambda n: (n, si)
            S.op('pe', lambda e: e.matmul(PS[0][:, 0:256], lhsT=lrT1[0:33, tc0:tc0 + 128], rhs=W2b[0:33, d * 256:(d + 1) * 256],
                                          start=True, stop=True), reads=[('lrT', tg), 'lr1', 'W2b'], writes=[('ps', 0)])
            S.op('act', lambda e: e.activation(out=E32, in_=PS[0][:, 0:256], func=AF.Exp, scale=-1.0), reads=[('ps', 0)], writes=[K('E32')])
            S.op('act', lambda e: e.activation(out=L32, in_=E32, func=AF.Ln, bias=1.0), reads=[K('E32')], writes=[K('L32')])
            for p in range(2):
                S.op('pe', lambda e, p=p: e.matmul(PS[1][:, p * 128:(p + 1) * 128], lhsT=L32[:, p * 128:(p + 1) * 128], rhs=Ud,
                                                  start=True, stop=True), reads=[K('L32'), Uk], writes=[('ps', 1)])
            S.op('pe', lambda e: e.matmul(PS[2][:, 0:256], lhsT=Rd, rhs=L32, start=True, stop=True), reads=[K('L32'), Rk], writes=[('ps', 2)])
            S.op('act', lambda e: e.activation(out=eT, in_=PS[1][:, 0:256].rearrange("p (a b) -> p a b", a=2), func=AF.Exp),
                 reads=[('ps', 1)], writes=[K('eT')])
            S.op('act', lambda e: e.activation(out=enT, in_=PS[1][:, 0:256].rearrange("p (a b) -> p a b", a=2), func=AF.Exp, scale=-1.0),
                 reads=[('ps', 1)], writes=[K('enT')])
            S.op('act', lambda e: e.activation(out=krem, in_=PS[2][:, 0:256], func=AF.Exp), reads=[('ps', 2)], writes=[K('krem')])
            S.op('dve', lambda e: e.tensor_tensor(out=qtT, in0=qbT[:, :, tc0:tc0 + 128], in1=eT, op=ALU.mult),
                 reads=[('qbT', 0, tg), ('qbT', 1, tg), K('eT')], writes=[K('qtT')])
            S.op('pool', lambda e: e.tensor_tensor(out=ktT, in0=kbT[:, :, tc0:tc0 + 128], in1=enT, op=ALU.mult),
                 reads=[('kbT', 0, tg), ('kbT', 1, tg), K('enT')], writes=[K('ktT')])
            S.op('pool', lambda e: e.tensor_tensor(out=kend, in0=kbk[:, ti, :], in1=krem, op=ALU.mult),
                 reads=[('kbk', ti), K('krem')], writes=[K('kend')])
            for h in range(4):
                p, pbs = h // 2, 64 * (h % 2)
                S.op('pe', lambda e, h=h, p=p, pbs=pbs: e.matmul(PA[h % 2][:, p * 128:(p + 1) * 128], lhsT=ktT[pbs:pbs + 64, p, :],
                                                              rhs=qtT[pbs:pbs + 64, p, :], start=True, stop=True),
                     reads=[K('ktT'), K('qtT')], writes=[('ps', pak[h % 2])])
            for h in range(4):
                S.op('dve', lambda e, h=h: e.tensor_tensor(out=Abf[:, h, :], in0=PA[h % 2][:, (h // 2) * 128:(h // 2 + 1) * 128],
                                                           in1=mk, op=ALU.mult),
                     reads=[('ps', pak[h % 2]), mkk], writes=[K('Abf')])

        def gla_back(d, ti, si):
            T_ = TS[si]
            eT, qtT, kend, Abf = T_['eT'], T_['qtT'], T_['kend'], T_['Abf']
            K = lambda n: (n, si)
            last = 127 if d == 0 else 0
            for h in range(4):
                p, pbs = h // 2, 64 * (h % 2)
                S.op('pe', lambda e, h=h, p=p: e.matmul(PO[h % 2][:, p * 128:(p + 1) * 128], lhsT=Abf[:, h, :], rhs=vb[:, ti, h * 128:(h + 1) * 128],
                                                       start=True, stop=False), reads=[K('Abf'), ('vb', ti)], writes=[('ps', pok[h % 2])])
                S.op('pe', lambda e, h=h, p=p, pbs=pbs: e.matmul(PO[h % 2][:, p * 128:(p + 1) * 128], lhsT=qtT[pbs:pbs + 64, p, :],
                                                              rhs=Sbf[d][pbs:pbs + 64, p, :], start=False, stop=True),
                     reads=[K('qtT'), ('Sbf', d, 0), ('Sbf', d, 1)], writes=[('ps', pok[h % 2])])
            first = ti not in visited
            visited.add(ti)
            for par in range(2):
                ov_ = oacc[:, ti, :].rearrange("p (a b v) -> p a b v", a=2, b=2)[:, :, par, :]
                pv_ = PO[par][:, 0:256].rearrange("p (a v) -> p a v", a=2)
                if first:
                    S.op('act', lambda e, ov_=ov_, pv_=pv_: e.activation(out=ov_, in_=pv_, func=AF.Copy),
                         reads=[('ps', pok[par])], writes=[('oacc', ti, par)])
                else:
                    S.op('dve', lambda e, ov_=ov_, pv_=pv_: e.tensor_tensor(out=ov_, in0=pv_, in1=ov_, op=ALU.add),
                         reads=[('ps', pok[par]), ('oacc', ti, par)], writes=[('oacc', ti, par)])
            for p in range(2):
                S.op('pe', lambda e, p=p: e.matmul(PS[5][:, p * 256:(p + 1) * 256], lhsT=kend[:, p * 128:(p + 1) * 128],
                                                  rhs=vb[:, ti, p * 256:(p + 1) * 256], start=True, stop=True),
                     reads=[K('kend'), ('vb', ti)], writes=[('ps', 5)])
            for p in range(2):
                for hp in range(2):
                    r0 = hp * 64
                    S.op('dve', lambda e, p=p, hp=hp, r0=r0: e.scalar_tensor_tensor(
                        out=Sst[d][r0:r0 + 64, p, :], in0=Sst[d][r0:r0 + 64, p, :], scalar=eT[r0:r0 + 64, p, last:last + 1],
                        in1=PS[5][r0:r0 + 64, p * 256 + hp * 128: p * 256 + (hp + 1) * 128], op0=ALU.mult, op1=ALU.add),
                        reads=[('ps', 5), K('eT'), ('Sst', d, hp)], writes=[('Sst', d, hp)])
            for hp in range(2):
                r0 = hp * 64
                S.op('act', lambda e, r0=r0: e.activation(out=Sbf[d][r0:r0 + 64, :, :], in_=Sst[d][r0:r0 + 64, :, :], func=AF.Copy),
                     reads=[('Sst', d, hp)], writes=[('Sbf', d, hp)])
            if not first:
                S.op('dve', lambda e: e.tensor_tensor(out=sq, in0=oacc[:, ti, :], in1=oacc[:, ti, :], op=ALU.mult),
                     reads=[('oacc', ti, 0), ('oacc', ti, 1)], writes=['sq'])
                S.op('dve', lambda e: e.tensor_reduce(out=ss4, in_=sq.rearrange("p (h v) -> p h v", h=4), axis=AX.X, op=ALU.add),
                     reads=['sq'], writes=['ss4'])
                S.op('act', lambda e: e.activation(out=rs4, in_=ss4, func=AF.Ln, scale=1.0 / 128.0, bias=EPS), reads=['ss4'], writes=['rs4t'])
                S.op('act', lambda e: e.activation(out=rs4, in_=rs4, func=AF.Exp, scale=-0.5), reads=['rs4t'], writes=['rs4'])
                S.op('dve', lambda e: e.tensor_tensor(out=t1b.rearrange("p (h v) -> p h v", h=4),
                                                      in0=oacc[:, ti, :].rearrange("p (h v) -> p h v", h=4),
                                                      in1=rs4.unsqueeze(2).to_broadcast([128, 4, 128]), op=ALU.mult),
                     reads=[('oacc', ti, 0), ('oacc', ti, 1), 'rs4'], writes=['t1b'])
                S.op('pool', lambda e: e.tensor_tensor(out=o_tok[:, ti, 512:1024], in0=t1b, in1=sgg[:, ti, :], op=ALU.mult),
                     reads=['t1b', ('sgg', ti)], writes=[('otok', ti)])

        gitems = []
        for k in range(NT):
            gitems.append((0, orders[0][k]))
            gitems.append((1, orders[1][k]))
        pend = None
        for n_, (d, ti) in enumerate(gitems):
            gla_front(d, ti, n_ % 2)
            if pend is not None:
                gla_back(*pend)
            pend = (d, ti, n_ % 2)
        gla_back(*pend)
        S.barrier()

    def odd_mixer(j, hT, o_tok, ctx_out):
        wb = [V(R3T, [128, 8, 512], BF16), V(R3T + 8 * KB, [128, 8, 512], BF16)]
        ropeC = V(R3T + 16 * KB, [128, SEQ], F32)
        ropeS = V(R3T + 24 * KB, [128, SEQ], F32)
        pT = V(R0 + 66 * KB, [128, 5, 4, 128], BF16)
        es = V(R3T + 32 * KB, [128, 16], F32)
        den = V(R3T + 32 * KB + 64, [128, 4, 1], F32)
        tq = [V(R0 + 62 * KB, [128, 512], F32), V(R0 + 64 * KB, [128, 512], F32)]
        assert R3T + 33 * KB <= R3END
        S.dma('sp', 'rc', ropeC, ropeC_d, writes=['ropeC'])
        S.dma('sp', 'rs', ropeS, ropeS_d, writes=['ropeS'])
        S.dma('sp', 'sk', es, sink_d[j].broadcast_to([128, 16]), writes=['es_raw'])
        S.op('act', lambda e: e.activation(out=es, in_=es, func=AF.Exp), reads=['es_raw'], writes=['es'])
        kT2 = V(R0, [128, 4, NTOK], BF16)
        krT2 = V(R0 + 18 * KB, [128, 4, SEQ], BF16)
        v65 = V(R0 + 34 * KB, [128, NT, 4, 65], BF16)
        qT = V(R0 + 44 * KB, [128, 2, NTOK], BF16)
        qrT = V(R0 + 53 * KB, [128, 2, SEQ], BF16)
        S.op('pool', lambda e: e.memset(v65[:, :, :, 64:65], 1.0), writes=['v1'])
        ntl = NT if ctx_out else 16

        def rope_evac(ps_a, pb_a, ps_b, pb_b, t0, t1, dst, key):
            n = t1 - t0
            b = S.rot('tq', 2)
            S.op('dve', lambda e: e.tensor_tensor(out=tq[b][:, 0:n], in0=ps_a[:, 0:n], in1=ropeC[:, t0:t1], op=ALU.mult),
                 reads=[('ps', pb_a), 'ropeC'], writes=[('tq', b)])
            b2 = S.rot('tq', 2)
            S.op('dve', lambda e: e.tensor_tensor(out=tq[b2][:, 0:n], in0=ps_b[:, 0:n], in1=ropeS[:, t0:t1], op=ALU.mult),
                 reads=[('ps', pb_b), 'ropeS'], writes=[('tq', b2)])
            S.op('pool', lambda e: e.tensor_tensor(out=dst, in0=tq[b][:, 0:n], in1=tq[b2][:, 0:n], op=ALU.add),
                 reads=[('tq', b), ('tq', b2)], writes=[key])

        def proj_pair(wbuf, wk, ca, cb, scale, dst_plain, dst_rope, nm, c):
            for (t0, t1) in GROUPS:
                n = t1 - t0
                pa = S.rot('psP', 4); psa = PS[pa]
                rk = wk + [('hT', kc, i) for kc in range(8) for i in range(t0 // 128, t1 // 128)]
                for kc in range(8):
                    S.op('pe', lambda e, kc=kc: e.matmul(psa[:, 0:n], lhsT=wbuf[:, kc, ca:ca + 128], rhs=hT[:, kc, t0:t1],
                                                        start=(kc == 0), stop=(kc == 7)), reads=rk, writes=[('ps', pa)])
                ev_copy('act', dst_plain[:, c, t0:t1], psa[:, 0:n], pa, (nm, c, t0), scale=scale)
                if t0 < SEQ:
                    pb2 = S.rot('psP', 4); psb = PS[pb2]
                    for kc in range(8):
                        S.op('pe', lambda e, kc=kc: e.matmul(psb[:, 0:n], lhsT=wbuf[:, kc, cb:cb + 128], rhs=hT[:, kc, t0:t1],
                                                            start=(kc == 0), stop=(kc == 7)), reads=rk, writes=[('ps', pb2)])
                    rope_evac(psa, pa, psb, pb2, t0, t1, dst_rope[:, c, t0:t1], (nm + 'r', c, t0))

        bi = S.rot('wb', 2); wk = load_w(wb[bi], swin_d[j, :, 2048:2560], 512, ('wb', bi))
        bi2 = S.rot('wb', 2); wk2 = load_w(wb[bi2], swin_d[j, :, 2560:3072], 512, ('wb', bi2))
        for g in range(4):
            for (t0, t1) in GROUPS:
                n = t1 - t0
                pa = S.rot('psP', 4); psa = PS[pa]
                rk = [('hT', kc, i) for kc in range(8) for i in range(t0 // 128, t1 // 128)]
                for kc in range(8):
                    S.op('pe', lambda e, kc=kc: e.matmul(psa[:, 0:n], lhsT=wb[bi][:, kc, g * 128:(g + 1) * 128], rhs=hT[:, kc, t0:t1],
                                                        start=(kc == 0), stop=(kc == 7)), reads=wk + rk, writes=[('ps', pa)])
                ev_copy('dve', kT2[:, g, t0:t1], psa[:, 0:n], pa, ('kT2', g, t0))
                if t0 < SEQ:
                    pb2 = S.rot('psP', 4); psb = PS[pb2]
                    for kc in range(8):
                        S.op('pe', lambda e, kc=kc: e.matmul(psb[:, 0:n], lhsT=wb[bi2][:, kc, g * 128:(g + 1) * 128], rhs=hT[:, kc, t0:t1],
                                                            start=(kc == 0), stop=(kc == 7)), reads=wk2 + rk, writes=[('ps', pb2)])
                    rope_evac(psa, pa, psb, pb2, t0, t1, krT2[:, g, t0:t1], ('krT2', g, t0))
        bi = S.rot('wb', 2); wk = load_w(wb[bi], swin_d[j, :, 3072:3328], 256, ('wb', bi))
        def evv(ps, pb, i):
            ev_copy(alt('ev'), v65[:, i, :, 0:64], ps[:, 0:256].rearrange("p (h d) -> p h d", h=4), pb, ('v65', i))
        proj_tm(hT, wb[bi], wk, 0, 256, evv, range(NT))

        for g in range(4):
            bi = S.rot('wb', 2); wk = load_w(wb[bi], swin_d[j, :, g * 256:(g + 1) * 256], 256, ('wb', bi))
            bi2 = S.rot('wb', 2); wk2 = load_w(wb[bi2], swin_d[j, :, 1024 + g * 256:1024 + (g + 1) * 256], 256, ('wb', bi2))
            for c in range(2):
                for (t0, t1) in GROUPS:
                    n = t1 - t0
                    pa = S.rot('psP', 4); psa = PS[pa]
                    rk = [('hT', kc, i) for kc in range(8) for i in range(t0 // 128, t1 // 128)]
                    for kc in range(8):
                        S.op('pe', lambda e, kc=kc: e.matmul(psa[:, 0:n], lhsT=wb[bi][:, kc, c * 128:(c + 1) * 128], rhs=hT[:, kc, t0:t1],
                                                            start=(kc == 0), stop=(kc == 7)), reads=wk + rk, writes=[('ps', pa)])
                    ev_copy('dve', qT[:, c, t0:t1], psa[:, 0:n], pa, ('qT', c, t0), scale=0.125)
                    if t0 < SEQ:
                        pb2 = S.rot('psP', 4); psb = PS[pb2]
                        for kc in range(8):
                            S.op('pe', lambda e, kc=kc: e.matmul(psb[:, 0:n], lhsT=wb[bi2][:, kc, c * 128:(c + 1) * 128], rhs=hT[:, kc, t0:t1],
                                                                start=(kc == 0), stop=(kc == 7)), reads=wk2 + rk, writes=[('ps', pb2)])
                        n_ = n
                        b = S.rot('tq', 2)
                        S.op('dve', lambda e, b=b: e.scalar_tensor_tensor(out=tq[b][:, 0:n_], in0=psa[:, 0:n_], scalar=0.125,
                                                                          in1=ropeC[:, t0:t1], op0=ALU.mult, op1=ALU.mult),
                             reads=[('ps', pa), 'ropeC'], writes=[('tq', b)])
                        b2 = S.rot('tq', 2)
                        S.op('dve', lambda e, b2=b2: e.scalar_tensor_tensor(out=tq[b2][:, 0:n_], in0=psb[:, 0:n_], scalar=0.125,
                                                                            in1=ropeS[:, t0:t1], op0=ALU.mult, op1=ALU.mult),
                             reads=[('ps', pb2), 'ropeS'], writes=[('tq', b2)])
                        S.op('pool', lambda e, b=b, b2=b2: e.tensor_tensor(out=qrT[:, c, t0:t1], in0=tq[b][:, 0:n_], in1=tq[b2][:, 0:n_], op=ALU.add),
                             reads=[('tq', b), ('tq', b2)], writes=[('qrT', c, t0)])
            pT2 = [pT, V(R3T + 33 * KB, [128, 5, 4, 128], BF16)]
            den2 = [den, V(R3T + 32 * KB + 128, [128, 4, 1], F32)]

            def sw_blocks(i):
                if i < 16:
                    blocks = []
                    if i > 0:
                        blocks.append((i - 1, 'p'))
                    blocks.append((i, 'l'))
                    if i < 15:
                        blocks.append((i + 1, 'n'))
                    return blocks + [(16, 'c'), (17, 'c')]
                return [(16, 'c'), (17, 'c')]

            def sw_scores(i, pi):
                blocks = sw_blocks(i)
                pTc = pT2[pi]
                tgq = [t0 for (t0, t1_) in GROUPS if t0 <= i * 128 < t1_][0]
                for bi_, (jt, kind) in enumerate(blocks):
                    tgk = [t0 for (t0, t1_) in GROUPS if t0 <= jt * 128 < t1_][0]
                    roped = kind in ('p', 'l', 'n')
                    ksrc = krT2 if roped else kT2
                    qsrc = qrT if roped else qT
                    kkey = ('krT2', g, tgk) if roped else ('kT2', g, tgk)
                    qn = 'qrT' if roped else 'qT'
                    par = S.rot('swsc', 2)
                    for half in range(2):
                        bk = 2 * par + half
                        r0 = half * 64
                        first = True
                        if kind in ('p', 'n'):
                            bm = bmp if kind == 'p' else bmn
                            S.op('pe', lambda e, bk=bk, bm=bm: e.matmul(PS[bk][:, 0:256].rearrange("p (h q) -> p h q", h=2), lhsT=ident,
                                                                      rhs=bm.unsqueeze(1).to_broadcast([128, 2, 128]), start=True, stop=False),
                                 reads=['ident', 'bmp', 'bmn'], writes=[('ps', bk)])
                            first = False
                        S.op('pe', lambda e, bk=bk, r0=r0, jt=jt, ksrc=ksrc, qsrc=qsrc, first=first: e.matmul(
                            PS[bk][:, 0:256].rearrange("p (c q) -> p c q", c=2), lhsT=ksrc[r0:r0 + 64, g, jt * 128:(jt + 1) * 128],
                            rhs=qsrc[r0:r0 + 64, :, i * 128:(i + 1) * 128], start=first, stop=True),
                            reads=[kkey, (qn, 0, tgq), (qn, 1, tgq)], writes=[('ps', bk)])
                        S.op('act', lambda e, bk=bk, bi_=bi_, half=half: e.activation(
                            out=pTc[:, bi_, :, :].rearrange("p (c f) q -> p c f q", c=2)[:, :, half, :],
                            in_=PS[bk][:, 0:256].rearrange("p (c q) -> p c q", c=2), func=AF.Exp),
                            reads=[('ps', bk)], writes=[('pT', pi, bi_, half)])

            def sw_pv(i, pi):
                blocks = sw_blocks(i)
                nb = len(blocks)
                pTc = pT2[pi]
                dn = den2[pi]
                ob = 4 + pi
                for hh in range(4):
                    od = PS[ob][:, hh * 65:(hh + 1) * 65]
                    for bi_, (jt, kind) in enumerate(blocks):
                        S.op('pe', lambda e, od=od, bi_=bi_, jt=jt, hh=hh: e.matmul(od, lhsT=pTc[:, bi_, hh, :], rhs=v65[:, jt, g, :],
                                                                                 start=(bi_ == 0), stop=(bi_ == nb - 1)),
                             reads=[('pT', pi, bi_, 0), ('pT', pi, bi_, 1), ('v65', jt), 'v1'], writes=[('ps', ob)])
                ov = PS[ob][:, 0:260].rearrange("p (h e) -> p h e", e=65)
                S.op('dve', lambda e: e.tensor_tensor(out=dn, in0=ov[:, :, 64:65], in1=es[:, g * 4:(g + 1) * 4].unsqueeze(2), op=ALU.add),
                     reads=[('ps', ob), 'es'], writes=[('den_t', pi)])
                S.op('dve', lambda e: e.reciprocal(out=dn, in_=dn), reads=[('den_t', pi)], writes=[('den', pi)])
                S.op('dve', lambda e: e.tensor_tensor(
                    out=o_tok[:, i, g * 256:(g + 1) * 256].rearrange("p (h d) -> p h d", h=4), in0=ov[:, :, 0:64],
                    in1=dn.to_broadcast([128, 4, 64]), op=ALU.mult), reads=[('ps', ob), ('den', pi)], writes=[('otok', i)])

            pend = None
            for n_, i in enumerate(range(ntl)):
                sw_scores(i, n_ % 2)
                if pend is not None:
                    sw_pv(*pend)
                pend = (i, n_ % 2)
            sw_pv(*pend)
        S.barrier()

    def out_proj_loads(wsrc, first_layer, tiles):
        wout = V(R3T, [128, 8, 1024], BF16)
        sv = wsrc.rearrange("(c p) n -> p c n", p=128)
        for hq in range(4):
            S.dma('pool', f'wout{hq}', wout[:, hq * 2:(hq + 1) * 2, :], sv[:, hq * 2:(hq + 1) * 2, :], writes=[('wout', hq)])
        for i in tiles:
            src = tok_rows(i) if first_layer else xres_d[i * 128:(i + 1) * 128, :]
            S.dma('sp', f'xld{i}', x_res[:, i, :], src, writes=[('xres', i)])

    def out_proj(wsrc, OT, l, first_layer, tiles):
        wout = V(R3T, [128, 8, 1024], BF16)
        tmp = [V(R3T + 16 * KB, [128, 1024], F32), V(R3T + 20 * KB, [128, 1024], F32)]
        wk = [('wout', hq) for hq in range(4)]
        for i in tiles:
            w = 0 if i < 16 else 1
            b = S.rot('tmp', 2)
            PY = [PS[4], PS[5], PS[6][:, :].bitcast(F32), PS[7][:, :].bitcast(F32)]
            ys = S.rot('psY', 2)
            for hf in range(2):
                pb = 4 + 2 * ys + hf
                for kc in range(8):
                    S.op('pe', lambda e, kc=kc, hf=hf, pb=pb: e.matmul(PY[pb - 4][:, :], lhsT=OT[:, kc, i * 128:(i + 1) * 128],
                                                                     rhs=wout[:, kc, hf * 512:(hf + 1) * 512], start=(kc == 0), stop=(kc == 7)),
                         reads=wk + [('OT', i)], writes=[('ps', pb)])
                S.op('dve', lambda e, hf=hf, pb=pb, w=w, b=b: e.tensor_tensor(out=tmp[b][:, hf * 512:(hf + 1) * 512], in0=PY[pb - 4][:, :],
                                                                             in1=gate_bc[:, w, hf * 512:(hf + 1) * 512], op=ALU.mult),
                     reads=[('ps', pb), ('gate', w, hf)], writes=[('tmp', b, hf)])
            S.op('pool', lambda e, b=b: e.tensor_tensor(out=x_res[:, i, :], in0=x_res[:, i, :], in1=tmp[b], op=ALU.add),
                 reads=[('tmp', b, 0), ('tmp', b, 1), ('xres', i)], writes=[('xres', i)])

    def mlp(l, hT, tiles):
        uT = V(R1, [128, 4, NTOK], BF16)
        tmp = [V(R1 + 18 * KB, [128, 1024], F32), V(R1 + 22 * KB, [128, 1024], F32)]
        rl = [V(R1 + 26 * KB, [128, 512], F32), V(R1 + 28 * KB, [128, 512], F32)]
        W1 = [V(R3T, [128, 8, 512], BF16), V(R3T + 8 * KB, [128, 8, 512], BF16)]
        W2 = [V(R3T + 16 * KB, [128, 4, 1024], BF16), V(R3T + 24 * KB, [128, 4, 1024], BF16)]
        assert R3T + 32 * KB <= R3END
        thi = (max(tiles) + 1) * 128
        def issue_w(blk):
            bi = blk % 2
            w1v = w1_d[l, :, blk * 512:(blk + 1) * 512].rearrange("(c p) n -> p c n", p=128)
            w2v = w2_d[l, blk * 512:(blk + 1) * 512, :].rearrange("(c p) n -> p c n", p=128)
            k1, k2 = [], []
            for hq in range(2):
                S.dma('pool', f'w1_{bi}_{hq}', W1[bi][:, hq * 4:(hq + 1) * 4, :], w1v[:, hq * 4:(hq + 1) * 4, :], writes=[('W1', bi, hq)])
                k1.append(('W1', bi, hq))
            for hq in range(2):
                S.dma('pool', f'w2_{bi}_{hq}', W2[bi][:, hq * 2:(hq + 1) * 2, :], w2v[:, hq * 2:(hq + 1) * 2, :], writes=[('W2', bi, hq)])
                k2.append(('W2', bi, hq))
            return bi, k1, k2

        nxt = issue_w(0)
        for blk in range(8):
            bi, k1, k2 = nxt
            if blk + 1 < 8:
                nxt = issue_w(blk + 1)
            PY = [PS[4], PS[5], PS[6][:, :].bitcast(F32), PS[7][:, :].bitcast(F32)]

            def u_phase(t0, t1):
                n = t1 - t0
                for fc in range(4):
                    pb = S.rot('psU', 4)
                    for kc in range(8):
                        S.op('pe', lambda e, kc=kc, fc=fc, pb=pb: e.matmul(PS[pb][:, 0:n], lhsT=W1[bi][:, kc, fc * 128:(fc + 1) * 128],
                                                                         rhs=hT[:, kc, t0:t1], start=(kc == 0), stop=(kc == 7)),
                             reads=k1 + [('hT', kc, i) for i in range(t0 // 128, t1 // 128)], writes=[('ps', pb)])
                    rb = S.rot('rl', 2)
                    S.op('act', lambda e, pb=pb, rb=rb: e.activation(out=rl[rb][:, 0:n], in_=PS[pb][:, 0:n], func=AF.Relu),
                         reads=[('ps', pb)], writes=[('rl', rb)])
                    S.op('dve', lambda e, rb=rb, fc=fc: e.tensor_tensor(out=uT[:, fc, t0:t1], in0=rl[rb][:, 0:n], in1=rl[rb][:, 0:n], op=ALU.mult),
                         reads=[('rl', rb)], writes=[('uT', fc, t0)])

            def y_phase(t0, t1):
                for i in range(t0 // 128, t1 // 128):
                    if i not in tiles:
                        continue
                    w = 2 + (0 if i < 16 else 1)
                    b = S.rot('tmp', 2)
                    ys = S.rot('psY', 2)
                    for hf in range(2):
                        pb = 4 + 2 * ys + hf
                        for fc in range(4):
                            S.op('pe', lambda e, fc=fc, hf=hf, pb=pb: e.matmul(PY[pb - 4][:, :], lhsT=uT[:, fc, i * 128:(i + 1) * 128],
                                                                             rhs=W2[bi][:, fc, hf * 512:(hf + 1) * 512], start=(fc == 0), stop=(fc == 3)),
                                 reads=k2 + [('uT', fc, t0)], writes=[('ps', pb)])
                        S.op('dve', lambda e, hf=hf, pb=pb, w=w, b=b: e.tensor_tensor(out=tmp[b][:, hf * 512:(hf + 1) * 512], in0=PY[pb - 4][:, :],
                                                                                     in1=gate_bc[:, w, hf * 512:(hf + 1) * 512], op=ALU.mult),
                             reads=[('ps', pb), ('gate', w, hf)], writes=[('tmp', b, hf)])
                    S.op('pool', lambda e, b=b, i=i: e.tensor_tensor(out=x_res[:, i, :], in0=x_res[:, i, :], in1=tmp[b], op=ALU.add),
                         reads=[('tmp', b, 0), ('tmp', b, 1), ('xres', i)], writes=[('xres', i)])

            grp = [(t0, t1) for (t0, t1) in GROUPS if t0 < thi]
            prev = None
            for gidx, (t0, t1) in enumerate(grp):
                u_phase(t0, t1)
                if prev is not None:
                    y_phase(*prev)
                prev = (t0, t1)
            y_phase(*prev)

    hT_A = V(R1, [128, 8, NTOK], BF16)
    hT_B = V(R2, [128, 8, NTOK], BF16)
    o_tok = V(R2, [128, NT, D], BF16)
    OT = V(R1, [128, 8, NTOK], BF16)

    for i in range(NT):
        S.dma('sp', f'xld{i}', x_res[:, i, :], tok_rows(i), writes=[('xres', i)])
    import os
    STOP_L = int(os.environ.get('STOP_L', '0'))
    cur_l = [0]

    def chk(name):
        if stop == name and cur_l[0] == STOP_L:
            S.barrier()
            raise _Stop()

    try:
      for l in range(n_layers):
        ctx_out = l < n_layers - 1
        cur_l[0] = l
        adaln(l)
        S.barrier()
        chk('adaln')
        norm_to_hT(hT_A, GpM, 'GpM', 0, list(range(NT)), R2)
        if l > 0:
            pass
        if l > 0:
            for i in range(NT):
                S.dma('sp', f'xst{i}', xres_d[i * 128:(i + 1) * 128, :], x_res[:, i, :], reads=[('xres', i)])
        S.barrier()
        chk('normA')
        if l % 2 == 0:
            even_mixer(l // 2, hT_A, o_tok)
            tiles = list(range(NT))
        else:
            odd_mixer(l // 2, hT_A, o_tok, ctx_out)
            tiles = list(range(NT if ctx_out else 16))
        chk('mixer')
        out_proj_loads(about_d[l // 2] if l % 2 == 0 else swout_d[l // 2], l == 0, tiles)
        transpose_otok(o_tok, OT, tiles)
        S.barrier()
        chk('tr')
        out_proj(about_d[l // 2] if l % 2 == 0 else swout_d[l // 2], OT, l, l == 0, tiles)
        S.barrier()
        norm_to_hT(hT_B, GpL, 'GpL', 2, tiles, R1)
        S.barrier()
        chk('norm2')
        mlp(l, hT_B, tiles)
        S.barrier()
    except _Stop:
        pass
    nf = V(R3T, [128, D], F32)
    junk = V(R1, [128, D], F32)
    yo = [V(R1 + 4 * KB, [128, D], F32), V(R1 + 8 * KB, [128, D], F32)]
    S.dma('sp', 'nf', nf, nfin_d.broadcast_to([128, D]), writes=['nf'])
    for i in range(16):
        S.op('act', lambda e, i=i: e.activation(out=junk, in_=x_res[:, i, :], func=AF.Square, accum_out=ssA[:, i:i + 1]),
             reads=[('xres', i)], writes=['junk', ('ssA', i)])
    S.op('act', lambda e: e.activation(out=rsA[:, 0:16], in_=ssA[:, 0:16], func=AF.Sqrt, scale=1.0 / D, bias=EPS),
         reads=[('ssA', i) for i in range(16)], writes=['rsA_t'])
    S.op('dve', lambda e: e.reciprocal(out=rsA[:, 0:16], in_=rsA[:, 0:16]), reads=['rsA_t'], writes=['rsA'])
    for i in range(16):
        b = S.rot('yo', 2)
        S.op('dve', lambda e, i=i, b=b: e.scalar_tensor_tensor(out=yo[b], in0=x_res[:, i, :], scalar=rsA[:, i:i + 1], in1=nf,
                                                             op0=ALU.mult, op1=ALU.mult), reads=[('xres', i), 'rsA', 'nf'], writes=[('yo', b)])
        S.dma('sp', f'ost{b}', out_d[i * 128:(i + 1) * 128, :], yo[b], reads=[('yo', b)])
    S.barrier()
    for c in reversed(ps_cm):
        c.__exit__(None, None, None)
    arena_cm.__exit__(None, None, None)
    S.close()
    return nc


def _prep_shared(inp):
    f = np.float32
    sh = {}
    sh["ada_w"] = np.ascontiguousarray(inp["ada_w"], dtype=f)
    ada_b = np.asarray(inp["ada_b"], dtype=f)
    sh["ada_bT"] = np.ascontiguousarray(ada_b.reshape(DEPTH, 48, 128).transpose(0, 2, 1))
    sh["ada_bf"] = np.ascontiguousarray(ada_b.reshape(DEPTH, 1, 6 * D))
    sh["nmix"] = np.ascontiguousarray(np.asarray(inp["norm_mix"], dtype=f).reshape(DEPTH, 8, 128).transpose(0, 2, 1))
    sh["nmlp"] = np.ascontiguousarray(np.asarray(inp["norm_mlp"], dtype=f).reshape(DEPTH, 8, 128).transpose(0, 2, 1))
    sh["nfin"] = np.ascontiguousarray(np.asarray(inp["norm_final"], dtype=f).reshape(1, D))
    sh["w1"] = np.ascontiguousarray(inp["mlp_w1"], dtype=f)
    sh["w2"] = np.ascontiguousarray(inp["mlp_w2"], dtype=f)
    sh["abin"] = np.ascontiguousarray(inp["ab_w_in"], dtype=f)
    sh["about"] = np.ascontiguousarray(inp["ab_w_out"], dtype=f)
    rpb = np.asarray(inp["na_rpb"], dtype=f)
    nab = np.empty((2, N_PAT, 128, 8, 128), dtype=f)
    for pi, idx in enumerate(NA_PATS):
        valid = idx >= 0
        ic = np.where(valid, idx, 0)
        for jl in range(2):
            flat = rpb[jl].reshape(8, 15 * 31)
            g = flat[:, ic]
            g = np.where(valid[None], g, f(-1e30))
            nab[jl, pi] = g.transpose(1, 0, 2)
    sh["nabias"] = np.ascontiguousarray(nab.reshape(2, N_PAT, 128, 1024))
    wa2 = np.asarray(inp["gla_wa2"], dtype=f)
    ba = np.asarray(inp["gla_ba"], dtype=f)
    wa2b = np.zeros((2, 33, 512), dtype=f)
    for jl in range(2):
        for d in range(2):
            wa2b[jl, d * 16:(d + 1) * 16, d * 256:(d + 1) * 256] = wa2[jl, d]
            wa2b[jl, 32, d * 256:(d + 1) * 256] = ba[jl, d]
    sh["wa2b"] = wa2b
    sh["gn"] = np.ascontiguousarray(np.asarray(inp["gla_gnorm"], dtype=f).reshape(2, 1, 512))
    sw = np.asarray(inp["swa_w_in"], dtype=f)
    qcols = np.arange(1024)
    swap = qcols ^ 1
    kdup = np.concatenate([np.concatenate([1024 + g * 64 + np.arange(64)] * 2) for g in range(4)])
    kdup_sw = np.concatenate([np.concatenate([1024 + g * 64 + (np.arange(64) ^ 1)] * 2) for g in range(4)])
    vcols = 1280 + np.arange(256)
    cols = np.concatenate([qcols, swap, kdup, kdup_sw, vcols])
    sh["swin"] = np.ascontiguousarray(sw[:, :, cols])
    sh["swout"] = np.ascontiguousarray(inp["swa_w_out"], dtype=f)
    sh["sink"] = np.ascontiguousarray(np.asarray(inp["swa_sink"], dtype=f).reshape(2, 1, 16))
    Ct, St = _rope_tables()
    sh["ropeC"], sh["ropeS"] = Ct, St
    return sh


_NC_CACHE = {}


def kernel(**inp):
    n_layers = int(inp.pop("_n_layers", DEPTH))
    f = np.float32
    sh = _prep_shared(inp)
    x = np.asarray(inp["x"], dtype=f); c = np.asarray(inp["c"], dtype=f)
    ctx = np.asarray(inp["ctx"], dtype=f); c_ctx = np.asarray(inp["c_ctx"], dtype=f)
    in_maps = []
    for b in range(8):
        m = dict(sh)
        m["x"] = np.ascontiguousarray(x[b]); m["ctx"] = np.ascontiguousarray(ctx[b])
        m["scin"] = np.ascontiguousarray(np.concatenate([c[b].reshape(8, 128).T, c_ctx.reshape(8, 128).T], axis=1))
        in_maps.append(m)
    if n_layers not in _NC_CACHE:
        _NC_CACHE[n_layers] = build(n_layers)
    nc = _NC_CACHE[n_layers]
    res = run_bass_kernel_spmd(nc, in_maps, core_ids=list(range(8)))
    return np.stack([r["out"] for r in res.results], axis=0).astype(f)
```
